# Optimizing a Trainium2 kernel written in Bass

```python
import math, functools
import jax, jax.numpy as jnp
from jax import lax
import numpy as np

D_MODEL = 1024
BATCH = 4
SEQ = 4096
DEPTH = 1
DEC_BATCH = 128
DEC_SEQ = 1
PAST_LEN = 8192
PAGE_SIZE = 128

RET_HEADS = 4
RET_DK = 128
RET_DV = 256
RET_QK = RET_HEADS * RET_DK
RET_V = RET_HEADS * RET_DV
RET_CHUNK = 128
SWA_HEADS = 16
SWA_KV_HEADS = 4
SWA_GROUP = SWA_HEADS // SWA_KV_HEADS
SWA_HD = 64
SWA_Q = SWA_HEADS * SWA_HD
SWA_KV = SWA_KV_HEADS * SWA_HD
WINDOW = 128
SWA_BLOCK = WINDOW
ROPE_THETA = 10000.0
EPS = 1e-6
IN_WIDTH = 2 * RET_QK + 2 * RET_V + 2 * SWA_Q + 2 * SWA_KV + 2 * D_MODEL

kernel_name = 'retnet_swa_sink_gated_hybrid_step'


def _split_points():
    widths = (RET_QK, RET_QK, RET_V, RET_V, SWA_Q, SWA_KV, SWA_KV, SWA_Q, D_MODEL, D_MODEL)
    pts, acc = [], 0
    for w in widths[:-1]:
        acc += w
        pts.append(acc)
    return pts


def rms_norm(x, g):
    xf = x.astype(jnp.float32)
    y = xf * lax.rsqrt(jnp.mean(xf * xf, axis=-1, keepdims=True) + EPS) * g.astype(jnp.float32)
    return y.astype(x.dtype)


def rope(x, pos):
    d = x.shape[-1]
    inv = ROPE_THETA ** (-jnp.arange(0, d, 2, dtype=jnp.float32) / d)
    ang = pos.astype(jnp.float32)[:, None] * inv[None, :]
    c = jnp.cos(ang)[:, None, :]
    s = jnp.sin(ang)[:, None, :]
    xf = x.astype(jnp.float32)
    x1, x2 = xf[..., : d // 2], xf[..., d // 2:]
    return jnp.concatenate([x1 * c - x2 * s, x2 * c + x1 * s], axis=-1).astype(x.dtype)


def ret_log_decay():
    return jnp.log(1.0 - 2.0 ** (-5.0 - jnp.arange(RET_HEADS, dtype=jnp.float32)))


def retention_chunk(S, q, k, v, lg):
    L = q.shape[1]
    n = jnp.arange(L, dtype=jnp.float32)
    diff = n[:, None] - n[None, :]
    dmask = jnp.where(diff[None] >= 0, jnp.exp(jnp.maximum(diff, 0.0)[None] * lg[:, None, None]), 0.0)
    scores = jnp.einsum('bnhd,bmhd->bhnm', q, k) * dmask[None]
    o = jnp.einsum('bhnm,bmhe->bnhe', scores, v)
    cross = jnp.exp((n[:, None] + 1.0) * lg[None, :])
    o = o + jnp.einsum('bnhd,bhde->bnhe', q, S) * cross[None, :, :, None]
    k_dec = jnp.exp((L - 1.0 - n)[:, None] * lg[None, :])
    S_new = jnp.exp(L * lg)[None, :, None, None] * S + jnp.einsum('bmhd,bmhe,mh->bhde', k, v, k_dec)
    return S_new, o


def retention_prompt(q, k, v, lg):
    B, T = q.shape[:2]
    nc = T // RET_CHUNK

    def to_chunks(a):
        return a.reshape((B, nc, RET_CHUNK) + a.shape[2:]).swapaxes(0, 1)

    S0 = jnp.zeros((B, RET_HEADS, RET_DK, RET_DV), jnp.float32)
    S, o = lax.scan(lambda s, c: retention_chunk(s, c[0], c[1], c[2], lg), S0,
                    (to_chunks(q), to_chunks(k), to_chunks(v)))
    o = o.swapaxes(0, 1).reshape(B, T, RET_HEADS, RET_DV)
    return o, S


def retention_step(q, k, v, state, lg):
    S_new, o = retention_chunk(state.astype(jnp.float32), q, k, v, lg)
    return o, S_new


def sink_attention(q, k, v, allowed, sinks):
    s = jnp.einsum('bnqhgd,bnkhd->bnhgqk', q.astype(jnp.float32), k.astype(jnp.float32)) * (SWA_HD ** -0.5)
    s = jnp.where(allowed[None, :, None, None], s, -jnp.inf)
    sink = jnp.broadcast_to(sinks.astype(jnp.float32)[None, None, :, :, None, None], s.shape[:-1] + (1,))
    p = jax.nn.softmax(jnp.concatenate([s, sink], axis=-1), axis=-1)[..., :-1]
    return jnp.einsum('bnhgqk,bnkhd->bnqhgd', p, v.astype(jnp.float32))


def swa_prompt(q, k, v, sinks):
    B, T = q.shape[:2]
    nb = T // SWA_BLOCK
    qb = q.reshape(B, nb, SWA_BLOCK, SWA_KV_HEADS, SWA_GROUP, SWA_HD)
    kb = k.reshape(B, nb, SWA_BLOCK, SWA_KV_HEADS, SWA_HD)
    vb = v.reshape(B, nb, SWA_BLOCK, SWA_KV_HEADS, SWA_HD)

    def with_prev(a):
        prev = jnp.pad(a[:, :-1], ((0, 0), (1, 0), (0, 0), (0, 0), (0, 0)))
        return jnp.concatenate([prev, a], axis=2)

    i = jnp.arange(SWA_BLOCK)[:, None]
    j = jnp.arange(2 * SWA_BLOCK)[None, :]
    d = i + SWA_BLOCK - j
    band = (d >= 0) & (d < WINDOW)
    allowed = band[None] & ((jnp.arange(nb)[:, None, None] > 0) | (j[None] >= SWA_BLOCK))
    o = sink_attention(qb, with_prev(kb), with_prev(vb), allowed, sinks)
    o = o.reshape(B, T, SWA_KV_HEADS, SWA_GROUP, SWA_HD)
    keep = min(WINDOW, T)
    return o, k[:, T - keep:], v[:, T - keep:]


def swa_sample(q, k, v, sinks, cache_k, cache_v):
    Ts = q.shape[1]
    Wc = cache_k.shape[1]
    kk = jnp.concatenate([cache_k.astype(k.dtype), k], axis=1)
    vv = jnp.concatenate([cache_v.astype(v.dtype), v], axis=1)
    j = jnp.arange(Ts)[:, None]
    m = jnp.arange(Wc + Ts)[None, :]
    d = Wc + j - m
    allowed = ((d >= 0) & (d < WINDOW))[None]
    o = sink_attention(q[:, None], kk[:, None], vv[:, None], allowed, sinks)[:, 0]
    return o, kk[:, Ts:], vv[:, Ts:]


def hybrid_layer(x, pos, retention_fn, swa_fn, norm_g, w_in, ret_norm_g, swa_q_g, swa_k_g,
                 swa_sinks, w_br_ret, w_br_swa, w_out):
    B, T, _ = x.shape
    h = rms_norm(x, norm_g)
    proj = jnp.einsum('btd,de->bte', h, w_in)
    rq, rk, rv, rg, sq, sk, sv, sg, mg_r, mg_s = jnp.split(proj, _split_points(), axis=-1)
    rq = rope(rq.reshape(B, T, RET_HEADS, RET_DK), pos).astype(jnp.float32)
    rk = rope(rk.reshape(B, T, RET_HEADS, RET_DK), pos).astype(jnp.float32) * (RET_DK ** -0.5)
    rv = rv.reshape(B, T, RET_HEADS, RET_DV).astype(jnp.float32)
    o_r, ret_state = retention_fn(rq, rk, rv)
    o_r = rms_norm(o_r, ret_norm_g).reshape(B, T, RET_V).astype(x.dtype) * jax.nn.silu(rg)
    br_r = jnp.einsum('bte,ed->btd', o_r, w_br_ret)
    sq = rope(rms_norm(sq.reshape(B, T, SWA_HEADS, SWA_HD), swa_q_g), pos)
    sk = rope(rms_norm(sk.reshape(B, T, SWA_KV_HEADS, SWA_HD), swa_k_g), pos)
    sv = sv.reshape(B, T, SWA_KV_HEADS, SWA_HD)
    o_s, k_buf, v_buf = swa_fn(sq.reshape(B, T, SWA_KV_HEADS, SWA_GROUP, SWA_HD), sk, sv,
                               swa_sinks.reshape(SWA_KV_HEADS, SWA_GROUP))
    o_s = o_s.reshape(B, T, SWA_Q).astype(x.dtype) * jax.nn.silu(sg)
    br_s = jnp.einsum('bte,ed->btd', o_s, w_br_swa)
    merged = jax.nn.sigmoid(mg_r) * br_r + jax.nn.sigmoid(mg_s) * br_s
    y = x + jnp.einsum('btd,de->bte', merged, w_out)
    return y, ret_state, k_buf, v_buf


def setup_inputs(seed: int = 0) -> dict:
    key = jax.random.key(seed)
    ks = jax.random.split(key, 14)
    nrm = jax.random.normal
    f32 = jnp.float32
    swa_cache = min(WINDOW, PAST_LEN)
    return {
        'x_prompt': nrm(ks[0], (BATCH, SEQ, D_MODEL), f32),
        'x_sample': nrm(ks[1], (DEC_BATCH, DEC_SEQ, D_MODEL), f32),
        'state_ret': 0.05 * nrm(ks[2], (DEPTH, DEC_BATCH, RET_HEADS, RET_DK, RET_DV), f32),
        'cache_swa_k': nrm(ks[3], (DEPTH, DEC_BATCH, swa_cache, SWA_KV_HEADS, SWA_HD), f32),
        'cache_swa_v': nrm(ks[4], (DEPTH, DEC_BATCH, swa_cache, SWA_KV_HEADS, SWA_HD), f32),
        'norm_g': 1.0 + 0.02 * nrm(ks[5], (DEPTH, D_MODEL), f32),
        'w_in': nrm(ks[6], (DEPTH, D_MODEL, IN_WIDTH), f32) * D_MODEL ** -0.5,
        'ret_norm_g': 1.0 + 0.02 * nrm(ks[7], (DEPTH, RET_HEADS, RET_DV), f32),
        'swa_q_g': 1.0 + 0.02 * nrm(ks[8], (DEPTH, SWA_HD), f32),
        'swa_k_g': 1.0 + 0.02 * nrm(ks[9], (DEPTH, SWA_HD), f32),
        'swa_sinks': 0.5 * nrm(ks[10], (DEPTH, SWA_HEADS), f32),
        'w_br_ret': nrm(ks[11], (DEPTH, RET_V, D_MODEL), f32) * RET_V ** -0.5,
        'w_br_swa': nrm(ks[12], (DEPTH, SWA_Q, D_MODEL), f32) * SWA_Q ** -0.5,
        'w_out': nrm(ks[13], (DEPTH, D_MODEL, D_MODEL), f32) * D_MODEL ** -0.5,
    }


def reference(x_prompt, x_sample, state_ret, cache_swa_k, cache_swa_v, norm_g, w_in, ret_norm_g,
              swa_q_g, swa_k_g, swa_sinks, w_br_ret, w_br_swa, w_out):
    lg = ret_log_decay()
    pos_p = jnp.arange(x_prompt.shape[1], dtype=jnp.int32)
    pos_s = PAST_LEN + jnp.arange(x_sample.shape[1], dtype=jnp.int32)
    yp, ys = x_prompt, x_sample
    rp_l, rs_l, kp_l, vp_l, ksm_l, vsm_l = [], [], [], [], [], []
    for l in range(DEPTH):
        weights = (norm_g[l], w_in[l], ret_norm_g[l], swa_q_g[l], swa_k_g[l], swa_sinks[l],
                   w_br_ret[l], w_br_swa[l], w_out[l])
        yp, rp, kp, vp = hybrid_layer(yp, pos_p, functools.partial(retention_prompt, lg=lg),
                                      swa_prompt, *weights)
        ys, rs, ksm, vsm = hybrid_layer(ys, pos_s,
                                        functools.partial(retention_step, state=state_ret[l], lg=lg),
                                        functools.partial(swa_sample, cache_k=cache_swa_k[l], cache_v=cache_swa_v[l]),
                                        *weights)
        rp_l.append(rp); rs_l.append(rs); kp_l.append(kp); vp_l.append(vp); ksm_l.append(ksm); vsm_l.append(vsm)
    return (yp, ys, jnp.stack(rp_l), jnp.stack(rs_l), jnp.stack(kp_l), jnp.stack(vp_l),
            jnp.stack(ksm_l), jnp.stack(vsm_l))
```

```python
import numpy as np
import concourse.bass as bass
import concourse.mybir as mybir
from concourse.bass_utils import run_bass_kernel_spmd

F32 = mybir.dt.float32
BF16 = mybir.dt.bfloat16
AF = mybir.ActivationFunctionType
ALU = mybir.AluOpType
AX = mybir.AxisListType

GRAN = 512


class Buf:
    def __init__(self, tensor, off, n, rowlen, keys, esz):
        self.tensor = tensor
        self.off = off
        self.n = n
        self.rowlen = rowlen
        self.keys = keys
        self.esz = esz

    def ap(self, dims, off=0, p0=0, np_=128):
        return bass.AP(self.tensor, p0 * self.rowlen + self.off + off,
                       [[self.rowlen, np_]] + [list(d) for d in dims])

    def flat(self, n=None, off=0, p0=0, np_=128):
        return self.ap([[1, self.n - off if n is None else n]], off=off, p0=p0, np_=np_)


class Prog:
    def __init__(self, nc, arena_bytes, n_dma_sems=52):
        self.nc = nc
        self.ops = []
        self.last_w = {}
        self.readers = {}
        self.arena_bytes = arena_bytes
        self.arena_top = 0
        self.n_dma_sems = n_dma_sems
        self.dram_key = 0
        self.t16 = None
        self.ps = []

    def setup_mem(self, t16, ps_tensors):
        self.t16 = t16
        self.t32 = t16.bitcast(F32)
        self.ps = [(p, p.bitcast(BF16)) for p in ps_tensors]

    def alloc_off(self, nbytes):
        off = self.arena_top
        self.arena_top = (off + nbytes + GRAN - 1) // GRAN * GRAN
        assert self.arena_top <= self.arena_bytes, (self.arena_top, self.arena_bytes)
        return off

    def sb_at(self, boff, n, dtype):
        esz = 4 if dtype == F32 else 2
        t = self.t32 if dtype == F32 else self.t16
        assert boff % esz == 0
        keys = frozenset(("sb", g) for g in range(boff // GRAN, (boff + n * esz - 1) // GRAN + 1))
        return Buf(t, boff // esz, n, self.arena_bytes // esz, keys, esz)

    def sb(self, n, dtype):
        esz = 4 if dtype == F32 else 2
        return self.sb_at(self.alloc_off(n * esz), n, dtype)

    def psum(self, bank, dtype=F32, off=0, n=None):
        t = self.ps[bank][0 if dtype == F32 else 1]
        full = 512 if dtype == F32 else 1024
        if n is None:
            n = full - off
        return Buf(t, off, n, full, frozenset([("ps", bank)]), 4 if dtype == F32 else 2)

    def dram(self, tensor, esz=4):
        self.dram_key += 1
        return Buf(tensor, 0, 0, 0, frozenset([("dr", self.dram_key)]), esz)

    def op(self, eng, fn, reads=(), writes=(), dma=False):
        idx = len(self.ops)
        deps = set()
        rk = set()
        for b in reads:
            rk |= b.keys
        wk = set()
        for b in writes:
            wk |= b.keys
        for k in rk:
            w = self.last_w.get(k)
            if w is not None:
                deps.add(w)
        for k in wk:
            w = self.last_w.get(k)
            if w is not None:
                deps.add(w)
            for r in self.readers.get(k, ()):
                deps.add(r)
        for k in rk:
            lst = self.readers.setdefault(k, [])
            if not dma:
                lst[:] = [r for r in lst if self.ops[r]["dma"] or self.ops[r]["eng"] != eng]
            lst.append(idx)
        for k in wk:
            self.readers[k] = []
            self.last_w[k] = idx
        deps.discard(idx)
        self.ops.append(dict(eng=eng, fn=fn, deps=deps, dma=dma))
        return idx

    def dma(self, queue, out_ap, in_ap, reads, writes, **kw):
        def fn(e):
            return e.dma_start(out=out_ap, in_=in_ap, **kw)
        return self.op(queue, fn, reads, writes, dma=True)

    def emit(self):
        nc = self.nc
        ops = self.ops
        engs = ["pe", "act", "dve", "pool", "sp"]
        eng_obj = dict(pe=nc.tensor, act=nc.scalar, dve=nc.vector, pool=nc.gpsimd, sp=nc.sync)
        sig = [False] * len(ops)
        for i, o in enumerate(ops):
            nd = set()
            for d in o["deps"]:
                od = ops[d]
                if (not od["dma"]) and od["eng"] == "pe" and o["eng"] == "pe" and not o["dma"]:
                    continue
                nd.add(d)
                sig[d] = True
            o["deps"] = nd
        cnt = {e: 0 for e in engs}
        dma_n = 0
        n_sw = 12
        n_hw = self.n_dma_sems - n_sw
        qn = {"pool": 0, "hw": 0}
        for i, o in enumerate(ops):
            if o["dma"]:
                if o["eng"] == "pool":
                    o["dsem"] = qn["pool"] % n_sw
                    o["duse"] = qn["pool"] // n_sw + 1
                    qn["pool"] += 1
                else:
                    o["dsem"] = n_sw + qn["hw"] % n_hw
                    o["duse"] = qn["hw"] // n_hw + 1
                    qn["hw"] += 1
                dma_n += 1
            elif sig[i]:
                cnt[o["eng"]] += 1
                o["sigval"] = cnt[o["eng"]]
        self.stats = dict(cnt=dict(cnt), dma=dma_n, nops=len(ops))

        import contextlib
        with contextlib.ExitStack() as st:
            esem = {e: st.enter_context(nc.semaphore("s_" + e)) for e in engs}
            dsem = [st.enter_context(nc.semaphore("d%d" % i)) for i in range(self.n_dma_sems)]
            block = st.enter_context(nc.Block())
            per_eng = {e: [i for i, o in enumerate(ops) if o["eng"] == e] for e in engs}

            def run(e, eng):
                seen = {}
                for i in per_eng[e]:
                    o = ops[i]
                    need = {}
                    for d in o["deps"]:
                        od = ops[d]
                        if od["dma"]:
                            key = ("d", od["dsem"])
                            val = 16 * od["duse"]
                        else:
                            key = ("e", od["eng"])
                            val = od["sigval"]
                        if need.get(key, 0) < val:
                            need[key] = val
                    if o["dma"] and o["duse"] > 1:
                        key = ("d", o["dsem"])
                        val = 16 * (o["duse"] - 1)
                        if need.get(key, 0) < val:
                            need[key] = val
                    for key, val in need.items():
                        if seen.get(key, 0) >= val:
                            continue
                        seen[key] = val
                        s = dsem[key[1]] if key[0] == "d" else esem[key[1]]
                        eng.wait_ge(s, val)
                    if o["fn"] is None:
                        continue
                    ins = o["fn"](eng)
                    if o["dma"]:
                        ins.then_inc(dsem[o["dsem"]], 16)
                    elif sig[i]:
                        ins.then_inc(esem[e], 1)

            @block.tensor
            def _(eng):
                run("pe", eng)

            @block.scalar
            def _(eng):
                run("act", eng)

            @block.vector
            def _(eng):
                run("dve", eng)

            @block.gpsimd
            def _(eng):
                run("pool", eng)

            @block.sync
            def _(eng):
                run("sp", eng)


import contextlib
import ml_dtypes

D = 1024
NT = 16
BLK = 4
EPS = 1e-6
GAM = [1.0 - 2.0 ** (-5.0 - h) for h in range(4)]
COLS = dict(rq=0, rk=512, rv=1024, rg=2048, sq=3072, skv=4096, sg=4608, mr=5632, ms=6656)
MASKNEG = -30000.0


def build_program(with_sample=True, dbg=False):
    nc = bass.Bass("TRN2", target_bir_lowering=False)

    def din(name, shape, dt=F32):
        return nc.dram_tensor(name, shape, dt, kind="ExternalInput")

    def dout(name, shape, dt=F32):
        return nc.dram_tensor(name, shape, dt, kind="ExternalOutput")

    xm = din("xm", [2048, D]); xp = din("xp", [2048, D]); xsd = din("xs", [16, D])
    w_in = din("w_in", [D, 7680]); w_o3 = din("w_o3", [3, D, D])
    norm_g = din("norm_g", [128, 8]); ret_g = din("ret_g", [128, 8])
    gqk = din("gqk", [4, 64]); sinks = din("sinks", [16])
    state = din("state", [16, 4, 128, 256]); ckd = din("ck", [16, 128, 256]); cvd = din("cv", [16, 128, 256])
    rtab_main = din("rtab_main", [16, 128, 2048]); rtab_pre = din("rtab_pre", [16, 128, 1024])
    rtab_samp = din("rtab_samp", [16, 2048]); stab = din("stab", [18, 128, 128])
    identd = din("ident", [128, 128]); cmaskd = din("cmask", [128, 128]); swamaskd = din("swamask", [128, 1024])
    eyed = din("eye16", [128, 256])

    yp = dout("yp", [2048, D]); ysd = dout("ys", [16, D]); rsp = dout("rsp", [4, 128, 256])
    rss = dout("rss", [16, 4, 128, 256]); kpd = dout("kp", [128, 256]); vpd = dout("vp", [128, 256])
    ksd = dout("ks", [16, 128, 256]); vsd = dout("vs", [16, 128, 256])

    dbg1 = dout("dbg1", [4, 128, 1024], BF16) if dbg else None
    dbg2 = dout("dbg2", [4, 128, 1024], BF16) if dbg else None
    dbg3 = dout("dbg3", [4, 128, 1024], BF16) if dbg else None
    w_in_bf = nc.dram_tensor("w_in_bf", [15, 128, 4096], BF16, kind="Internal")
    w_o3_bf = nc.dram_tensor("w_o3_bf", [6, 128, 4096], BF16, kind="Internal")

    ARENA = 204 * 1024
    with contextlib.ExitStack() as st:
        t16 = st.enter_context(nc.sbuf_tensor("arena", [128, ARENA // 2], BF16))
        pst = [st.enter_context(nc.psum_tensor("ps%d" % i, [128, 512], F32)) for i in range(8)]
        P = Prog(nc, ARENA)
        P.setup_mem(t16, pst)
        _build(nc, P, locals(), with_sample)
        P.emit()
    return nc, P


def _build(nc, P, T, with_sample):
    xm, xp, xsd, w_in, w_o3 = T["xm"], T["xp"], T["xsd"], T["w_in"], T["w_o3"]
    w_in_bf, w_o3_bf = T["w_in_bf"], T["w_o3_bf"]
    d_in = P.dram(None)
    d_wslab = {}
    for c0 in range(0, 7680, 512):
        d_wslab[("in", c0)] = P.dram(None)
    for m in range(3):
        for n in range(2):
            d_wslab[("o", m, n)] = P.dram(None)
    d_outs = []

    def newout():
        b = P.dram(None)
        d_outs.append(b)
        return b

    ident = P.sb(128, BF16); cmask = P.sb(128, F32); swamask = P.sb(1024, BF16)
    gcol = P.sb(8, F32); gretcol = P.sb(8, F32); gq4 = P.sb(256, F32); esrep = P.sb(16, F32)
    rstd1 = P.sb(40, F32); r2 = P.sb(40, F32); mhalf = P.sb(16, F32); rh = P.sb(40, F32)
    P.dma("pool", ident.flat(), T["identd"].ap(), [d_in], [ident])
    P.dma("pool", swamask.flat(), T["swamaskd"].ap(), [d_in], [swamask])
    P.dma("sp", cmask.flat(), T["cmaskd"].ap(), [d_in], [cmask])
    P.dma("sp", gcol.flat(8), T["norm_g"].ap(), [d_in], [gcol])
    P.dma("sp", gretcol.flat(8), T["ret_g"].ap(), [d_in], [gretcol])
    P.dma("sp", gq4.flat(), bass.AP(T["gqk"], 0, [[0, 128], [1, 256]]), [d_in], [gq4])
    P.dma("sp", esrep.flat(), bass.AP(T["sinks"], 0, [[0, 128], [1, 16]]), [d_in], [esrep])
    P.op("act", lambda e: e.activation(out=esrep.flat(), in_=esrep.flat(), func=AF.Exp), [esrep], [esrep])
    P.op("dve", lambda e: e.memset(mhalf.flat(), -0.5), [], [mhalf])

    NSLAB = 4
    slabs = [P.sb(8 * 512, BF16) for _ in range(NSLAB)]
    slab_i = [0]
    pre_slabs = []
    for j, c0 in enumerate((512, 1024, 1536)):
        sl = slabs[j]
        for kh in range(2):
            half = P.sb_at(sl.off * 2 + kh * 4096, 2048, BF16)
            P.dma("pool", half.ap([[512, 4], [1, 512]]), bass.AP(w_in, kh * 512 * 7680 + c0, [[7680, 128], [128 * 7680, 4], [1, 512]]),
                  [d_in], [half])
        pre_slabs.append(sl)
    slab_i[0] = 0

    thr = [P.dram(None) for _ in range(3)]
    thr_i = [0]

    def conv_in(c0):
        tk = thr[thr_i[0] % 3]
        thr_i[0] += 1
        P.dma("pool", bass.AP(w_in_bf, (c0 // 512) * 128 * 4096, [[4096, 128], [512, 8], [1, 512]]),
              bass.AP(w_in, c0, [[7680, 128], [128 * 7680, 8], [1, 512]]), [d_in], [d_wslab[("in", c0)], tk])

    def conv_o(m, n):
        tk = thr[thr_i[0] % 3]
        thr_i[0] += 1
        P.dma("pool", bass.AP(w_o3_bf, (2 * m + n) * 128 * 4096, [[4096, 128], [512, 8], [1, 512]]),
              bass.AP(w_o3, m * D * D + n * 512, [[D, 128], [128 * D, 8], [1, 512]]), [d_in], [d_wslab[("o", m, n)], tk])

    conv_q = []
    for c0 in (0, 512, 1024, 1536, 2048, 2560):
        conv_q.append(lambda c0=c0: conv_in(c0))
    conv_q.append(lambda: conv_o(0, 0)); conv_q.append(lambda: conv_o(0, 1))
    for c0 in (4096, 3072, 3584, 4608, 5120):
        conv_q.append(lambda c0=c0: conv_in(c0))
    conv_q.append(lambda: conv_o(1, 0)); conv_q.append(lambda: conv_o(1, 1))
    for c0 in range(5632, 7680, 512):
        conv_q.append(lambda c0=c0: conv_in(c0))
    conv_q.append(lambda: conv_o(2, 0)); conv_q.append(lambda: conv_o(2, 1))

    def emit_convs(n):
        for _ in range(n):
            if conv_q:
                conv_q.pop(0)()

    xbuf = [P.sb(1024, F32) for _ in range(2)]
    xb16 = P.sb(1024, BF16); junk = P.sb(1024, BF16)
    hT = [P.sb(8 * 512, BF16) for _ in range(2)]
    hTm1 = P.sb(8 * 128, BF16)
    arena0 = P.alloc_off(BLK * 6144)
    arena1 = P.alloc_off(BLK * 6144)
    ybuf = [P.sb(1024, F32) for _ in range(2)]
    brr = [P.sb(1024, BF16)]
    brs = [P.sb(1024, BF16)]
    br_rest0 = P.arena_top
    brr += [P.sb(1024, BF16) for _ in range(BLK - 1)]
    brs += [P.sb(1024, BF16) for _ in range(BLK - 1)]
    eye16 = P.sb(256, F32)
    pTs = P.sb(256, BF16)
    Pm = P.sb_at(br_rest0, 4096, BF16)
    Sbf2 = [P.sb_at(br_rest0 + 8192 + j * 2048, 1024, BF16) for j in range(2)]
    Kc = P.sb_at(arena0 + 6144, 4096, BF16)
    Vc = P.sb_at(arena0 + 6144 + 8192, 16 * 264, BF16)
    rtab = [P.sb(1024, F32) for _ in range(2)]
    rtab_i = [0]
    stabs = [P.sb(128, F32) for _ in range(2)]
    swt = [P.sb(256, F32) for _ in range(2)]
    tmpXr = [P.sb(512, F32) for _ in range(2)]; tmpAr = [P.sb(512, F32) for _ in range(2)]; tmpBr = [P.sb(512, F32) for _ in range(2)]
    tmpX, tmpA, tmpB = tmpXr[0], tmpAr[0], tmpBr[0]
    tmp_i = [0]
    xT = P.sb(1024, BF16)
    tok = P.sb(1024, BF16)
    Sst = P.sb(1024, F32); Sbf = P.sb(1024, BF16)
    scT = P.sb(512, BF16); small = P.sb(64, F32); small2 = P.sb(64, F32); small3 = P.sb(64, F32)
    qkT2 = [P.sb(1024, BF16) for _ in range(2)]
    scT2 = [scT, P.sb(512, BF16)]
    qT = qkT2[0]
    Qm = qkT2[1]
    Kb = scT2
    kT = [P.sb(512, BF16) for _ in range(5)]
    Sf32 = [P.sb_at(tmpXr[0].off * 4 + j * 4096, 1024, F32) for j in range(2)]
    Sf32 += [P.sb_at(hT[1].off * 2 + j * 4096, 1024, F32) for j in range(2)]
    Sbf2 += [P.sb_at(kT[0].off * 2 + j * 2048, 1024, BF16) for j in range(2)]
    vaug = [P.sb(4 * 66, BF16) for _ in range(5)]
    pT = [[P.sb(512, BF16) for _ in range(2)] for _ in range(2)]
    KTr = [P.sb_at(pT[j][0].off * 2, 1024, BF16) for j in range(2)]
    gate2 = P.sb(1024, BF16)
    kf32 = P.sb(256, F32); vf32 = P.sb(256, F32)
    sg16 = [P.sb(512, BF16) for _ in range(2)]
    for v in vaug:
        P.op("dve", lambda e, v=v: e.memset(v.flat(), 1.0), [], [v])

    def arena_of(which):
        base = arena0 if which == 0 else arena1

        def arena(i, boff, n, dt):
            return P.sb_at(base + i * 6144 + boff, n, dt)
        return arena

    arena = arena_of(0)

    B_TP = 0
    mm_banks = [1, 2]
    mm_i = [0]

    wide_banks = [1, 2]
    wide_i = [0]

    def next_mm(wide=False):
        if wide:
            b = wide_banks[wide_i[0] % len(wide_banks)]
            wide_i[0] += 1
            return b
        b = mm_banks[mm_i[0] % len(mm_banks)]
        mm_i[0] += 1
        return b

    oslab_i = [0]
    no_prefetch = [False]

    BLOCK_SEQ = [0, 512, 1024, 1536, 2048, 2560, 4096, 3072, 3584, 4608, 5120, 5632, 6144, 6656, 7168]
    slab_seq = []
    slab_pos = [0]
    slab_pending = {}

    def issue_in_slab(pos):
        c0 = slab_seq[pos]
        s = slabs[pos % 2]
        src = bass.AP(w_in_bf, (c0 // 512) * 128 * 4096, [[4096, 128], [1, 4096]])
        P.dma("sp", s.flat(4096), src, [d_wslab[("in", c0)]], [s])
        slab_pending[pos] = s

    def load_slab(kind, *a):
        if kind == "in":
            pos = slab_pos[0]
            assert slab_seq[pos] == a[0], (pos, slab_seq[pos], a[0])
            if pos not in slab_pending:
                issue_in_slab(pos)
            s = slab_pending.pop(pos)
            slab_pos[0] += 1
            if pos + 1 < len(slab_seq) and not (no_prefetch[0]):
                issue_in_slab(pos + 1)
            return s
        else:
            m, n = a
            s = slabs[2 + n]
            src = bass.AP(w_o3_bf, (2 * m + n) * 128 * 4096, [[4096, 128], [1, 4096]])
            key = d_wslab[("o", m, n)]
        P.dma("sp", s.flat(4096), src, [key], [s])
        return s

    def rstd_pow(col_buf, col, np_):
        P.op("pool", lambda e: e.tensor_tensor(out=col_buf.flat(1, off=col, np_=np_), in0=col_buf.flat(1, off=col, np_=np_),
                                               in1=mhalf.flat(1, np_=np_), op=ALU.pow), [col_buf, mhalf], [col_buf])

    def xload(src_t, row0, np_, xslot):
        xb_ = xbuf[xslot]
        P.dma("sp", xb_.flat(np_=np_), bass.AP(src_t, row0 * D, [[D, np_], [1, D]]), [d_in], [xb_])

    def stage0(src_t, row0, np_, slot, hbuf, hcol, xslot, preloaded=False):
        xb_ = xbuf[xslot]
        if not preloaded:
            P.dma("sp", xb_.flat(np_=np_), bass.AP(src_t, row0 * D, [[D, np_], [1, D]]), [d_in], [xb_])
        P.op("act", lambda e: e.activation(out=junk.flat(np_=np_), in_=xb_.flat(np_=np_), func=AF.Square,
                                           accum_out=rstd1.flat(1, off=slot, np_=np_)), [xb_], [junk, rstd1])
        P.op("dve", lambda e: e.tensor_scalar(rstd1.flat(1, off=slot, np_=np_), rstd1.flat(1, off=slot, np_=np_), 1.0 / D, EPS,
                                              op0=ALU.mult, op1=ALU.add), [rstd1], [rstd1])
        rstd_pow(rstd1, slot, np_)
        P.op("dve", lambda e: e.scalar_tensor_tensor(out=r2.flat(1, off=slot, np_=np_), in0=rstd1.flat(1, off=slot, np_=np_),
                                                     scalar=1.0 / 64, in1=rstd1.flat(1, off=slot, np_=np_),
                                                     op0=ALU.mult, op1=ALU.mult), [rstd1], [r2])
        P.op("dve", lambda e: e.tensor_scalar(rh.flat(1, off=slot, np_=np_), rstd1.flat(1, off=slot, np_=np_), 0.5, None, op0=ALU.mult), [rstd1], [rh])
        P.op("dve", lambda e: e.tensor_copy(xb16.flat(np_=np_), xb_.flat(np_=np_)), [xb_], [xb16])
        pt = P.psum(B_TP, BF16)
        for j in range(8):
            P.op("pe", lambda e, j=j: e.transpose(pt.flat(np_, off=j * np_), xb16.flat(128, off=j * 128, np_=np_),
                                                  ident.ap([[1, np_]], np_=np_)), [xb16, ident], [pt])
        P.op("dve", lambda e: e.tensor_tensor(out=hbuf.ap([[512, 8], [1, np_]], off=hcol), in0=pt.ap([[np_, 8], [1, np_]]),
                                              in1=gcol.ap([[1, 8], [0, np_]]), op=ALU.mult), [pt, gcol], [hbuf])

    inproj_wide = [True]

    def inproj(slab, hbuf, hcol, np_, kst=512):
        pb = P.psum(next_mm(wide=inproj_wide[0]), F32)
        for k in range(8):
            P.op("pe", lambda e, k=k: e.matmul(pb.flat(512, np_=np_), hbuf.ap([[1, np_]], off=k * kst + hcol),
                                               slab.flat(512, off=k * 512), start=(k == 0), stop=(k == 7)),
                 [hbuf, slab], [pb])
        return pb

    def ret_rope(pb, slot, tabC, tabS, out_bf, np_):
        A = tmpAr[tmp_i[0] % 2]; B = tmpBr[tmp_i[0] % 2]
        tmp_i[0] += 1
        tb_ = tabC_buf[0]
        P.op("dve", lambda e: e.scalar_tensor_tensor(out=A.flat(np_=np_), in0=pb.flat(np_=np_), scalar=rstd1.flat(1, off=slot, np_=np_),
                                                     in1=tabC, op0=ALU.mult, op1=ALU.mult), [pb, rstd1, tb_], [A])
        for hf in range(2):
            P.op("dve", lambda e, hf=hf: e.scalar_tensor_tensor(out=B.ap([[128, 4], [1, 64]], off=hf * 64, np_=np_),
                                                                in0=pb.ap([[128, 4], [1, 64]], off=(1 - hf) * 64, np_=np_),
                                                                scalar=rstd1.flat(1, off=slot, np_=np_), in1=tabS(hf),
                                                                op0=ALU.mult, op1=ALU.mult), [pb, rstd1, tb_], [B])
        P.op("dve", lambda e: e.tensor_tensor(out=out_bf.flat(512, np_=np_), in0=A.flat(np_=np_), in1=B.flat(np_=np_),
                                              op=ALU.add), [A, B], [out_bf])

    tabC_buf = [None]

    def transposes(src_bf, np_, n=8, width=128, dst=None, eng="act", colscale=None):
        if dst is None:
            dst = xT
        pt = P.psum(B_TP, BF16)
        for j in range(n):
            P.op("pe", lambda e, j=j: e.transpose(pt.ap([[1, np_]], off=j * np_, np_=width),
                                                  src_bf.flat(width, off=j * width, np_=np_),
                                                  ident.ap([[1, np_]], np_=np_)), [src_bf, ident], [pt])
        if colscale is not None:
            P.op("dve", lambda e: e.tensor_tensor(out=dst.ap([[np_, n], [1, np_]], np_=width), in0=pt.ap([[np_, n], [1, np_]], np_=width),
                                                  in1=colscale.ap([[1, n], [0, np_]], np_=width), op=ALU.mult), [pt, colscale], [dst])
        elif eng == "act":
            P.op("act", lambda e: e.activation(out=dst.flat(n * np_, np_=width), in_=pt.flat(n * np_, np_=width), func=AF.Copy),
                 [pt], [dst])
        else:
            P.op("dve", lambda e: e.tensor_copy(dst.flat(n * np_, np_=width), pt.flat(n * np_, np_=width)), [pt], [dst])
        return dst

    op_banks = [[1, 2]]
    op_i = [0]

    def outproj(m, src_T, np_, evac):
        for n in range(2):
            s = oslabs[(m, n)]
            pb = P.psum(op_banks[0][op_i[0] % len(op_banks[0])], F32)
            op_i[0] += 1
            for k in range(8):
                P.op("pe", lambda e, k=k, s=s, pb=pb: e.matmul(pb.flat(512, np_=np_), src_T.ap([[1, np_]], off=k * np_),
                                                               s.flat(512, off=k * 512), start=(k == 0), stop=(k == 7)),
                     [src_T, s], [pb])
            evac(n, pb)

    oslabs = {}

    pstb = [P.psum(4 + h, F32) for h in range(4)]
    def pre_block(blk):
        hb = hT[blk % 2]
        xload(xp, blk * BLK * 128, 128, (blk * BLK) % 2)
        for i in range(BLK):
            t = blk * BLK + i
            if i + 1 < BLK:
                xload(xp, (t + 1) * 128, 128, (t + 1) % 2)
            stage0(xp, t * 128, 128, 16 + t, hb, i * 128, t % 2, preloaded=True)
        if blk == NT // BLK - 1:
            P.op("pool", lambda e: e.tensor_copy(hTm1.ap([[128, 8], [1, 128]]), hb.ap([[512, 8], [1, 128]], off=3 * 128)), [hb], [hTm1])
        kr = [arena(i, 1024, 512, BF16) for i in range(BLK)]
        vv = [arena(i, 2048, 1024, BF16) for i in range(BLK)]
        s_rk = pre_slabs[0]
        for i in range(BLK):
            t = blk * BLK + i
            tb = rtab[rtab_i[0] % 2]
            rtab_i[0] += 1
            P.dma("sp", tb.flat(1024), bass.AP(T["rtab_pre"], t * 128 * 1024, [[1024, 128], [1, 1024]]), [d_in], [tb])
            pb = inproj(s_rk, hb, i * 128, 128)
            tabC_buf[0] = tb
            ret_rope(pb, 16 + t, tb.flat(512), lambda hf, tb=tb: tb.ap([[128, 4], [1, 64]], off=512 + hf * 64), kr[i], 128)
            if t < 8:
                emit_convs(1)
        for n in range(2):
            s_rv = pre_slabs[1 + n]
            for i in range(BLK):
                t = blk * BLK + i
                pb = inproj(s_rv, hb, i * 128, 128)
                P.op("act", lambda e, pb=pb, i=i, n=n, t=t: e.activation(out=vv[i].flat(512, off=n * 512), in_=pb.flat(), func=AF.Copy,
                                                                      scale=rstd1.flat(1, off=16 + t)), [pb, rstd1], [vv[i]])
        for i in range(BLK):
            t = blk * BLK + i
            for h in range(4):
                P.op("pe", lambda e, h=h, i=i, t=t: e.matmul(pstb[h].flat(256), kr[i].flat(128, off=h * 128),
                                                          vv[i].flat(256, off=h * 256), start=(t == 0), stop=(t == NT - 1)),
                     [kr[i], vv[i]], [pstb[h]])

    inproj_wide[0] = False
    mm_banks[:] = [1, 2, 3]
    for blk in range(NT // BLK):
        pre_block(blk)
    inproj_wide[0] = True
    mm_banks[:] = [1, 2]
    for j in range(4):
        P.op("dve", lambda e, j=j: e.tensor_copy(Sst.flat(256, off=j * 256), pstb[j].flat(256)), [pstb[j]], [Sst])
    P.op("act", lambda e: e.activation(out=Sbf.flat(), in_=Sst.flat(), func=AF.Copy), [Sst], [Sbf])

    nblocks = NT // BLK + (1 if with_sample else 0)
    slab_seq.extend(BLOCK_SEQ * nblocks)
    d_yp = newout(); d_rsp = newout(); d_kp = newout(); d_vp = newout()
    gtile = [0]
    kc_keys = []
    vc_keys = []
    swt_i = [0]

    def do_stage0_block(blk):
        hb = hT[blk % 2]
        if blk < NT // BLK:
            for i in range(BLK):
                t = blk * BLK + i
                stage0(xm, t * 128, 128, t, hb, i * 128, t % 2)
        else:
            stage0(xsd, 0, 16, 32, hb, 0, 0)

    G = dict(locals())

    def blkinfo(blk):
        samp = blk >= NT // BLK
        return dict(samp=samp, np_=16 if samp else 128, ntl=1 if samp else BLK, hb=hT[blk % 2],
                    slots=[32] if samp else [blk * BLK + i for i in range(BLK)], blk=blk)

    def gen_S0(blk):
        hb = hT[blk % 2]
        if blk < NT // BLK:
            xload(xm, blk * BLK * 128, 128, (blk * BLK) % 2)
            for i in range(BLK):
                t = blk * BLK + i
                if i + 1 < BLK:
                    xload(xm, (t + 1) * 128, 128, (t + 1) % 2)
                stage0(xm, t * 128, 128, t, hb, i * 128, t % 2, preloaded=True)
                yield
        else:
            stage0(xsd, 0, 16, 32, hb, 0, 0)
            yield

    def passA_bufs(blk):
        ar = arena_of((3 * blk) % 2)
        n = 1 if blk >= NT // BLK else BLK
        return dict(qr=[ar(i, 0, 512, BF16) for i in range(n)], kr=[ar(i, 1024, 512, BF16) for i in range(n)],
                    vv=[ar(i, 2048, 1024, BF16) for i in range(n)], GG=[ar(i, 4096, 1024, BF16) for i in range(n)])

    def gen_A1(blk):
        I = blkinfo(blk)
        samp, np_, ntl, hb, slots = I["samp"], I["np_"], I["ntl"], I["hb"], I["slots"]
        Bf = passA_bufs(blk)
        qr, kr, vv, GG = Bf["qr"], Bf["kr"], Bf["vv"], Bf["GG"]
        wide_banks[:] = [1, 2, 3, 4]
        if samp:
            sample_prefetch(P, {**G, **I}, T, d_in)
        if with_sample and blk == 1:
            sample_cache_copy(P, T, d_in, newout)
        tspecs = [(o0, i) for o0 in (0, 1024) for i in range(ntl)]
        tbufs = {}

        def tload(j):
            o0, i = tspecs[j]
            tb = rtab[rtab_i[0] % 2]
            rtab_i[0] += 1
            if samp:
                P.dma("sp", tb.flat(1024, np_=16), bass.AP(T["rtab_samp"], o0, [[2048, 16], [1, 1024]]), [d_in], [tb])
            else:
                P.dma("sp", tb.flat(1024), bass.AP(T["rtab_main"], slots[i] * 128 * 2048 + o0, [[2048, 128], [1, 1024]]), [d_in], [tb])
            tbufs[j] = tb

        tload(0)
        tj = 0
        for nm, c0 in (("q", 0), ("k", 512)):
            s = load_slab("in", c0)
            o0 = 0 if nm == "q" else 1024
            for i in range(ntl):
                tb = tbufs[tj]
                if tj + 1 < len(tspecs):
                    tload(tj + 1)
                tj += 1
                pb = inproj(s, hb, i * 128, np_)
                tabC_buf[0] = tb
                ret_rope(pb, slots[i], tb.flat(512, np_=np_),
                         lambda hf, tb=tb: tb.ap([[128, 4], [1, 64]], off=512 + hf * 64, np_=np_),
                         qr[i] if nm == "q" else kr[i], np_)
                yield
        for n in range(2):
            s = load_slab("in", 1024 + n * 512)
            for i in range(ntl):
                pb = inproj(s, hb, i * 128, np_)
                P.op("act", lambda e, pb=pb, i=i, n=n: e.activation(out=vv[i].flat(512, off=n * 512, np_=np_), in_=pb.flat(np_=np_), func=AF.Copy,
                                                                   scale=rstd1.flat(1, off=slots[i], np_=np_)), [pb, rstd1], [vv[i]])
                yield
        for n in range(2):
            s = load_slab("in", 2048 + n * 512)
            for i in range(ntl):
                pb = inproj(s, hb, i * 128, np_)
                P.op("act", lambda e, pb=pb, i=i, n=n: e.activation(out=GG[i].flat(512, off=n * 512, np_=np_), in_=pb.flat(np_=np_), func=AF.Silu,
                                                                   scale=rstd1.flat(1, off=slots[i], np_=np_)), [pb, rstd1], [GG[i]])
                yield
        wide_banks[:] = [1, 2]

    def gen_A2(blk):
        I = blkinfo(blk)
        samp, np_, ntl = I["samp"], I["np_"], I["ntl"]
        Bf = passA_bufs(blk)
        L = {**G, **I, **Bf}
        oslabs[(0, 0)] = load_slab("o", 0, 0)
        oslabs[(0, 1)] = load_slab("o", 0, 1)
        op_banks[0] = [1, 2] if samp else [4, 5]
        if not samp:
            ret_mixer_front(P, L, 0)
            yield
        for i in range(ntl):
            if not samp:
                yield from ret_mixer_prompt(P, L, i, (lambda i=i: ret_mixer_front(P, L, i + 1)) if i + 1 < ntl else None)
            else:
                ret_mixer_sample(P, L, T, d_in, newout)
            transposes(tok, np_, colscale=gretcol)
            yield

            def ev(n, pb, i=i):
                P.op("act", lambda e: e.activation(out=brr[i].flat(512, off=n * 512, np_=np_), in_=pb.flat(np_=np_), func=AF.Copy), [pb], [brr[i]])
            outproj(0, xT, np_, ev)
            yield
        if blk == NT // BLK - 1:
            for h in range(4):
                P.op("dve", lambda e, h=h: e.tensor_scalar(Sst.flat(256, off=h * 256), Sst.flat(256, off=h * 256), float(GAM[h] ** 2048), None,
                                                           op0=ALU.mult), [Sst], [Sst])
            P.dma("sp", bass.AP(T["rsp"], 0, [[256, 128], [128 * 256, 4], [1, 256]]), Sst.ap([[256, 4], [1, 256]]), [Sst], [d_rsp])

    def gen_C1(blk):
        I = blkinfo(blk)
        samp, np_, ntl, hb, slots = I["samp"], I["np_"], I["ntl"], I["hb"], I["slots"]
        ar = arena_of((3 * blk + 2) % 2)
        mbuf = [ar(i, 0, 1024, F32) for i in range(ntl)]
        tokC = [ar(i, 4096, 1024, BF16) for i in range(ntl)]
        for n in range(2):
            s = load_slab("in", 5632 + n * 512)
            for i in range(ntl):
                pb = inproj(s, hb, i * 128, np_)
                sg = sg16[(n * ntl + i) % 2]
                P.op("act", lambda e, pb=pb, i=i, sg=sg: e.activation(out=sg.flat(np_=np_), in_=pb.flat(np_=np_), func=AF.Tanh,
                                                                     scale=rh.flat(1, off=slots[i], np_=np_)), [pb, rh], [sg])
                P.op("dve", lambda e, i=i, n=n, sg=sg: e.scalar_tensor_tensor(out=mbuf[i].flat(512, off=n * 512, np_=np_), in0=sg.flat(np_=np_), scalar=1.0,
                                                                             in1=brr[i].flat(512, off=n * 512, np_=np_), op0=ALU.add, op1=ALU.mult), [sg, brr[i]], [mbuf[i]])
                yield

    def gen_C1b(blk):
        I = blkinfo(blk)
        samp, np_, ntl, hb, slots = I["samp"], I["np_"], I["ntl"], I["hb"], I["slots"]
        ar = arena_of((3 * blk + 2) % 2)
        mbuf = [ar(i, 0, 1024, F32) for i in range(ntl)]
        tokC = [ar(i, 4096, 1024, BF16) for i in range(ntl)]
        oslabs[(2, 0)] = load_slab("o", 2, 0)
        oslabs[(2, 1)] = load_slab("o", 2, 1)
        for n in range(2):
            s = load_slab("in", 6656 + n * 512)
            for i in range(ntl):
                pb = inproj(s, hb, i * 128, np_)
                sg = sg16[(n * ntl + i) % 2]
                tA = tmpAr[(n * ntl + i) % 2]
                P.op("act", lambda e, pb=pb, i=i, sg=sg: e.activation(out=sg.flat(np_=np_), in_=pb.flat(np_=np_), func=AF.Tanh,
                                                                     scale=rh.flat(1, off=slots[i], np_=np_)), [pb, rh], [sg])
                P.op("dve", lambda e, i=i, n=n, sg=sg, tA=tA: e.scalar_tensor_tensor(out=tA.flat(np_=np_), in0=sg.flat(np_=np_), scalar=1.0,
                                                                                    in1=brs[i].flat(512, off=n * 512, np_=np_), op0=ALU.add, op1=ALU.mult), [sg, brs[i]], [tA])
                P.op("pool", lambda e, i=i, n=n, tA=tA: e.tensor_tensor(out=tokC[i].flat(512, off=n * 512, np_=np_), in0=mbuf[i].flat(512, off=n * 512, np_=np_),
                                                                       in1=tA.flat(np_=np_), op=ALU.add), [mbuf[i], tA], [tokC[i]])
                yield

    def gen_C2(blk):
        I = blkinfo(blk)
        samp, np_, ntl, slots = I["samp"], I["np_"], I["ntl"], I["slots"]
        ar = arena_of((3 * blk + 2) % 2)
        tokC = [ar(i, 4096, 1024, BF16) for i in range(ntl)]
        op_banks[0] = [5, 6, 7]
        for i in range(ntl):
            xb_ = ybuf[i % 2]
            if samp:
                P.dma("sp", xb_.flat(np_=16), xsd.ap(), [d_in], [xb_])
            else:
                P.dma("sp", xb_.flat(), bass.AP(xm, slots[i] * 128 * D, [[D, 128], [1, D]]), [d_in], [xb_])
            transposes(tokC[i], np_)
            yield

            def ev(n, pb, xb_=xb_):
                P.op("dve", lambda e: e.scalar_tensor_tensor(out=xb_.flat(512, off=n * 512, np_=np_), in0=pb.flat(np_=np_), scalar=0.5,
                                                             in1=xb_.flat(512, off=n * 512, np_=np_), op0=ALU.mult, op1=ALU.add), [pb, xb_], [xb_])
            outproj(2, xT, np_, ev)
            if samp:
                d_ys = newout()
                P.dma("sp", T["ysd"].ap(), xb_.flat(np_=16), [xb_], [d_ys])
            else:
                P.dma("pool", bass.AP(T["yp"], slots[i] * 128 * D, [[D, 128], [1, D]]), xb_.flat(), [xb_], [d_yp])
            yield
        op_banks[0] = [1, 2]

    def run(g):
        for _ in g:
            pass

    def inter(g1, g2, r1=1, r2=1):
        live = [[g1, r1], [g2, r2]]
        while live:
            for ent in list(live):
                for _ in range(ent[1]):
                    try:
                        next(ent[0])
                    except StopIteration:
                        live.remove(ent)
                        break

    def chain(*gs):
        for g in gs:
            yield from g

    no_prefetch[0] = True
    run(gen_S0(0))
    run(gen_A1(0))

    def with_convs(g):
        for _ in g:
            emit_convs(1)
            yield
        emit_convs(100)
    for blk in range(nblocks):
        I = blkinfo(blk)
        no_prefetch[0] = (blk == 0)
        gB1, gB2 = swa_pass(P, {**G, **I, "arena": arena_of((3 * blk + 1) % 2)}, T, d_in, newout)
        if I["samp"]:
            run(gen_A2(blk)); run(gB1); run(gB2); run(gen_C1(blk)); run(gen_C1b(blk)); run(gen_C2(blk))
            continue
        inter(with_convs(gen_A2(blk)) if blk == 0 else gen_A2(blk), gB1, 1, 1)
        inter(gB2, gen_C1(blk), 1, 1)
        if blk + 1 < nblocks:
            inter(gen_C1b(blk), gen_S0(blk + 1), 1, 1)
            inter(gen_C2(blk), gen_A1(blk + 1), 1, 3)
        else:
            run(gen_C1b(blk))
            run(gen_C2(blk))
    P.op("sp", None, d_outs, [])


def ret_mixer_front(P, L, i):
    qr, kr = L["qr"][i], L["kr"][i]
    ident, cmask = L["ident"], L["cmask"]
    qkT = L["qkT2"][i % 2]
    scT = L["scT2"][i % 2]
    pt = P.psum(0, BF16)
    for h in range(4):
        P.op("pe", lambda e, h=h: e.transpose(pt.flat(128, off=h * 128), qr.flat(128, off=h * 128), ident.flat()), [qr, ident], [pt])
    for h in range(4):
        P.op("pe", lambda e, h=h: e.transpose(pt.flat(128, off=(4 + h) * 128), kr.flat(128, off=h * 128), ident.flat()), [kr, ident], [pt])
    P.op("act", lambda e: e.activation(out=qkT.flat(), in_=pt.flat(), func=AF.Copy), [pt], [qkT])
    psc = P.psum(3, F32)
    for h in range(4):
        P.op("pe", lambda e, h=h: e.matmul(psc.flat(128, off=h * 128), qkT.flat(128, off=(4 + h) * 128), qkT.flat(128, off=h * 128),
                                           start=True, stop=True), [qkT], [psc])
    P.op("dve", lambda e: e.tensor_tensor(out=scT.ap([[128, 4], [1, 128]]), in0=psc.ap([[128, 4], [1, 128]]),
                                          in1=cmask.ap([[0, 4], [1, 128]]), op=ALU.mult), [psc, cmask], [scT])


def ret_mixer_prompt(P, L, i, mid=None):
    qr, kr, vv, GG = L["qr"][i], L["kr"][i], L["vv"][i], L["GG"][i]
    Sst, Sbf, small, junk, tok, mhalf = (L[k] for k in ("Sst", "Sbf", "small", "junk", "tok", "mhalf"))
    qkT = L["qkT2"][i % 2]
    scT = L["scT2"][i % 2]
    po = [P.psum(4, F32), P.psum(5, F32)]
    for h in range(4):
        pb = po[h // 2]
        P.op("pe", lambda e, h=h, pb=pb: e.matmul(pb.flat(256, off=(h % 2) * 256), scT.flat(128, off=h * 128), vv.flat(256, off=h * 256),
                                                  start=True, stop=False), [scT, vv], [pb])
        P.op("pe", lambda e, h=h, pb=pb: e.matmul(pb.flat(256, off=(h % 2) * 256), qkT.flat(128, off=h * 128), Sbf.flat(256, off=h * 256),
                                                  start=False, stop=True), [qkT, Sbf], [pb])
    yield
    pstb = [P.psum(6, F32), P.psum(7, F32)]
    for h in range(4):
        pb = pstb[h // 2]
        P.op("pe", lambda e, h=h, pb=pb: e.matmul(pb.flat(256, off=(h % 2) * 256), kr.flat(128, off=h * 128), vv.flat(256, off=h * 256),
                                                  start=True, stop=True), [kr, vv], [pb])
    yield
    for j in range(2):
        P.op("dve", lambda e, j=j: e.tensor_tensor(out=Sst.flat(512, off=j * 512), in0=pstb[j].flat(), in1=Sst.flat(512, off=j * 512),
                                                   op=ALU.add), [pstb[j], Sst], [Sst])
    P.op("act", lambda e: e.activation(out=Sbf.flat(), in_=Sst.flat(), func=AF.Copy), [Sst], [Sbf])
    yield
    for h in range(4):
        pb = po[h // 2]
        P.op("act", lambda e, h=h, pb=pb: e.activation(out=junk.flat(256), in_=pb.flat(256, off=(h % 2) * 256), func=AF.Square,
                                                       accum_out=small.flat(1, off=h)), [pb], [junk, small])
    P.op("dve", lambda e: e.tensor_scalar(small.flat(4), small.flat(4), 1.0 / 256, EPS, op0=ALU.mult, op1=ALU.add), [small], [small])
    P.op("pool", lambda e: e.tensor_tensor(out=small.flat(4), in0=small.flat(4), in1=mhalf.flat(4), op=ALU.pow), [small, mhalf], [small])
    for h in range(4):
        pb = po[h // 2]
        P.op("dve", lambda e, h=h, pb=pb: e.scalar_tensor_tensor(out=tok.flat(256, off=h * 256), in0=pb.flat(256, off=(h % 2) * 256),
                                                                 scalar=small.flat(1, off=h), in1=GG.flat(256, off=h * 256),
                                                                 op0=ALU.mult, op1=ALU.mult), [pb, small, GG], [tok])
    yield
    if mid is not None:
        mid()
        yield


def ret_mixer_sample(P, L, T, d_in, newout):
    qr, kr, vv, GG = L["qr"][0], L["kr"][0], L["vv"][0], L["GG"][0]
    ident, small, small2, junk, tok, mhalf, tmpA, Sst = (L[k] for k in ("ident", "small", "small2", "junk", "tok", "mhalf", "tmpA", "Sst"))
    eye16, Qm, Sf32, Kb, Sbf2, qT, transposes = (L[k] for k in ("eye16", "Qm", "Sf32", "Kb", "Sbf2", "qT", "transposes"))
    N = 16
    P.dma("sp", eye16.flat(), T["eyed"].ap(), [d_in], [eye16])
    P.op("dve", lambda e: e.tensor_tensor(out=tmpA.flat(512, np_=N), in0=qr.flat(512, np_=N), in1=kr.flat(512, np_=N), op=ALU.mult), [qr, kr], [tmpA])
    P.op("dve", lambda e: e.tensor_reduce(out=small.flat(4, off=8, np_=N), in_=tmpA.ap([[128, 4], [1, 128]], np_=N), op=ALU.add, axis=AX.X), [tmpA], [small])
    transposes(qr, N, n=4, width=128, dst=qT)
    for h in range(4):
        P.op("dve", lambda e, h=h: e.tensor_tensor(out=Qm.ap([[16, 16], [1, 16]], off=h * 256), in0=qT.ap([[0, 16], [1, 16]], off=h * 16),
                                                   in1=eye16.ap([[16, 16], [1, 16]]), op=ALU.mult), [qT, eye16], [Qm])
    po = [P.psum(4 + h, F32) for h in range(4)]
    d_rss = newout()
    state = T["state"]
    for b in range(N):
        sbf = Sbf2[b % 4]
        sf = Sf32[b % 4]
        kb = Kb[b % 2]
        src = bass.AP(state, b * 4 * 128 * 256, [[256, 128], [128 * 256, 4], [1, 256]])
        P.dma("pool", sbf.ap([[256, 4], [1, 256]]), src, [d_in], [sbf])
        P.dma("sp", sf.ap([[256, 4], [1, 256]]), src, [d_in], [sf])
        for h in range(4):
            P.op("pe", lambda e, h=h, b=b, sbf=sbf: e.matmul(po[h].flat(256, np_=N), Qm.ap([[1, 16]], off=h * 256 + b * 16), sbf.flat(256, off=h * 256),
                                                            start=(b == 0), stop=(b == N - 1)), [Qm, sbf], [po[h]])
        P.op("dve", lambda e, b=b, kb=kb: e.tensor_scalar(kb.flat(512, np_=N), kr.flat(512, np_=N), ident.ap([[1, 1]], off=b, np_=N), None, op0=ALU.mult),
             [kr, ident], [kb])
        for j in range(2):
            pb = P.psum(1 + (2 * b + j) % 3, F32)
            for hh in range(2):
                h = 2 * j + hh
                P.op("pe", lambda e, h=h, hh=hh, pb=pb, kb=kb: e.matmul(pb.flat(256, off=hh * 256), kb.flat(128, off=h * 128, np_=N), vv.flat(256, off=h * 256, np_=N),
                                                                     start=True, stop=True), [kb, vv], [pb])
            for hh in range(2):
                h = 2 * j + hh
                P.op("dve", lambda e, h=h, hh=hh, pb=pb, sf=sf: e.scalar_tensor_tensor(out=sf.flat(256, off=h * 256), in0=sf.flat(256, off=h * 256), scalar=float(GAM[h]),
                                                                                    in1=pb.flat(256, off=hh * 256), op0=ALU.mult, op1=ALU.add), [sf, pb], [sf])
        P.dma("act", bass.AP(T["rss"], b * 4 * 128 * 256, [[256, 128], [128 * 256, 4], [1, 256]]), sf.ap([[256, 4], [1, 256]]), [sf], [d_rss])
    o32 = Sst
    for h in range(4):
        P.op("dve", lambda e, h=h: e.tensor_scalar(o32.flat(256, off=h * 256, np_=N), vv.flat(256, off=h * 256, np_=N), small.flat(1, off=8 + h, np_=N), None, op0=ALU.mult),
             [vv, small], [o32])
        P.op("dve", lambda e, h=h: e.scalar_tensor_tensor(out=o32.flat(256, off=h * 256, np_=N), in0=po[h].flat(256, np_=N), scalar=float(GAM[h]),
                                                          in1=o32.flat(256, off=h * 256, np_=N), op0=ALU.mult, op1=ALU.add), [po[h], o32], [o32])
    for h in range(4):
        P.op("act", lambda e, h=h: e.activation(out=junk.flat(256, np_=N), in_=o32.flat(256, off=h * 256, np_=N), func=AF.Square,
                                                accum_out=small2.flat(1, off=h, np_=N)), [o32], [junk, small2])
    P.op("dve", lambda e: e.tensor_scalar(small2.flat(4, np_=N), small2.flat(4, np_=N), 1.0 / 256, EPS, op0=ALU.mult, op1=ALU.add), [small2], [small2])
    P.op("pool", lambda e: e.tensor_tensor(out=small2.flat(4, np_=N), in0=small2.flat(4, np_=N), in1=mhalf.flat(4, np_=N), op=ALU.pow), [small2, mhalf], [small2])
    for h in range(4):
        P.op("dve", lambda e, h=h: e.scalar_tensor_tensor(out=tok.flat(256, off=h * 256, np_=N), in0=o32.flat(256, off=h * 256, np_=N),
                                                          scalar=small2.flat(1, off=h, np_=N), in1=GG.flat(256, off=h * 256, np_=N),
                                                          op0=ALU.mult, op1=ALU.mult), [o32, small2, GG], [tok])


def swa_pass(P, L, T, d_in, newout):
    samp, np_, ntl, hb, slots, blk = (L[k] for k in ("samp", "np_", "ntl", "hb", "slots", "blk"))
    arena, load_slab, inproj, transposes, outproj, oslabs = (L[k] for k in ("arena", "load_slab", "inproj", "transposes", "outproj", "oslabs"))
    ident, swamask, rstd1, r2, mhalf, gq4, esrep = (L[k] for k in ("ident", "swamask", "rstd1", "r2", "mhalf", "gq4", "esrep"))
    tmpX, tmpA, tmpB, small, small2, junk, tok, xT = (L[k] for k in ("tmpX", "tmpA", "tmpB", "small", "small2", "junk", "tok", "xT"))
    qT, kT, vaug, pT, gate2, kf32, vf32, stabs, swt, brs, gtile = (L[k] for k in
        ("qT", "kT", "vaug", "pT", "gate2", "kf32", "vf32", "stabs", "swt", "brs", "gtile"))
    q16 = [arena(i, 0, 1024, BF16) for i in range(ntl)]
    kdup = [arena(i, 2048, 512, BF16) for i in range(ntl)]
    gate = [arena(i, 4096, 1024, BF16) for i in range(ntl)]
    swtb = {}
    kt_todo = []

    def load_tab(tslot, key):
        r = L["swt_i"][0] % 2
        L["swt_i"][0] += 1
        sb_, w = stabs[r], swt[r]
        P.dma("sp", sb_.flat(np_=np_), bass.AP(T["stab"], tslot * 128 * 128, [[128, np_], [1, 128]]), [d_in], [sb_])
        P.op("pool", lambda e: e.tensor_tensor(out=w.ap([[128, 2], [1, 128]], np_=np_), in0=sb_.ap([[0, 2], [1, 128]], np_=np_),
                                               in1=gq4.ap([[128, 2], [1, 128]], np_=np_), op=ALU.mult), [sb_, gq4], [w])
        swtb[key] = w

    def qk_norm_rope(pb, nh, slot, w, coff, out_ap_fn, outbuf, ssbuf, soff):
        n = nh * 64
        ti = L["tmp_i"][0] % 2
        L["tmp_i"][0] += 1
        tmpX, tmpA, tmpB = L["tmpXr"][ti], L["tmpAr"][ti], L["tmpBr"][ti]
        P.op("act", lambda e: e.activation(out=tmpX.flat(n, np_=np_), in_=pb.flat(n, np_=np_), func=AF.Square), [pb], [tmpX])
        P.op("dve", lambda e: e.tensor_reduce(out=ssbuf.flat(nh, off=soff, np_=np_), in_=tmpX.ap([[64, nh], [1, 64]], np_=np_),
                                              op=ALU.add, axis=AX.X), [tmpX], [ssbuf])
        P.op("dve", lambda e: e.tensor_scalar(ssbuf.flat(nh, off=soff, np_=np_), ssbuf.flat(nh, off=soff, np_=np_),
                                              r2.flat(1, off=slot, np_=np_), EPS, op0=ALU.mult, op1=ALU.add), [ssbuf, r2], [ssbuf])
        P.op("pool", lambda e: e.tensor_tensor(out=ssbuf.flat(nh, off=soff, np_=np_), in0=ssbuf.flat(nh, off=soff, np_=np_),
                                               in1=mhalf.flat(nh, np_=np_), op=ALU.pow), [ssbuf, mhalf], [ssbuf])
        P.op("dve", lambda e: e.tensor_scalar(ssbuf.flat(nh, off=soff, np_=np_), ssbuf.flat(nh, off=soff, np_=np_),
                                              rstd1.flat(1, off=slot, np_=np_), None, op0=ALU.mult), [ssbuf, rstd1], [ssbuf])
        P.op("dve", lambda e: e.tensor_tensor(out=tmpA.ap([[64, nh], [1, 64]], np_=np_), in0=pb.ap([[64, nh], [1, 64]], np_=np_),
                                              in1=w.ap([[0, nh], [1, 64]], off=coff, np_=np_), op=ALU.mult), [pb, w], [tmpA])
        for hf in range(2):
            P.op("dve", lambda e, hf=hf: e.tensor_tensor(out=tmpB.ap([[64, nh], [1, 32]], off=hf * 32, np_=np_),
                                                         in0=pb.ap([[64, nh], [1, 32]], off=(1 - hf) * 32, np_=np_),
                                                         in1=w.ap([[0, nh], [1, 32]], off=coff + 64 + hf * 32, np_=np_), op=ALU.mult),
                 [pb, w], [tmpB])
        P.op("dve", lambda e: e.tensor_tensor(out=tmpA.flat(n, np_=np_), in0=tmpA.flat(n, np_=np_), in1=tmpB.flat(n, np_=np_), op=ALU.add),
             [tmpA, tmpB], [tmpA])
        P.op("pool", lambda e: e.tensor_tensor(out=out_ap_fn(), in0=tmpA.ap([[64, nh], [1, 64]], np_=np_),
                                               in1=ssbuf.ap([[1, nh], [0, 64]], off=soff, np_=np_), op=ALU.mult), [tmpA, ssbuf], [outbuf])

    def kv_tile(pb, slot, w, ring, kd, last):
        qk_norm_rope(pb, 4, slot, w, 128, lambda: kf32.ap([[64, 4], [1, 64]], np_=np_), kf32, L["small3"], 0)
        P.op("pool", lambda e: e.tensor_copy(kd.ap([[128, 4], [64, 2], [1, 64]], np_=np_), kf32.ap([[64, 4], [0, 2], [1, 64]], np_=np_)), [kf32], [kd])
        P.op("act", lambda e: e.activation(out=vaug[ring].ap([[66, 4], [1, 64]], np_=np_), in_=pb.ap([[64, 4], [1, 64]], off=256, np_=np_),
                                           func=AF.Copy, scale=rstd1.flat(1, off=slot, np_=np_)), [pb, rstd1], [vaug[ring]])
        if last or samp:
            P.op("act", lambda e: e.activation(out=vf32.flat(256, np_=np_), in_=pb.flat(256, off=256, np_=np_), func=AF.Copy,
                                               scale=rstd1.flat(1, off=slot, np_=np_)), [pb, rstd1], [vf32])
        if last:
            P.dma("sp", T["kpd"].ap(), kf32.flat(256), [kf32], [L["d_kp"]])
            P.dma("sp", T["vpd"].ap(), vf32.flat(256), [vf32], [L["d_vp"]])
        if not samp:
            kt_todo.append((kd, ring))

    def flush_kt():
        for kd, ring in kt_todo:
            transposes(kd, 128, n=4, width=128, dst=kT[ring])
        del kt_todo[:]

    g0 = gtile[0]

    def stage1():
        s_kv = load_slab("in", 4096)
        if blk == 0:
            load_tab(16, "m1")
            pb = inproj(s_kv, L["hTm1"], 0, 128, kst=128)
            kv_tile(pb, 31, swtb["m1"], 0, kdup[0], False)
        flush_kt()
        for i in range(ntl):
            load_tab(17 if samp else slots[i], i)
            pb = inproj(s_kv, hb, i * 128, np_)
            kv_tile(pb, slots[i], swtb[i], (g0 + i + 1) % 5, kdup[i], (not samp) and slots[i] == NT - 1)
            yield
        for n in range(2):
            s = load_slab("in", 3072 + n * 512)
            for i in range(ntl):
                load_tab(17 if samp else slots[i], i)
                pb = inproj(s, hb, i * 128, np_)
                qk_norm_rope(pb, 8, slots[i], swtb[i], 0, lambda i=i, n=n: q16[i].ap([[64, 8], [1, 64]], off=n * 512, np_=np_), q16[i], small2, n * 8)
                yield
        for n in range(2):
            s = load_slab("in", 4608 + n * 512)
            for i in range(ntl):
                pb = inproj(s, hb, i * 128, np_)
                P.op("act", lambda e, pb=pb, i=i, n=n: e.activation(out=gate[i].flat(512, off=n * 512, np_=np_), in_=pb.flat(np_=np_), func=AF.Silu,
                                                                   scale=rstd1.flat(1, off=slots[i], np_=np_)), [pb, rstd1], [gate[i]])
                yield


    def mixers():
        flush_kt()
        oslabs[(1, 0)] = load_slab("o", 1, 0)
        oslabs[(1, 1)] = load_slab("o", 1, 1)
        L["op_banks"][0] = [1, 2] if samp else [3, 4]
        pov = [P.psum(5, F32), P.psum(6, F32), P.psum(7, F32)]

        def povslot(h):
            return pov[h // 6], (h % 6) * 65

        def scores(i, g):
            var = 0 if slots[i] == 0 else 1
            cur, prev = (g0 + i + 1) % 5, (g0 + i) % 5
            pTs = pT[g % 2]
            for par in range(2):
                bank = P.psum((3 + par) if (i == 0 or g % 2 == 0) else (1 + par), F32)
                for bi, kTb in enumerate((kT[prev], kT[cur])):
                    P.op("pe", lambda e, bank=bank, bi=bi, kTb=kTb, g=g, par=par: e.matmul(
                        bank.flat(256, off=bi * 256), kTb.ap([[1, 128]], off=g * 128, p0=64 * par, np_=64),
                        qT.ap([[128, 2], [1, 128]], off=2 * g * 128, p0=64 * par, np_=64), start=True, stop=False), [kTb, qT], [bank])
                    P.op("pe", lambda e, bank=bank, bi=bi, var=var: e.matmul(
                        bank.flat(256, off=bi * 256), ident.flat(), swamask.ap([[0, 2], [1, 128]], off=var * 256 + bi * 128),
                        start=False, stop=True), [ident, swamask], [bank])
                P.op("act", lambda e, bank=bank, par=par, pTs=pTs: e.activation(out=pTs[par].flat(), in_=bank.flat(), func=AF.Exp, scale=0.125),
                     [bank], [pTs[par]])

        def pv(i, g):
            cur, prev = (g0 + i + 1) % 5, (g0 + i) % 5
            pTs = pT[g % 2]
            for par in range(2):
                for jj in range(2):
                    h = 4 * g + 2 * jj + par
                    pb, off = povslot(h)
                    P.op("pe", lambda e, pb=pb, off=off, par=par, jj=jj, pTs=pTs, g=g, prev=prev: e.matmul(
                        pb.flat(65, off=off), pTs[par].flat(128, off=jj * 128), vaug[prev].flat(65, off=g * 66), start=True, stop=False),
                        [pTs[par], vaug[prev]], [pb])
                    P.op("pe", lambda e, pb=pb, off=off, par=par, jj=jj, pTs=pTs, g=g, cur=cur: e.matmul(
                        pb.flat(65, off=off), pTs[par].flat(128, off=(2 + jj) * 128), vaug[cur].flat(65, off=g * 66), start=False, stop=True),
                        [pTs[par], vaug[cur]], [pb])

        def front(i):
            transposes(q16[i], 128, n=8, width=128, dst=qT)
            scores(i, 0)

        def mix(i):
            if samp:
                swa_mixer_sample(P, L, T, d_in, newout, q16[i], kdup[i], vaug[(g0 + i + 1) % 5], pov, povslot)
            else:
                if i == 0:
                    front(0)
                    yield
                for g in range(4):
                    if g + 1 < 4:
                        scores(i, g + 1)
                    elif i + 1 < ntl:
                        front(i + 1)
                    yield
                    pv(i, g)
                    yield
            for bnk in range(3):
                nh = 6 if bnk < 2 else 4
                P.op("dve", lambda e, bnk=bnk, nh=nh: e.tensor_tensor(out=small2.flat(nh, off=16 + bnk * 6, np_=np_), in0=pov[bnk].ap([[65, nh]], off=64, np_=np_),
                                                                     in1=esrep.flat(nh, off=bnk * 6, np_=np_), op=ALU.add), [pov[bnk], esrep], [small2])
            P.op("dve", lambda e: e.reciprocal(out=small2.flat(16, off=16, np_=np_), in_=small2.flat(16, off=16, np_=np_)), [small2], [small2])
            P.op("pool", lambda e, i=i: e.tensor_tensor(out=gate2.ap([[64, 16], [1, 64]], np_=np_), in0=gate[i].ap([[64, 16], [1, 64]], np_=np_),
                                                        in1=small2.ap([[1, 16], [0, 64]], off=16, np_=np_), op=ALU.mult), [gate[i], small2], [gate2])
            for bnk in range(3):
                nh = 6 if bnk < 2 else 4
                P.op("dve", lambda e, bnk=bnk, nh=nh: e.tensor_tensor(out=tok.ap([[64, nh], [1, 64]], off=bnk * 384, np_=np_),
                                                                     in0=pov[bnk].ap([[65, nh], [1, 64]], np_=np_),
                                                                     in1=gate2.ap([[64, nh], [1, 64]], off=bnk * 384, np_=np_), op=ALU.mult),
                     [pov[bnk], gate2], [tok])
            if T["dbg"] and blk == 0:
                P.dma("sp", bass.AP(T["dbg3"], i * 128 * 1024, [[1024, 128], [1, 1024]]), tok.flat(), [tok], [newout()])
            yield
            transposes(tok, np_)
            yield

            def ev(n, pb, i=i):
                P.op("act", lambda e: e.activation(out=brs[i].flat(512, off=n * 512, np_=np_), in_=pb.flat(np_=np_), func=AF.Copy), [pb], [brs[i]])
            outproj(1, xT, np_, ev)
            yield
        for i in range(ntl):
            yield from mix(i)
        gtile[0] += ntl


    return stage1(), mixers()


def sample_cache_copy(P, T, d_in, newout):
    for src_t, dst_t in ((T["ckd"], T["ksd"]), (T["cvd"], T["vsd"])):
        for hb_ in range(2):
            d1 = newout()
            o = hb_ * 8 * 128 * 256
            P.dma("sp", bass.AP(dst_t, o, [[128 * 256, 8], [1, 127 * 256]]), bass.AP(src_t, o + 256, [[128 * 256, 8], [1, 127 * 256]]), [d_in], [d1])


def sample_prefetch(P, L, T, d_in):
    Kc, Vc = L["Kc"], L["Vc"]
    ckd, cvd = T["ckd"], T["cvd"]
    N = 16
    P.op("pool", lambda e: e.memset(Vc.flat(), 1.0), [], [Vc])
    thr = [P.dram(None) for _ in range(6)]
    j = 0
    for p0, npp in ((0, 64), (64, 63)):
        k = P.dram(None)
        L["kc_keys"].append(k)
        P.dma("pool", Kc.ap([[256, 16], [1, 256]], p0=p0, np_=npp), bass.AP(ckd, 256 * (1 + p0), [[256, npp], [128 * 256, 16], [1, 256]]),
              [d_in, Kc], [k, thr[j % 6]])
        j += 1
        for b in range(N):
            k = P.dram(None)
            L["vc_keys"].append(k)
            P.dma("pool", Vc.ap([[66, 4], [1, 64]], off=b * 264, p0=p0, np_=npp),
                  bass.AP(cvd, b * 128 * 256 + 256 * (1 + p0), [[256, npp], [64, 4], [1, 64]]), [d_in, Vc], [k, thr[j % 6]])
            j += 1


def swa_mixer_sample(P, L, T, d_in, newout, q16, kd, vaug_s, pov, povslot):
    ident, kf32, vf32, eye16, Kc, Vc, KTr, pTs, Pm, qT, transposes = (L[k] for k in
        ("ident", "kf32", "vf32", "eye16", "Kc", "Vc", "KTr", "pTs", "Pm", "qT", "transposes"))
    ckd, cvd = T["ckd"], T["cvd"]
    N = 16
    for dst_t, newrow in ((T["ksd"], kf32), (T["vsd"], vf32)):
        d2 = newout()
        P.dma("sp", bass.AP(dst_t, 127 * 256, [[128 * 256, 16], [1, 256]]), newrow.flat(256, np_=N), [newrow], [d2])
    P.dma("sp", Kc.ap([[256, 16], [64, 4], [1, 64]], p0=127, np_=1), kd.ap([[128, 4], [1, 64]], np_=N), [kd, Kc] + L["kc_keys"], [Kc])
    P.dma("sp", Vc.ap([[264, 16], [66, 4], [1, 64]], p0=127, np_=1), vaug_s.ap([[66, 4], [1, 64]], np_=N), [vaug_s, Vc] + L["vc_keys"], [Vc])
    transposes(q16, N, n=16, width=64, dst=qT)
    sc = P.psum(3, F32)
    for bp in range(N // 2):
        ktr = KTr[bp % 2]
        pt = P.psum(0, BF16)
        for j in range(8):
            b, g = 2 * bp + j // 4, j % 4
            P.op("pe", lambda e, j=j, b=b, g=g, pt=pt: e.transpose(pt.ap([[1, 128]], off=j * 128, np_=64), Kc.ap([[1, 64]], off=b * 256 + g * 64), ident.flat()),
                 [Kc, ident], [pt])
        P.op("act", lambda e, pt=pt, ktr=ktr: e.activation(out=ktr.flat(1024, np_=64), in_=pt.flat(1024, np_=64), func=AF.Copy), [pt], [ktr])
        for j in range(8):
            b, g = 2 * bp + j // 4, j % 4
            P.op("pe", lambda e, j=j, b=b, g=g, ktr=ktr: e.matmul(sc.ap([[1, 4]], off=b * 16 + g * 4), ktr.ap([[1, 128]], off=j * 128, np_=64),
                                                                 qT.ap([[16, 4]], off=4 * g * 16 + b, np_=64), start=True, stop=True), [ktr, qT], [sc])
    P.op("act", lambda e: e.activation(out=pTs.flat(256), in_=sc.flat(256), func=AF.Exp, scale=0.125), [sc], [pTs])
    for h in range(16):
        P.op("dve", lambda e, h=h: e.tensor_tensor(out=Pm.ap([[16, 16], [1, 16]], off=h * 256), in0=pTs.ap([[0, 16], [16, 16]], off=h),
                                                   in1=eye16.ap([[16, 16], [1, 16]]), op=ALU.mult), [pTs, eye16], [Pm])
    for h in range(16):
        g = h // 4
        pb, off = povslot(h)
        for b in range(N):
            P.op("pe", lambda e, h=h, g=g, b=b, pb=pb, off=off: e.matmul(pb.flat(65, off=off, np_=N), Pm.ap([[1, 16]], off=h * 256 + b * 16),
                                                                        Vc.ap([[1, 65]], off=b * 264 + g * 66), start=(b == 0), stop=(b == N - 1)),
                 [Pm, Vc], [pb])


def _tables(half):
    f64 = np.float64
    T = np.arange(2048, dtype=f64)
    lg = np.log(np.array(GAM, dtype=f64))
    inv_r = 10000.0 ** (-np.arange(0, 128, 2, dtype=f64) / 128)
    inv_s = 10000.0 ** (-np.arange(0, 64, 2, dtype=f64) / 64)

    def cs2(pos, inv):
        ang = pos[:, None] * inv[None, :]
        c, s_ = np.cos(ang), np.sin(ang)
        return np.concatenate([c, c], 1), np.concatenate([-s_, s_], 1)

    pos_main = half * 2048 + T
    c2, s2 = cs2(pos_main, inv_r)
    aq = np.exp((T[:, None] + 1) * lg[None, :])
    ak = np.exp(-(T[:, None] + 1) * lg[None, :]) / np.sqrt(128.0)
    rt = np.stack([c2[:, None, :] * aq[:, :, None], s2[:, None, :] * aq[:, :, None],
                   c2[:, None, :] * ak[:, :, None], s2[:, None, :] * ak[:, :, None]], 1)
    rtab_main = rt.reshape(16, 128, 2048).astype(np.float32)
    c2p, s2p = cs2(T, inv_r)
    akp = np.exp((2047 - T[:, None]) * lg[None, :]) / np.sqrt(128.0)
    rtp = np.stack([c2p[:, None, :] * akp[:, :, None], s2p[:, None, :] * akp[:, :, None]], 1)
    rtab_pre = rtp.reshape(16, 128, 1024).astype(np.float32)
    c2s, s2s = cs2(np.full(16, 8192.0), inv_r)
    one = np.ones((16, 4, 1))
    rts = np.stack([c2s[:, None, :] * one, s2s[:, None, :] * one, c2s[:, None, :] * one / np.sqrt(128.0),
                    s2s[:, None, :] * one / np.sqrt(128.0)], 1)
    rtab_samp = rts.reshape(16, 2048).astype(np.float32)
    stab = np.zeros((18, 128, 128), np.float32)
    cm, sm = cs2(pos_main, inv_s)
    stab[:16] = np.concatenate([cm, sm], 1).reshape(16, 128, 128)
    cp, sp_ = cs2(1920 + np.arange(128, dtype=f64), inv_s)
    stab[16] = np.concatenate([cp, sp_], 1)
    cs_, ss_ = cs2(np.full(128, 8192.0), inv_s)
    stab[17] = np.concatenate([cs_, ss_], 1)
    k = np.arange(128)[:, None]
    q = np.arange(128)[None, :]
    cmask = (q >= k).astype(np.float32)
    prev = np.where(k > q, 0.0, MASKNEG)
    cur = np.where(k <= q, 0.0, MASKNEG)
    first_prev = prev if half == 1 else np.full((128, 128), MASKNEG)
    swamask = np.stack([first_prev, cur, prev, cur], 1).reshape(128, 512)
    swamask = np.concatenate([swamask, np.zeros((128, 512))], 1).astype(np.float32)
    eye16 = np.tile(np.eye(16, dtype=np.float32).reshape(1, 256), (128, 1))
    return dict(rtab_main=rtab_main, rtab_pre=rtab_pre, rtab_samp=rtab_samp, stab=stab, cmask=cmask,
                swamask=swamask, eye16=eye16, ident=np.eye(128, dtype=np.float32))


_CACHE = {}


def kernel(x_prompt, x_sample, state_ret, cache_swa_k, cache_swa_v, norm_g, w_in, ret_norm_g,
           swa_q_g, swa_k_g, swa_sinks, w_br_ret, w_br_swa, w_out, _with_sample=True, _dbg=False):
    f = lambda a: np.ascontiguousarray(np.asarray(a, dtype=np.float32))
    x_prompt, x_sample, state_ret, cache_swa_k, cache_swa_v = map(f, (x_prompt, x_sample, state_ret, cache_swa_k, cache_swa_v))
    w_in_ = f(w_in)[0]
    w_o3 = np.ascontiguousarray(np.stack([f(w_br_ret)[0], f(w_br_swa)[0], f(w_out)[0]], 0))
    gq, gk = f(swa_q_g)[0], f(swa_k_g)[0]
    sw = lambda g: np.concatenate([g[32:], g[:32]])
    gqk = np.ascontiguousarray(np.stack([gq, sw(gq), gk, sw(gk)], 0))
    ng = np.ascontiguousarray(f(norm_g)[0].reshape(8, 128).T)
    if "nc" not in _CACHE:
        _CACHE["nc"] = build_program(with_sample=_with_sample, dbg=_dbg)[0]
        _CACHE["tabs"] = [_tables(0), _tables(1)]
    nc = _CACHE["nc"]
    in_maps = []
    for c in range(8):
        b, half = c // 2, c % 2
        m = dict(_CACHE["tabs"][half])
        m["xm"] = np.ascontiguousarray(x_prompt[b, half * 2048:(half + 1) * 2048])
        m["xp"] = np.ascontiguousarray(x_prompt[b, 0:2048]) if half == 1 else np.zeros((2048, D), np.float32)
        m["xs"] = np.ascontiguousarray(x_sample[16 * c:16 * c + 16, 0])
        m["w_in"] = w_in_
        m["w_o3"] = w_o3
        m["norm_g"] = ng
        m["ret_g"] = np.ascontiguousarray(f(ret_norm_g)[0].reshape(8, 128).T)
        m["gqk"] = gqk
        m["sinks"] = np.ascontiguousarray(f(swa_sinks)[0])
        m["state"] = np.ascontiguousarray(state_ret[0, 16 * c:16 * c + 16])
        m["ck"] = np.ascontiguousarray(cache_swa_k[0, 16 * c:16 * c + 16].reshape(16, 128, 256))
        m["cv"] = np.ascontiguousarray(cache_swa_v[0, 16 * c:16 * c + 16].reshape(16, 128, 256))
        in_maps.append(m)
    res = run_bass_kernel_spmd(nc, in_maps, core_ids=list(range(8)))
    R = res.results
    if _dbg:
        _CACHE["dbg"] = {k: np.asarray(R[0][k]).astype(np.float32) for k in ("dbg1", "dbg2", "dbg3")}
    yp = np.zeros((4, 4096, D), np.float32)
    ys = np.zeros((128, 1, D), np.float32)
    rsp = np.zeros((1, 4, 4, 128, 256), np.float32)
    rss = np.zeros((1, 128, 4, 128, 256), np.float32)
    kp = np.zeros((1, 4, 128, 4, 64), np.float32)
    vp = np.zeros((1, 4, 128, 4, 64), np.float32)
    ks = np.zeros((1, 128, 128, 4, 64), np.float32)
    vs = np.zeros((1, 128, 128, 4, 64), np.float32)
    for c in range(8):
        b, half = c // 2, c % 2
        r = R[c]
        yp[b, half * 2048:(half + 1) * 2048] = r["yp"]
        ys[16 * c:16 * c + 16, 0] = r["ys"]
        rss[0, 16 * c:16 * c + 16] = r["rss"]
        ks[0, 16 * c:16 * c + 16] = r["ks"].reshape(16, 128, 4, 64)
        vs[0, 16 * c:16 * c + 16] = r["vs"].reshape(16, 128, 4, 64)
        if half == 1:
            rsp[0, b] = r["rsp"]
            kp[0, b] = r["kp"].reshape(128, 4, 64)
            vp[0, b] = r["vp"].reshape(128, 4, 64)
    return yp, ys, rsp, rss, kp, vp, ks, vs
```

```python
import numpy as np
import concourse.bass as bass
import concourse.mybir as mybir
from concourse.bass_utils import run_bass_kernel_spmd

F32 = mybir.dt.float32
BF16 = mybir.dt.bfloat16
AF = mybir.ActivationFunctionType
ALU = mybir.AluOpType
AX = mybir.AxisListType

GRAN = 512


class Buf:
    def __init__(self, tensor, off, n, rowlen, keys, esz):
        self.tensor = tensor
        self.off = off
        self.n = n
        self.rowlen = rowlen
        self.keys = keys
        self.esz = esz

    def ap(self, dims, off=0, p0=0, np_=128):
        return bass.AP(self.tensor, p0 * self.rowlen + self.off + off,
                       [[self.rowlen, np_]] + [list(d) for d in dims])

    def flat(self, n=None, off=0, p0=0, np_=128):
        return self.ap([[1, self.n - off if n is None else n]], off=off, p0=p0, np_=np_)


class Prog:
    def __init__(self, nc, arena_bytes, n_dma_sems=52):
        self.nc = nc
        self.ops = []
        self.last_w = {}
        self.readers = {}
        self.arena_bytes = arena_bytes
        self.arena_top = 0
        self.n_dma_sems = n_dma_sems
        self.dram_key = 0
        self.t16 = None
        self.ps = []

    def setup_mem(self, t16, ps_tensors):
        self.t16 = t16
        self.t32 = t16.bitcast(F32)
        self.ps = [(p, p.bitcast(BF16)) for p in ps_tensors]

    def alloc_off(self, nbytes):
        off = self.arena_top
        self.arena_top = (off + nbytes + GRAN - 1) // GRAN * GRAN
        assert self.arena_top <= self.arena_bytes, (self.arena_top, self.arena_bytes)
        return off

    def sb_at(self, boff, n, dtype):
        esz = 4 if dtype == F32 else 2
        t = self.t32 if dtype == F32 else self.t16
        assert boff % esz == 0
        keys = frozenset(("sb", g) for g in range(boff // GRAN, (boff + n * esz - 1) // GRAN + 1))
        return Buf(t, boff // esz, n, self.arena_bytes // esz, keys, esz)

    def sb(self, n, dtype):
        esz = 4 if dtype == F32 else 2
        return self.sb_at(self.alloc_off(n * esz), n, dtype)

    def psum(self, bank, dtype=F32, off=0, n=None):
        t = self.ps[bank][0 if dtype == F32 else 1]
        full = 512 if dtype == F32 else 1024
        if n is None:
            n = full - off
        return Buf(t, off, n, full, frozenset([("ps", bank)]), 4 if dtype == F32 else 2)

    def dram(self, tensor, esz=4):
        self.dram_key += 1
        return Buf(tensor, 0, 0, 0, frozenset([("dr", self.dram_key)]), esz)

    def op(self, eng, fn, reads=(), writes=(), dma=False):
        idx = len(self.ops)
        deps = set()
        rk = set()
        for b in reads:
            rk |= b.keys
        wk = set()
        for b in writes:
            wk |= b.keys
        for k in rk:
            w = self.last_w.get(k)
            if w is not None:
                deps.add(w)
        for k in wk:
            w = self.last_w.get(k)
            if w is not None:
                deps.add(w)
            for r in self.readers.get(k, ()):
                deps.add(r)
        for k in rk:
            lst = self.readers.setdefault(k, [])
            if not dma:
                lst[:] = [r for r in lst if self.ops[r]["dma"] or self.ops[r]["eng"] != eng]
            lst.append(idx)
        for k in wk:
            self.readers[k] = []
            self.last_w[k] = idx
        deps.discard(idx)
        self.ops.append(dict(eng=eng, fn=fn, deps=deps, dma=dma))
        return idx

    def dma(self, queue, out_ap, in_ap, reads, writes, **kw):
        def fn(e):
            return e.dma_start(out=out_ap, in_=in_ap, **kw)
        return self.op(queue, fn, reads, writes, dma=True)

    def emit(self):
        nc = self.nc
        ops = self.ops
        engs = ["pe", "act", "dve", "pool", "sp"]
        eng_obj = dict(pe=nc.tensor, act=nc.scalar, dve=nc.vector, pool=nc.gpsimd, sp=nc.sync)
        sig = [False] * len(ops)
        for i, o in enumerate(ops):
            nd = set()
            for d in o["deps"]:
                od = ops[d]
                if (not od["dma"]) and od["eng"] == "pe" and o["eng"] == "pe" and not o["dma"]:
                    continue
                nd.add(d)
                sig[d] = True
            o["deps"] = nd
        cnt = {e: 0 for e in engs}
        dma_n = 0
        n_sw = 12
        n_hw = self.n_dma_sems - n_sw
        qn = {"pool": 0, "hw": 0}
        for i, o in enumerate(ops):
            if o["dma"]:
                if o["eng"] == "pool":
                    o["dsem"] = qn["pool"] % n_sw
                    o["duse"] = qn["pool"] // n_sw + 1
                    qn["pool"] += 1
                else:
                    o["dsem"] = n_sw + qn["hw"] % n_hw
                    o["duse"] = qn["hw"] // n_hw + 1
                    qn["hw"] += 1
                dma_n += 1
            elif sig[i]:
                cnt[o["eng"]] += 1
                o["sigval"] = cnt[o["eng"]]
        self.stats = dict(cnt=dict(cnt), dma=dma_n, nops=len(ops))

        import contextlib
        with contextlib.ExitStack() as st:
            esem = {e: st.enter_context(nc.semaphore("s_" + e)) for e in engs}
            dsem = [st.enter_context(nc.semaphore("d%d" % i)) for i in range(self.n_dma_sems)]
            block = st.enter_context(nc.Block())
            per_eng = {e: [i for i, o in enumerate(ops) if o["eng"] == e] for e in engs}

            def run(e, eng):
                seen = {}
                for i in per_eng[e]:
                    o = ops[i]
                    need = {}
                    for d in o["deps"]:
                        od = ops[d]
                        if od["dma"]:
                            key = ("d", od["dsem"])
                            val = 16 * od["duse"]
                        else:
                            key = ("e", od["eng"])
                            val = od["sigval"]
                        if need.get(key, 0) < val:
                            need[key] = val
                    if o["dma"] and o["duse"] > 1:
                        key = ("d", o["dsem"])
                        val = 16 * (o["duse"] - 1)
                        if need.get(key, 0) < val:
                            need[key] = val
                    for key, val in need.items():
                        if seen.get(key, 0) >= val:
                            continue
                        seen[key] = val
                        s = dsem[key[1]] if key[0] == "d" else esem[key[1]]
                        eng.wait_ge(s, val)
                    if o["fn"] is None:
                        continue
                    ins = o["fn"](eng)
                    if o["dma"]:
                        ins.then_inc(dsem[o["dsem"]], 16)
                    elif sig[i]:
                        ins.then_inc(esem[e], 1)

            @block.tensor
            def _(eng):
                run("pe", eng)

            @block.scalar
            def _(eng):
                run("act", eng)

            @block.vector
            def _(eng):
                run("dve", eng)

            @block.gpsimd
            def _(eng):
                run("pool", eng)

            @block.sync
            def _(eng):
                run("sp", eng)


import contextlib
import ml_dtypes

D = 1024
NT = 16
BLK = 4
EPS = 1e-6
GAM = [1.0 - 2.0 ** (-5.0 - h) for h in range(4)]
COLS = dict(rq=0, rk=512, rv=1024, rg=2048, sq=3072, skv=4096, sg=4608, mr=5632, ms=6656)
MASKNEG = -30000.0


def build_program(with_sample=True, dbg=False):
    nc = bass.Bass("TRN2", target_bir_lowering=False)

    def din(name, shape, dt=F32):
        return nc.dram_tensor(name, shape, dt, kind="ExternalInput")

    def dout(name, shape, dt=F32):
        return nc.dram_tensor(name, shape, dt, kind="ExternalOutput")

    xm = din("xm", [2048, D]); xp = din("xp", [2048, D]); xsd = din("xs", [16, D])
    w_in = din("w_in", [D, 7680]); w_o3 = din("w_o3", [3, D, D])
    norm_g = din("norm_g", [128, 8]); ret_g = din("ret_g", [128, 8])
    gqk = din("gqk", [4, 64]); sinks = din("sinks", [16])
    state = din("state", [16, 4, 128, 256]); ckd = din("ck", [16, 128, 256]); cvd = din("cv", [16, 128, 256])
    rtab_main = din("rtab_main", [16, 128, 2048]); rtab_pre = din("rtab_pre", [16, 128, 1024])
    rtab_samp = din("rtab_samp", [16, 2048]); stab = din("stab", [18, 128, 128])
    identd = din("ident", [128, 128]); cmaskd = din("cmask", [128, 128]); swamaskd = din("swamask", [128, 1024])
    eyed = din("eye16", [128, 256])

    yp = dout("yp", [2048, D]); ysd = dout("ys", [16, D]); rsp = dout("rsp", [4, 128, 256])
    rss = dout("rss", [16, 4, 128, 256]); kpd = dout("kp", [128, 256]); vpd = dout("vp", [128, 256])
    ksd = dout("ks", [16, 128, 256]); vsd = dout("vs", [16, 128, 256])

    dbg1 = dout("dbg1", [4, 128, 1024], BF16) if dbg else None
    dbg2 = dout("dbg2", [4, 128, 1024], BF16) if dbg else None
    dbg3 = dout("dbg3", [4, 128, 1024], BF16) if dbg else None
    w_in_bf = nc.dram_tensor("w_in_bf", [15, 128, 4096], BF16, kind="Internal")
    w_o3_bf = nc.dram_tensor("w_o3_bf", [6, 128, 4096], BF16, kind="Internal")

    ARENA = 204 * 1024
    with contextlib.ExitStack() as st:
        t16 = st.enter_context(nc.sbuf_tensor("arena", [128, ARENA // 2], BF16))
        pst = [st.enter_context(nc.psum_tensor("ps%d" % i, [128, 512], F32)) for i in range(8)]
        P = Prog(nc, ARENA)
        P.setup_mem(t16, pst)
        _build(nc, P, locals(), with_sample)
        P.emit()
    return nc, P


def _build(nc, P, T, with_sample):
    xm, xp, xsd, w_in, w_o3 = T["xm"], T["xp"], T["xsd"], T["w_in"], T["w_o3"]
    w_in_bf, w_o3_bf = T["w_in_bf"], T["w_o3_bf"]
    d_in = P.dram(None)
    d_wslab = {}
    for c0 in range(0, 7680, 512):
        d_wslab[("in", c0)] = P.dram(None)
    for m in range(3):
        for n in range(2):
            d_wslab[("o", m, n)] = P.dram(None)
    d_outs = []

    def newout():
        b = P.dram(None)
        d_outs.append(b)
        return b

    ident = P.sb(128, BF16); cmask = P.sb(128, F32); swamask = P.sb(1024, BF16)
    gcol = P.sb(8, F32); gretcol = P.sb(8, F32); gq4 = P.sb(256, F32); esrep = P.sb(16, F32)
    rstd1 = P.sb(40, F32); r2 = P.sb(40, F32); mhalf = P.sb(16, F32); rh = P.sb(40, F32)
    P.dma("pool", ident.flat(), T["identd"].ap(), [d_in], [ident])
    P.dma("pool", swamask.flat(), T["swamaskd"].ap(), [d_in], [swamask])
    P.dma("sp", cmask.flat(), T["cmaskd"].ap(), [d_in], [cmask])
    P.dma("sp", gcol.flat(8), T["norm_g"].ap(), [d_in], [gcol])
    P.dma("sp", gretcol.flat(8), T["ret_g"].ap(), [d_in], [gretcol])
    P.dma("sp", gq4.flat(), bass.AP(T["gqk"], 0, [[0, 128], [1, 256]]), [d_in], [gq4])
    P.dma("sp", esrep.flat(), bass.AP(T["sinks"], 0, [[0, 128], [1, 16]]), [d_in], [esrep])
    P.op("act", lambda e: e.activation(out=esrep.flat(), in_=esrep.flat(), func=AF.Exp), [esrep], [esrep])
    P.op("dve", lambda e: e.memset(mhalf.flat(), -0.5), [], [mhalf])

    NSLAB = 4
    slabs = [P.sb(8 * 512, BF16) for _ in range(NSLAB)]
    slab_i = [0]
    pre_slabs = []
    for j, c0 in enumerate((512, 1024, 1536)):
        sl = slabs[j]
        for kh in range(2):
            half = P.sb_at(sl.off * 2 + kh * 4096, 2048, BF16)
            P.dma("pool", half.ap([[512, 4], [1, 512]]), bass.AP(w_in, kh * 512 * 7680 + c0, [[7680, 128], [128 * 7680, 4], [1, 512]]),
                  [d_in], [half])
        pre_slabs.append(sl)
    slab_i[0] = 0

    thr = [P.dram(None) for _ in range(3)]
    thr_i = [0]

    def conv_in(c0):
        tk = thr[thr_i[0] % 3]
        thr_i[0] += 1
        P.dma("pool", bass.AP(w_in_bf, (c0 // 512) * 128 * 4096, [[4096, 128], [512, 8], [1, 512]]),
              bass.AP(w_in, c0, [[7680, 128], [128 * 7680, 8], [1, 512]]), [d_in], [d_wslab[("in", c0)], tk])

    def conv_o(m, n):
        tk = thr[thr_i[0] % 3]
        thr_i[0] += 1
        P.dma("pool", bass.AP(w_o3_bf, (2 * m + n) * 128 * 4096, [[4096, 128], [512, 8], [1, 512]]),
              bass.AP(w_o3, m * D * D + n * 512, [[D, 128], [128 * D, 8], [1, 512]]), [d_in], [d_wslab[("o", m, n)], tk])

    conv_q = []
    for c0 in (0, 512, 1024, 1536, 2048, 2560):
        conv_q.append(lambda c0=c0: conv_in(c0))
    conv_q.append(lambda: conv_o(0, 0)); conv_q.append(lambda: conv_o(0, 1))
    for c0 in (4096, 3072, 3584, 4608, 5120):
        conv_q.append(lambda c0=c0: conv_in(c0))
    conv_q.append(lambda: conv_o(1, 0)); conv_q.append(lambda: conv_o(1, 1))
    for c0 in range(5632, 7680, 512):
        conv_q.append(lambda c0=c0: conv_in(c0))
    conv_q.append(lambda: conv_o(2, 0)); conv_q.append(lambda: conv_o(2, 1))

    def emit_convs(n):
        for _ in range(n):
            if conv_q:
                conv_q.pop(0)()

    xbuf = [P.sb(1024, F32) for _ in range(2)]
    xb16 = P.sb(1024, BF16); junk = P.sb(1024, BF16)
    hT = [P.sb(8 * 512, BF16) for _ in range(2)]
    hTm1 = P.sb(8 * 128, BF16)
    arena0 = P.alloc_off(BLK * 6144)
    arena1 = P.alloc_off(BLK * 6144)
    ybuf = [P.sb(1024, F32) for _ in range(2)]
    brr = [P.sb(1024, BF16)]
    brs = [P.sb(1024, BF16)]
    br_rest0 = P.arena_top
    brr += [P.sb(1024, BF16) for _ in range(BLK - 1)]
    brs += [P.sb(1024, BF16) for _ in range(BLK - 1)]
    eye16 = P.sb(256, F32)
    pTs = P.sb(256, BF16)
    Pm = P.sb_at(br_rest0, 4096, BF16)
    Sbf2 = [P.sb_at(br_rest0 + 8192 + j * 2048, 1024, BF16) for j in range(2)]
    Kc = P.sb_at(arena0 + 6144, 4096, BF16)
    Vc = P.sb_at(arena0 + 6144 + 8192, 16 * 264, BF16)
    rtab = [P.sb(1024, F32) for _ in range(2)]
    rtab_i = [0]
    stabs = [P.sb(128, F32) for _ in range(2)]
    swt = [P.sb(256, F32) for _ in range(2)]
    tmpXr = [P.sb(512, F32) for _ in range(2)]; tmpAr = [P.sb(512, F32) for _ in range(2)]; tmpBr = [P.sb(512, F32) for _ in range(2)]
    tmpX, tmpA, tmpB = tmpXr[0], tmpAr[0], tmpBr[0]
    tmp_i = [0]
    xT = P.sb(1024, BF16)
    tok = P.sb(1024, BF16)
    Sst = P.sb(1024, F32); Sbf = P.sb(1024, BF16)
    scT = P.sb(512, BF16); small = P.sb(64, F32); small2 = P.sb(64, F32); small3 = P.sb(64, F32)
    qkT2 = [P.sb(1024, BF16) for _ in range(2)]
    scT2 = [scT, P.sb(512, BF16)]
    qT = qkT2[0]
    Qm = qkT2[1]
    Kb = scT2
    kT = [P.sb(512, BF16) for _ in range(5)]
    Sf32 = [P.sb_at(tmpXr[0].off * 4 + j * 4096, 1024, F32) for j in range(2)]
    Sf32 += [P.sb_at(hT[1].off * 2 + j * 4096, 1024, F32) for j in range(2)]
    Sbf2 += [P.sb_at(kT[0].off * 2 + j * 2048, 1024, BF16) for j in range(2)]
    vaug = [P.sb(4 * 66, BF16) for _ in range(5)]
    pT = [[P.sb(512, BF16) for _ in range(2)] for _ in range(2)]
    KTr = [P.sb_at(pT[j][0].off * 2, 1024, BF16) for j in range(2)]
    gate2 = P.sb(1024, BF16)
    kf32 = P.sb(256, F32); vf32 = P.sb(256, F32)
    sg16 = [P.sb(512, BF16) for _ in range(2)]
    for v in vaug:
        P.op("dve", lambda e, v=v: e.memset(v.flat(), 1.0), [], [v])

    def arena_of(which):
        base = arena0 if which == 0 else arena1

        def arena(i, boff, n, dt):
            return P.sb_at(base + i * 6144 + boff, n, dt)
        return arena

    arena = arena_of(0)

    B_TP = 0
    mm_banks = [1, 2]
    mm_i = [0]

    wide_banks = [1, 2]
    wide_i = [0]

    def next_mm(wide=False):
        if wide:
            b = wide_banks[wide_i[0] % len(wide_banks)]
            wide_i[0] += 1
            return b
        b = mm_banks[mm_i[0] % len(mm_banks)]
        mm_i[0] += 1
        return b

    oslab_i = [0]
    no_prefetch = [False]

    BLOCK_SEQ = [0, 512, 1024, 1536, 2048, 2560, 4096, 3072, 3584, 4608, 5120, 5632, 6144, 6656, 7168]
    slab_seq = []
    slab_pos = [0]
    slab_pending = {}

    def issue_in_slab(pos):
        c0 = slab_seq[pos]
        s = slabs[pos % 2]
        src = bass.AP(w_in_bf, (c0 // 512) * 128 * 4096, [[4096, 128], [1, 4096]])
        P.dma("sp", s.flat(4096), src, [d_wslab[("in", c0)]], [s])
        slab_pending[pos] = s

    def load_slab(kind, *a):
        if kind == "in":
            pos = slab_pos[0]
            assert slab_seq[pos] == a[0], (pos, slab_seq[pos], a[0])
            if pos not in slab_pending:
                issue_in_slab(pos)
            s = slab_pending.pop(pos)
            slab_pos[0] += 1
            if pos + 1 < len(slab_seq) and not (no_prefetch[0]):
                issue_in_slab(pos + 1)
            return s
        else:
            m, n = a
            s = slabs[2 + n]
            src = bass.AP(w_o3_bf, (2 * m + n) * 128 * 4096, [[4096, 128], [1, 4096]])
            key = d_wslab[("o", m, n)]
        P.dma("sp", s.flat(4096), src, [key], [s])
        return s

    def rstd_pow(col_buf, col, np_):
        P.op("pool", lambda e: e.tensor_tensor(out=col_buf.flat(1, off=col, np_=np_), in0=col_buf.flat(1, off=col, np_=np_),
                                               in1=mhalf.flat(1, np_=np_), op=ALU.pow), [col_buf, mhalf], [col_buf])

    def xload(src_t, row0, np_, xslot):
        xb_ = xbuf[xslot]
        P.dma("sp", xb_.flat(np_=np_), bass.AP(src_t, row0 * D, [[D, np_], [1, D]]), [d_in], [xb_])

    def stage0(src_t, row0, np_, slot, hbuf, hcol, xslot, preloaded=False):
        xb_ = xbuf[xslot]
        if not preloaded:
            P.dma("sp", xb_.flat(np_=np_), bass.AP(src_t, row0 * D, [[D, np_], [1, D]]), [d_in], [xb_])
        P.op("act", lambda e: e.activation(out=junk.flat(np_=np_), in_=xb_.flat(np_=np_), func=AF.Square,
                                           accum_out=rstd1.flat(1, off=slot, np_=np_)), [xb_], [junk, rstd1])
        P.op("dve", lambda e: e.tensor_scalar(rstd1.flat(1, off=slot, np_=np_), rstd1.flat(1, off=slot, np_=np_), 1.0 / D, EPS,
                                              op0=ALU.mult, op1=ALU.add), [rstd1], [rstd1])
        rstd_pow(rstd1, slot, np_)
        P.op("dve", lambda e: e.scalar_tensor_tensor(out=r2.flat(1, off=slot, np_=np_), in0=rstd1.flat(1, off=slot, np_=np_),
                                                     scalar=1.0 / 64, in1=rstd1.flat(1, off=slot, np_=np_),
                                                     op0=ALU.mult, op1=ALU.mult), [rstd1], [r2])
        P.op("dve", lambda e: e.tensor_scalar(rh.flat(1, off=slot, np_=np_), rstd1.flat(1, off=slot, np_=np_), 0.5, None, op0=ALU.mult), [rstd1], [rh])
        P.op("dve", lambda e: e.tensor_copy(xb16.flat(np_=np_), xb_.flat(np_=np_)), [xb_], [xb16])
        pt = P.psum(B_TP, BF16)
        for j in range(8):
            P.op("pe", lambda e, j=j: e.transpose(pt.flat(np_, off=j * np_), xb16.flat(128, off=j * 128, np_=np_),
                                                  ident.ap([[1, np_]], np_=np_)), [xb16, ident], [pt])
        P.op("dve", lambda e: e.tensor_tensor(out=hbuf.ap([[512, 8], [1, np_]], off=hcol), in0=pt.ap([[np_, 8], [1, np_]]),
                                              in1=gcol.ap([[1, 8], [0, np_]]), op=ALU.mult), [pt, gcol], [hbuf])

    inproj_wide = [True]

    def inproj(slab, hbuf, hcol, np_, kst=512):
        pb = P.psum(next_mm(wide=inproj_wide[0]), F32)
        for k in range(8):
            P.op("pe", lambda e, k=k: e.matmul(pb.flat(512, np_=np_), hbuf.ap([[1, np_]], off=k * kst + hcol),
                                               slab.flat(512, off=k * 512), start=(k == 0), stop=(k == 7)),
                 [hbuf, slab], [pb])
        return pb

    def ret_rope(pb, slot, tabC, tabS, out_bf, np_):
        A = tmpAr[tmp_i[0] % 2]; B = tmpBr[tmp_i[0] % 2]
        tmp_i[0] += 1
        tb_ = tabC_buf[0]
        P.op("dve", lambda e: e.scalar_tensor_tensor(out=A.flat(np_=np_), in0=pb.flat(np_=np_), scalar=rstd1.flat(1, off=slot, np_=np_),
                                                     in1=tabC, op0=ALU.mult, op1=ALU.mult), [pb, rstd1, tb_], [A])
        for hf in range(2):
            P.op("dve", lambda e, hf=hf: e.scalar_tensor_tensor(out=B.ap([[128, 4], [1, 64]], off=hf * 64, np_=np_),
                                                                in0=pb.ap([[128, 4], [1, 64]], off=(1 - hf) * 64, np_=np_),
                                                                scalar=rstd1.flat(1, off=slot, np_=np_), in1=tabS(hf),
                                                                op0=ALU.mult, op1=ALU.mult), [pb, rstd1, tb_], [B])
        P.op("dve", lambda e: e.tensor_tensor(out=out_bf.flat(512, np_=np_), in0=A.flat(np_=np_), in1=B.flat(np_=np_),
                                              op=ALU.add), [A, B], [out_bf])

    tabC_buf = [None]

    def transposes(src_bf, np_, n=8, width=128, dst=None, eng="act", colscale=None):
        if dst is None:
            dst = xT
        pt = P.psum(B_TP, BF16)
        for j in range(n):
            P.op("pe", lambda e, j=j: e.transpose(pt.ap([[1, np_]], off=j * np_, np_=width),
                                                  src_bf.flat(width, off=j * width, np_=np_),
                                                  ident.ap([[1, np_]], np_=np_)), [src_bf, ident], [pt])
        if colscale is not None:
            P.op("dve", lambda e: e.tensor_tensor(out=dst.ap([[np_, n], [1, np_]], np_=width), in0=pt.ap([[np_, n], [1, np_]], np_=width),
                                                  in1=colscale.ap([[1, n], [0, np_]], np_=width), op=ALU.mult), [pt, colscale], [dst])
        elif eng == "act":
            P.op("act", lambda e: e.activation(out=dst.flat(n * np_, np_=width), in_=pt.flat(n * np_, np_=width), func=AF.Copy),
                 [pt], [dst])
        else:
            P.op("dve", lambda e: e.tensor_copy(dst.flat(n * np_, np_=width), pt.flat(n * np_, np_=width)), [pt], [dst])
        return dst

    op_banks = [[1, 2]]
    op_i = [0]

    def outproj(m, src_T, np_, evac):
        for n in range(2):
            s = oslabs[(m, n)]
            pb = P.psum(op_banks[0][op_i[0] % len(op_banks[0])], F32)
            op_i[0] += 1
            for k in range(8):
                P.op("pe", lambda e, k=k, s=s, pb=pb: e.matmul(pb.flat(512, np_=np_), src_T.ap([[1, np_]], off=k * np_),
                                                               s.flat(512, off=k * 512), start=(k == 0), stop=(k == 7)),
                     [src_T, s], [pb])
            evac(n, pb)

    oslabs = {}

    pstb = [P.psum(4 + h, F32) for h in range(4)]
    def pre_block(blk):
        hb = hT[blk % 2]
        xload(xp, blk * BLK * 128, 128, (blk * BLK) % 2)
        for i in range(BLK):
            t = blk * BLK + i
            if i + 1 < BLK:
                xload(xp, (t + 1) * 128, 128, (t + 1) % 2)
            stage0(xp, t * 128, 128, 16 + t, hb, i * 128, t % 2, preloaded=True)
        if blk == NT // BLK - 1:
            P.op("pool", lambda e: e.tensor_copy(hTm1.ap([[128, 8], [1, 128]]), hb.ap([[512, 8], [1, 128]], off=3 * 128)), [hb], [hTm1])
        kr = [arena(i, 1024, 512, BF16) for i in range(BLK)]
        vv = [arena(i, 2048, 1024, BF16) for i in range(BLK)]
        s_rk = pre_slabs[0]
        for i in range(BLK):
            t = blk * BLK + i
            tb = rtab[rtab_i[0] % 2]
            rtab_i[0] += 1
            P.dma("sp", tb.flat(1024), bass.AP(T["rtab_pre"], t * 128 * 1024, [[1024, 128], [1, 1024]]), [d_in], [tb])
            pb = inproj(s_rk, hb, i * 128, 128)
            tabC_buf[0] = tb
            ret_rope(pb, 16 + t, tb.flat(512), lambda hf, tb=tb: tb.ap([[128, 4], [1, 64]], off=512 + hf * 64), kr[i], 128)
            if t < 8:
                emit_convs(1)
        for n in range(2):
            s_rv = pre_slabs[1 + n]
            for i in range(BLK):
                t = blk * BLK + i
                pb = inproj(s_rv, hb, i * 128, 128)
                P.op("act", lambda e, pb=pb, i=i, n=n, t=t: e.activation(out=vv[i].flat(512, off=n * 512), in_=pb.flat(), func=AF.Copy,
                                                                      scale=rstd1.flat(1, off=16 + t)), [pb, rstd1], [vv[i]])
        for i in range(BLK):
            t = blk * BLK + i
            for h in range(4):
                P.op("pe", lambda e, h=h, i=i, t=t: e.matmul(pstb[h].flat(256), kr[i].flat(128, off=h * 128),
                                                          vv[i].flat(256, off=h * 256), start=(t == 0), stop=(t == NT - 1)),
                     [kr[i], vv[i]], [pstb[h]])

    inproj_wide[0] = False
    mm_banks[:] = [1, 2, 3]
    for blk in range(NT // BLK):
        pre_block(blk)
    inproj_wide[0] = True
    mm_banks[:] = [1, 2]
    for j in range(4):
        P.op("dve", lambda e, j=j: e.tensor_copy(Sst.flat(256, off=j * 256), pstb[j].flat(256)), [pstb[j]], [Sst])
    P.op("act", lambda e: e.activation(out=Sbf.flat(), in_=Sst.flat(), func=AF.Copy), [Sst], [Sbf])

    nblocks = NT // BLK + (1 if with_sample else 0)
    slab_seq.extend(BLOCK_SEQ * nblocks)
    d_yp = newout(); d_rsp = newout(); d_kp = newout(); d_vp = newout()
    gtile = [0]
    kc_keys = []
    vc_keys = []
    swt_i = [0]

    def do_stage0_block(blk):
        hb = hT[blk % 2]
        if blk < NT // BLK:
            for i in range(BLK):
                t = blk * BLK + i
                stage0(xm, t * 128, 128, t, hb, i * 128, t % 2)
        else:
            stage0(xsd, 0, 16, 32, hb, 0, 0)

    G = dict(locals())

    def blkinfo(blk):
        samp = blk >= NT // BLK
        return dict(samp=samp, np_=16 if samp else 128, ntl=1 if samp else BLK, hb=hT[blk % 2],
                    slots=[32] if samp else [blk * BLK + i for i in range(BLK)], blk=blk)

    def gen_S0(blk):
        hb = hT[blk % 2]
        if blk < NT // BLK:
            xload(xm, blk * BLK * 128, 128, (blk * BLK) % 2)
            for i in range(BLK):
                t = blk * BLK + i
                if i + 1 < BLK:
                    xload(xm, (t + 1) * 128, 128, (t + 1) % 2)
                stage0(xm, t * 128, 128, t, hb, i * 128, t % 2, preloaded=True)
                yield
        else:
            stage0(xsd, 0, 16, 32, hb, 0, 0)
            yield

    def passA_bufs(blk):
        ar = arena_of((3 * blk) % 2)
        n = 1 if blk >= NT // BLK else BLK
        return dict(qr=[ar(i, 0, 512, BF16) for i in range(n)], kr=[ar(i, 1024, 512, BF16) for i in range(n)],
                    vv=[ar(i, 2048, 1024, BF16) for i in range(n)], GG=[ar(i, 4096, 1024, BF16) for i in range(n)])

    def gen_A1(blk):
        I = blkinfo(blk)
        samp, np_, ntl, hb, slots = I["samp"], I["np_"], I["ntl"], I["hb"], I["slots"]
        Bf = passA_bufs(blk)
        qr, kr, vv, GG = Bf["qr"], Bf["kr"], Bf["vv"], Bf["GG"]
        wide_banks[:] = [1, 2, 3, 4]
        if samp:
            sample_prefetch(P, {**G, **I}, T, d_in)
        if with_sample and blk == 1:
            sample_cache_copy(P, T, d_in, newout)
        tspecs = [(o0, i) for o0 in (0, 1024) for i in range(ntl)]
        tbufs = {}

        def tload(j):
            o0, i = tspecs[j]
            tb = rtab[rtab_i[0] % 2]
            rtab_i[0] += 1
            if samp:
                P.dma("sp", tb.flat(1024, np_=16), bass.AP(T["rtab_samp"], o0, [[2048, 16], [1, 1024]]), [d_in], [tb])
            else:
                P.dma("sp", tb.flat(1024), bass.AP(T["rtab_main"], slots[i] * 128 * 2048 + o0, [[2048, 128], [1, 1024]]), [d_in], [tb])
            tbufs[j] = tb

        tload(0)
        tj = 0
        for nm, c0 in (("q", 0), ("k", 512)):
            s = load_slab("in", c0)
            o0 = 0 if nm == "q" else 1024
            for i in range(ntl):
                tb = tbufs[tj]
                if tj + 1 < len(tspecs):
                    tload(tj + 1)
                tj += 1
                pb = inproj(s, hb, i * 128, np_)
                tabC_buf[0] = tb
                ret_rope(pb, slots[i], tb.flat(512, np_=np_),
                         lambda hf, tb=tb: tb.ap([[128, 4], [1, 64]], off=512 + hf * 64, np_=np_),
                         qr[i] if nm == "q" else kr[i], np_)
                yield
        for n in range(2):
            s = load_slab("in", 1024 + n * 512)
            for i in range(ntl):
                pb = inproj(s, hb, i * 128, np_)
                P.op("act", lambda e, pb=pb, i=i, n=n: e.activation(out=vv[i].flat(512, off=n * 512, np_=np_), in_=pb.flat(np_=np_), func=AF.Copy,
                                                                   scale=rstd1.flat(1, off=slots[i], np_=np_)), [pb, rstd1], [vv[i]])
                yield
        for n in range(2):
            s = load_slab("in", 2048 + n * 512)
            for i in range(ntl):
                pb = inproj(s, hb, i * 128, np_)
                P.op("act", lambda e, pb=pb, i=i, n=n: e.activation(out=GG[i].flat(512, off=n * 512, np_=np_), in_=pb.flat(np_=np_), func=AF.Silu,
                                                                   scale=rstd1.flat(1, off=slots[i], np_=np_)), [pb, rstd1], [GG[i]])
                yield
        wide_banks[:] = [1, 2]

    def gen_A2(blk):
        I = blkinfo(blk)
        samp, np_, ntl = I["samp"], I["np_"], I["ntl"]
        Bf = passA_bufs(blk)
        L = {**G, **I, **Bf}
        oslabs[(0, 0)] = load_slab("o", 0, 0)
        oslabs[(0, 1)] = load_slab("o", 0, 1)
        op_banks[0] = [1, 2] if samp else [4, 5]
        if not samp:
            ret_mixer_front(P, L, 0)
            yield
        for i in range(ntl):
            if not samp:
                yield from ret_mixer_prompt(P, L, i, (lambda i=i: ret_mixer_front(P, L, i + 1)) if i + 1 < ntl else None)
            else:
                ret_mixer_sample(P, L, T, d_in, newout)
            transposes(tok, np_, colscale=gretcol)
            yield

            def ev(n, pb, i=i):
                P.op("act", lambda e: e.activation(out=brr[i].flat(512, off=n * 512, np_=np_), in_=pb.flat(np_=np_), func=AF.Copy), [pb], [brr[i]])
            outproj(0, xT, np_, ev)
            yield
        if blk == NT // BLK - 1:
            for h in range(4):
                P.op("dve", lambda e, h=h: e.tensor_scalar(Sst.flat(256, off=h * 256), Sst.flat(256, off=h * 256), float(GAM[h] ** 2048), None,
                                                           op0=ALU.mult), [Sst], [Sst])
            P.dma("sp", bass.AP(T["rsp"], 0, [[256, 128], [128 * 256, 4], [1, 256]]), Sst.ap([[256, 4], [1, 256]]), [Sst], [d_rsp])

    def gen_C1(blk):
        I = blkinfo(blk)
        samp, np_, ntl, hb, slots = I["samp"], I["np_"], I["ntl"], I["hb"], I["slots"]
        ar = arena_of((3 * blk + 2) % 2)
        mbuf = [ar(i, 0, 1024, F32) for i in range(ntl)]
        tokC = [ar(i, 4096, 1024, BF16) for i in range(ntl)]
        for n in range(2):
            s = load_slab("in", 5632 + n * 512)
            for i in range(ntl):
                pb = inproj(s, hb, i * 128, np_)
                sg = sg16[(n * ntl + i) % 2]
                P.op("act", lambda e, pb=pb, i=i, sg=sg: e.activation(out=sg.flat(np_=np_), in_=pb.flat(np_=np_), func=AF.Tanh,
                                                                     scale=rh.flat(1, off=slots[i], np_=np_)), [pb, rh], [sg])
                P.op("dve", lambda e, i=i, n=n, sg=sg: e.scalar_tensor_tensor(out=mbuf[i].flat(512, off=n * 512, np_=np_), in0=sg.flat(np_=np_), scalar=1.0,
                                                                             in1=brr[i].flat(512, off=n * 512, np_=np_), op0=ALU.add, op1=ALU.mult), [sg, brr[i]], [mbuf[i]])
                yield

    def gen_C1b(blk):
        I = blkinfo(blk)
        samp, np_, ntl, hb, slots = I["samp"], I["np_"], I["ntl"], I["hb"], I["slots"]
        ar = arena_of((3 * blk + 2) % 2)
        mbuf = [ar(i, 0, 1024, F32) for i in range(ntl)]
        tokC = [ar(i, 4096, 1024, BF16) for i in range(ntl)]
        oslabs[(2, 0)] = load_slab("o", 2, 0)
        oslabs[(2, 1)] = load_slab("o", 2, 1)
        for n in range(2):
            s = load_slab("in", 6656 + n * 512)
            for i in range(ntl):
                pb = inproj(s, hb, i * 128, np_)
                sg = sg16[(n * ntl + i) % 2]
                tA = tmpAr[(n * ntl + i) % 2]
                P.op("act", lambda e, pb=pb, i=i, sg=sg: e.activation(out=sg.flat(np_=np_), in_=pb.flat(np_=np_), func=AF.Tanh,
                                                                     scale=rh.flat(1, off=slots[i], np_=np_)), [pb, rh], [sg])
                P.op("dve", lambda e, i=i, n=n, sg=sg, tA=tA: e.scalar_tensor_tensor(out=tA.flat(np_=np_), in0=sg.flat(np_=np_), scalar=1.0,
                                                                                    in1=brs[i].flat(512, off=n * 512, np_=np_), op0=ALU.add, op1=ALU.mult), [sg, brs[i]], [tA])
                P.op("pool", lambda e, i=i, n=n, tA=tA: e.tensor_tensor(out=tokC[i].flat(512, off=n * 512, np_=np_), in0=mbuf[i].flat(512, off=n * 512, np_=np_),
                                                                       in1=tA.flat(np_=np_), op=ALU.add), [mbuf[i], tA], [tokC[i]])
                yield

    def gen_C2(blk):
        I = blkinfo(blk)
        samp, np_, ntl, slots = I["samp"], I["np_"], I["ntl"], I["slots"]
        ar = arena_of((3 * blk + 2) % 2)
        tokC = [ar(i, 4096, 1024, BF16) for i in range(ntl)]
        op_banks[0] = [5, 6, 7]
        for i in range(ntl):
            xb_ = ybuf[i % 2]
            if samp:
                P.dma("sp", xb_.flat(np_=16), xsd.ap(), [d_in], [xb_])
            else:
                P.dma("pool", xb_.flat(), bass.AP(xm, slots[i] * 128 * D, [[D, 128], [1, D]]), [d_in], [xb_])
            transposes(tokC[i], np_)
            yield

            def ev(n, pb, xb_=xb_):
                P.op("dve", lambda e: e.scalar_tensor_tensor(out=xb_.flat(512, off=n * 512, np_=np_), in0=pb.flat(np_=np_), scalar=0.5,
                                                             in1=xb_.flat(512, off=n * 512, np_=np_), op0=ALU.mult, op1=ALU.add), [pb, xb_], [xb_])
            outproj(2, xT, np_, ev)
            if samp:
                d_ys = newout()
                P.dma("sp", T["ysd"].ap(), xb_.flat(np_=16), [xb_], [d_ys])
            else:
                P.dma("pool", bass.AP(T["yp"], slots[i] * 128 * D, [[D, 128], [1, D]]), xb_.flat(), [xb_], [d_yp])
            yield
        op_banks[0] = [1, 2]

    def run(g):
        for _ in g:
            pass

    def inter(g1, g2, r1=1, r2=1):
        live = [[g1, r1], [g2, r2]]
        while live:
            for ent in list(live):
                for _ in range(ent[1]):
                    try:
                        next(ent[0])
                    except StopIteration:
                        live.remove(ent)
                        break

    def chain(*gs):
        for g in gs:
            yield from g

    no_prefetch[0] = True
    run(gen_S0(0))
    run(gen_A1(0))

    def with_convs(g):
        for _ in g:
            emit_convs(1)
            yield
        emit_convs(100)
    for blk in range(nblocks):
        I = blkinfo(blk)
        no_prefetch[0] = (blk == 0)
        gB1, gB2 = swa_pass(P, {**G, **I, "arena": arena_of((3 * blk + 1) % 2)}, T, d_in, newout)
        if I["samp"]:
            run(gen_A2(blk)); run(gB1); run(gB2); run(gen_C1(blk)); run(gen_C1b(blk)); run(gen_C2(blk))
            continue
        inter(with_convs(gen_A2(blk)) if blk == 0 else gen_A2(blk), gB1, 1, 1)
        inter(gB2, gen_C1(blk), 1, 1)
        if blk + 1 < nblocks:
            inter(gen_C1b(blk), gen_S0(blk + 1), 1, 1)
            inter(gen_C2(blk), gen_A1(blk + 1), 1, 3)
        else:
            run(gen_C1b(blk))
            run(gen_C2(blk))
    P.op("sp", None, d_outs, [])


def ret_mixer_front(P, L, i):
    qr, kr = L["qr"][i], L["kr"][i]
    ident, cmask = L["ident"], L["cmask"]
    qkT = L["qkT2"][i % 2]
    scT = L["scT2"][i % 2]
    pt = P.psum(0, BF16)
    for h in range(4):
        P.op("pe", lambda e, h=h: e.transpose(pt.flat(128, off=h * 128), qr.flat(128, off=h * 128), ident.flat()), [qr, ident], [pt])
    for h in range(4):
        P.op("pe", lambda e, h=h: e.transpose(pt.flat(128, off=(4 + h) * 128), kr.flat(128, off=h * 128), ident.flat()), [kr, ident], [pt])
    P.op("act", lambda e: e.activation(out=qkT.flat(), in_=pt.flat(), func=AF.Copy), [pt], [qkT])
    psc = P.psum(3, F32)
    for h in range(4):
        P.op("pe", lambda e, h=h: e.matmul(psc.flat(128, off=h * 128), qkT.flat(128, off=(4 + h) * 128), qkT.flat(128, off=h * 128),
                                           start=True, stop=True), [qkT], [psc])
    P.op("dve", lambda e: e.tensor_tensor(out=scT.ap([[128, 4], [1, 128]]), in0=psc.ap([[128, 4], [1, 128]]),
                                          in1=cmask.ap([[0, 4], [1, 128]]), op=ALU.mult), [psc, cmask], [scT])


def ret_mixer_prompt(P, L, i, mid=None):
    qr, kr, vv, GG = L["qr"][i], L["kr"][i], L["vv"][i], L["GG"][i]
    Sst, Sbf, small, junk, tok, mhalf = (L[k] for k in ("Sst", "Sbf", "small", "junk", "tok", "mhalf"))
    qkT = L["qkT2"][i % 2]
    scT = L["scT2"][i % 2]
    po = [P.psum(4, F32), P.psum(5, F32)]
    for h in range(4):
        pb = po[h // 2]
        P.op("pe", lambda e, h=h, pb=pb: e.matmul(pb.flat(256, off=(h % 2) * 256), scT.flat(128, off=h * 128), vv.flat(256, off=h * 256),
                                                  start=True, stop=False), [scT, vv], [pb])
        P.op("pe", lambda e, h=h, pb=pb: e.matmul(pb.flat(256, off=(h % 2) * 256), qkT.flat(128, off=h * 128), Sbf.flat(256, off=h * 256),
                                                  start=False, stop=True), [qkT, Sbf], [pb])
    yield
    pstb = [P.psum(6, F32), P.psum(7, F32)]
    for h in range(4):
        pb = pstb[h // 2]
        P.op("pe", lambda e, h=h, pb=pb: e.matmul(pb.flat(256, off=(h % 2) * 256), kr.flat(128, off=h * 128), vv.flat(256, off=h * 256),
                                                  start=True, stop=True), [kr, vv], [pb])
    yield
    for j in range(2):
        P.op("dve", lambda e, j=j: e.tensor_tensor(out=Sst.flat(512, off=j * 512), in0=pstb[j].flat(), in1=Sst.flat(512, off=j * 512),
                                                   op=ALU.add), [pstb[j], Sst], [Sst])
    P.op("act", lambda e: e.activation(out=Sbf.flat(), in_=Sst.flat(), func=AF.Copy), [Sst], [Sbf])
    yield
    for h in range(4):
        pb = po[h // 2]
        P.op("act", lambda e, h=h, pb=pb: e.activation(out=junk.flat(256), in_=pb.flat(256, off=(h % 2) * 256), func=AF.Square,
                                                       accum_out=small.flat(1, off=h)), [pb], [junk, small])
    P.op("dve", lambda e: e.tensor_scalar(small.flat(4), small.flat(4), 1.0 / 256, EPS, op0=ALU.mult, op1=ALU.add), [small], [small])
    P.op("pool", lambda e: e.tensor_tensor(out=small.flat(4), in0=small.flat(4), in1=mhalf.flat(4), op=ALU.pow), [small, mhalf], [small])
    for h in range(4):
        pb = po[h // 2]
        P.op("dve", lambda e, h=h, pb=pb: e.scalar_tensor_tensor(out=tok.flat(256, off=h * 256), in0=pb.flat(256, off=(h % 2) * 256),
                                                                 scalar=small.flat(1, off=h), in1=GG.flat(256, off=h * 256),
                                                                 op0=ALU.mult, op1=ALU.mult), [pb, small, GG], [tok])
    yield
    if mid is not None:
        mid()
        yield


def ret_mixer_sample(P, L, T, d_in, newout):
    qr, kr, vv, GG = L["qr"][0], L["kr"][0], L["vv"][0], L["GG"][0]
    ident, small, small2, junk, tok, mhalf, tmpA, Sst = (L[k] for k in ("ident", "small", "small2", "junk", "tok", "mhalf", "tmpA", "Sst"))
    eye16, Qm, Sf32, Kb, Sbf2, qT, transposes = (L[k] for k in ("eye16", "Qm", "Sf32", "Kb", "Sbf2", "qT", "transposes"))
    N = 16
    P.dma("sp", eye16.flat(), T["eyed"].ap(), [d_in], [eye16])
    P.op("dve", lambda e: e.tensor_tensor(out=tmpA.flat(512, np_=N), in0=qr.flat(512, np_=N), in1=kr.flat(512, np_=N), op=ALU.mult), [qr, kr], [tmpA])
    P.op("dve", lambda e: e.tensor_reduce(out=small.flat(4, off=8, np_=N), in_=tmpA.ap([[128, 4], [1, 128]], np_=N), op=ALU.add, axis=AX.X), [tmpA], [small])
    transposes(qr, N, n=4, width=128, dst=qT)
    for h in range(4):
        P.op("dve", lambda e, h=h: e.tensor_tensor(out=Qm.ap([[16, 16], [1, 16]], off=h * 256), in0=qT.ap([[0, 16], [1, 16]], off=h * 16),
                                                   in1=eye16.ap([[16, 16], [1, 16]]), op=ALU.mult), [qT, eye16], [Qm])
    po = [P.psum(4 + h, F32) for h in range(4)]
    d_rss = newout()
    state = T["state"]
    for b in range(N):
        sbf = Sbf2[b % 4]
        sf = Sf32[b % 4]
        kb = Kb[b % 2]
        src = bass.AP(state, b * 4 * 128 * 256, [[256, 128], [128 * 256, 4], [1, 256]])
        P.dma("pool", sbf.ap([[256, 4], [1, 256]]), src, [d_in], [sbf])
        P.dma("sp", sf.ap([[256, 4], [1, 256]]), src, [d_in], [sf])
        for h in range(4):
            P.op("pe", lambda e, h=h, b=b, sbf=sbf: e.matmul(po[h].flat(256, np_=N), Qm.ap([[1, 16]], off=h * 256 + b * 16), sbf.flat(256, off=h * 256),
                                                            start=(b == 0), stop=(b == N - 1)), [Qm, sbf], [po[h]])
        P.op("dve", lambda e, b=b, kb=kb: e.tensor_scalar(kb.flat(512, np_=N), kr.flat(512, np_=N), ident.ap([[1, 1]], off=b, np_=N), None, op0=ALU.mult),
             [kr, ident], [kb])
        for j in range(2):
            pb = P.psum(1 + (2 * b + j) % 3, F32)
            for hh in range(2):
                h = 2 * j + hh
                P.op("pe", lambda e, h=h, hh=hh, pb=pb, kb=kb: e.matmul(pb.flat(256, off=hh * 256), kb.flat(128, off=h * 128, np_=N), vv.flat(256, off=h * 256, np_=N),
                                                                     start=True, stop=True), [kb, vv], [pb])
            for hh in range(2):
                h = 2 * j + hh
                P.op("dve", lambda e, h=h, hh=hh, pb=pb, sf=sf: e.scalar_tensor_tensor(out=sf.flat(256, off=h * 256), in0=sf.flat(256, off=h * 256), scalar=float(GAM[h]),
                                                                                    in1=pb.flat(256, off=hh * 256), op0=ALU.mult, op1=ALU.add), [sf, pb], [sf])
        P.dma("act", bass.AP(T["rss"], b * 4 * 128 * 256, [[256, 128], [128 * 256, 4], [1, 256]]), sf.ap([[256, 4], [1, 256]]), [sf], [d_rss])
    o32 = Sst
    for h in range(4):
        P.op("dve", lambda e, h=h: e.tensor_scalar(o32.flat(256, off=h * 256, np_=N), vv.flat(256, off=h * 256, np_=N), small.flat(1, off=8 + h, np_=N), None, op0=ALU.mult),
             [vv, small], [o32])
        P.op("dve", lambda e, h=h: e.scalar_tensor_tensor(out=o32.flat(256, off=h * 256, np_=N), in0=po[h].flat(256, np_=N), scalar=float(GAM[h]),
                                                          in1=o32.flat(256, off=h * 256, np_=N), op0=ALU.mult, op1=ALU.add), [po[h], o32], [o32])
    for h in range(4):
        P.op("act", lambda e, h=h: e.activation(out=junk.flat(256, np_=N), in_=o32.flat(256, off=h * 256, np_=N), func=AF.Square,
                                                accum_out=small2.flat(1, off=h, np_=N)), [o32], [junk, small2])
    P.op("dve", lambda e: e.tensor_scalar(small2.flat(4, np_=N), small2.flat(4, np_=N), 1.0 / 256, EPS, op0=ALU.mult, op1=ALU.add), [small2], [small2])
    P.op("pool", lambda e: e.tensor_tensor(out=small2.flat(4, np_=N), in0=small2.flat(4, np_=N), in1=mhalf.flat(4, np_=N), op=ALU.pow), [small2, mhalf], [small2])
    for h in range(4):
        P.op("dve", lambda e, h=h: e.scalar_tensor_tensor(out=tok.flat(256, off=h * 256, np_=N), in0=o32.flat(256, off=h * 256, np_=N),
                                                          scalar=small2.flat(1, off=h, np_=N), in1=GG.flat(256, off=h * 256, np_=N),
                                                          op0=ALU.mult, op1=ALU.mult), [o32, small2, GG], [tok])


def swa_pass(P, L, T, d_in, newout):
    samp, np_, ntl, hb, slots, blk = (L[k] for k in ("samp", "np_", "ntl", "hb", "slots", "blk"))
    arena, load_slab, inproj, transposes, outproj, oslabs = (L[k] for k in ("arena", "load_slab", "inproj", "transposes", "outproj", "oslabs"))
    ident, swamask, rstd1, r2, mhalf, gq4, esrep = (L[k] for k in ("ident", "swamask", "rstd1", "r2", "mhalf", "gq4", "esrep"))
    tmpX, tmpA, tmpB, small, small2, junk, tok, xT = (L[k] for k in ("tmpX", "tmpA", "tmpB", "small", "small2", "junk", "tok", "xT"))
    qT, kT, vaug, pT, gate2, kf32, vf32, stabs, swt, brs, gtile = (L[k] for k in
        ("qT", "kT", "vaug", "pT", "gate2", "kf32", "vf32", "stabs", "swt", "brs", "gtile"))
    q16 = [arena(i, 0, 1024, BF16) for i in range(ntl)]
    kdup = [arena(i, 2048, 512, BF16) for i in range(ntl)]
    gate = [arena(i, 4096, 1024, BF16) for i in range(ntl)]
    swtb = {}
    kt_todo = []

    def load_tab(tslot, key):
        r = L["swt_i"][0] % 2
        L["swt_i"][0] += 1
        sb_, w = stabs[r], swt[r]
        P.dma("sp", sb_.flat(np_=np_), bass.AP(T["stab"], tslot * 128 * 128, [[128, np_], [1, 128]]), [d_in], [sb_])
        P.op("pool", lambda e: e.tensor_tensor(out=w.ap([[128, 2], [1, 128]], np_=np_), in0=sb_.ap([[0, 2], [1, 128]], np_=np_),
                                               in1=gq4.ap([[128, 2], [1, 128]], np_=np_), op=ALU.mult), [sb_, gq4], [w])
        swtb[key] = w

    def qk_norm_rope(pb, nh, slot, w, coff, out_ap_fn, outbuf, ssbuf, soff):
        n = nh * 64
        ti = L["tmp_i"][0] % 2
        L["tmp_i"][0] += 1
        tmpX, tmpA, tmpB = L["tmpXr"][ti], L["tmpAr"][ti], L["tmpBr"][ti]
        P.op("act", lambda e: e.activation(out=tmpX.flat(n, np_=np_), in_=pb.flat(n, np_=np_), func=AF.Square), [pb], [tmpX])
        P.op("dve", lambda e: e.tensor_reduce(out=ssbuf.flat(nh, off=soff, np_=np_), in_=tmpX.ap([[64, nh], [1, 64]], np_=np_),
                                              op=ALU.add, axis=AX.X), [tmpX], [ssbuf])
        P.op("dve", lambda e: e.tensor_scalar(ssbuf.flat(nh, off=soff, np_=np_), ssbuf.flat(nh, off=soff, np_=np_),
                                              r2.flat(1, off=slot, np_=np_), EPS, op0=ALU.mult, op1=ALU.add), [ssbuf, r2], [ssbuf])
        P.op("pool", lambda e: e.tensor_tensor(out=ssbuf.flat(nh, off=soff, np_=np_), in0=ssbuf.flat(nh, off=soff, np_=np_),
                                               in1=mhalf.flat(nh, np_=np_), op=ALU.pow), [ssbuf, mhalf], [ssbuf])
        P.op("dve", lambda e: e.tensor_scalar(ssbuf.flat(nh, off=soff, np_=np_), ssbuf.flat(nh, off=soff, np_=np_),
                                              rstd1.flat(1, off=slot, np_=np_), None, op0=ALU.mult), [ssbuf, rstd1], [ssbuf])
        P.op("dve", lambda e: e.tensor_tensor(out=tmpA.ap([[64, nh], [1, 64]], np_=np_), in0=pb.ap([[64, nh], [1, 64]], np_=np_),
                                              in1=w.ap([[0, nh], [1, 64]], off=coff, np_=np_), op=ALU.mult), [pb, w], [tmpA])
        for hf in range(2):
            P.op("dve", lambda e, hf=hf: e.tensor_tensor(out=tmpB.ap([[64, nh], [1, 32]], off=hf * 32, np_=np_),
                                                         in0=pb.ap([[64, nh], [1, 32]], off=(1 - hf) * 32, np_=np_),
                                                         in1=w.ap([[0, nh], [1, 32]], off=coff + 64 + hf * 32, np_=np_), op=ALU.mult),
                 [pb, w], [tmpB])
        P.op("dve", lambda e: e.tensor_tensor(out=tmpA.flat(n, np_=np_), in0=tmpA.flat(n, np_=np_), in1=tmpB.flat(n, np_=np_), op=ALU.add),
             [tmpA, tmpB], [tmpA])
        P.op("pool", lambda e: e.tensor_tensor(out=out_ap_fn(), in0=tmpA.ap([[64, nh], [1, 64]], np_=np_),
                                               in1=ssbuf.ap([[1, nh], [0, 64]], off=soff, np_=np_), op=ALU.mult), [tmpA, ssbuf], [outbuf])

    def kv_tile(pb, slot, w, ring, kd, last):
        qk_norm_rope(pb, 4, slot, w, 128, lambda: kf32.ap([[64, 4], [1, 64]], np_=np_), kf32, L["small3"], 0)
        P.op("pool", lambda e: e.tensor_copy(kd.ap([[128, 4], [64, 2], [1, 64]], np_=np_), kf32.ap([[64, 4], [0, 2], [1, 64]], np_=np_)), [kf32], [kd])
        P.op("act", lambda e: e.activation(out=vaug[ring].ap([[66, 4], [1, 64]], np_=np_), in_=pb.ap([[64, 4], [1, 64]], off=256, np_=np_),
                                           func=AF.Copy, scale=rstd1.flat(1, off=slot, np_=np_)), [pb, rstd1], [vaug[ring]])
        if last or samp:
            P.op("act", lambda e: e.activation(out=vf32.flat(256, np_=np_), in_=pb.flat(256, off=256, np_=np_), func=AF.Copy,
                                               scale=rstd1.flat(1, off=slot, np_=np_)), [pb, rstd1], [vf32])
        if last:
            P.dma("sp", T["kpd"].ap(), kf32.flat(256), [kf32], [L["d_kp"]])
            P.dma("sp", T["vpd"].ap(), vf32.flat(256), [vf32], [L["d_vp"]])
        if not samp:
            kt_todo.append((kd, ring))

    def flush_kt():
        for kd, ring in kt_todo:
            transposes(kd, 128, n=4, width=128, dst=kT[ring])
        del kt_todo[:]

    g0 = gtile[0]

    def stage1():
        s_kv = load_slab("in", 4096)
        if blk == 0:
            load_tab(16, "m1")
            pb = inproj(s_kv, L["hTm1"], 0, 128, kst=128)
            kv_tile(pb, 31, swtb["m1"], 0, kdup[0], False)
        flush_kt()
        for i in range(ntl):
            load_tab(17 if samp else slots[i], i)
            pb = inproj(s_kv, hb, i * 128, np_)
            kv_tile(pb, slots[i], swtb[i], (g0 + i + 1) % 5, kdup[i], (not samp) and slots[i] == NT - 1)
            yield
        for n in range(2):
            s = load_slab("in", 3072 + n * 512)
            for i in range(ntl):
                load_tab(17 if samp else slots[i], i)
                pb = inproj(s, hb, i * 128, np_)
                qk_norm_rope(pb, 8, slots[i], swtb[i], 0, lambda i=i, n=n: q16[i].ap([[64, 8], [1, 64]], off=n * 512, np_=np_), q16[i], small2, n * 8)
                yield
        for n in range(2):
            s = load_slab("in", 4608 + n * 512)
            for i in range(ntl):
                pb = inproj(s, hb, i * 128, np_)
                P.op("act", lambda e, pb=pb, i=i, n=n: e.activation(out=gate[i].flat(512, off=n * 512, np_=np_), in_=pb.flat(np_=np_), func=AF.Silu,
                                                                   scale=rstd1.flat(1, off=slots[i], np_=np_)), [pb, rstd1], [gate[i]])
                yield


    def mixers():
        flush_kt()
        oslabs[(1, 0)] = load_slab("o", 1, 0)
        oslabs[(1, 1)] = load_slab("o", 1, 1)
        L["op_banks"][0] = [1, 2] if samp else [3, 4]
        pov = [P.psum(5, F32), P.psum(6, F32), P.psum(7, F32)]

        def povslot(h):
            return pov[h // 6], (h % 6) * 65

        def scores(i, g):
            var = 0 if slots[i] == 0 else 1
            cur, prev = (g0 + i + 1) % 5, (g0 + i) % 5
            pTs = pT[g % 2]
            for par in range(2):
                bank = P.psum((3 + par) if (i == 0 or g % 2 == 0) else (1 + par), F32)
                for bi, kTb in enumerate((kT[prev], kT[cur])):
                    P.op("pe", lambda e, bank=bank, bi=bi, kTb=kTb, g=g, par=par: e.matmul(
                        bank.flat(256, off=bi * 256), kTb.ap([[1, 128]], off=g * 128, p0=64 * par, np_=64),
                        qT.ap([[128, 2], [1, 128]], off=2 * g * 128, p0=64 * par, np_=64), start=True, stop=False), [kTb, qT], [bank])
                    P.op("pe", lambda e, bank=bank, bi=bi, var=var: e.matmul(
                        bank.flat(256, off=bi * 256), ident.flat(), swamask.ap([[0, 2], [1, 128]], off=var * 256 + bi * 128),
                        start=False, stop=True), [ident, swamask], [bank])
                P.op("act", lambda e, bank=bank, par=par, pTs=pTs: e.activation(out=pTs[par].flat(), in_=bank.flat(), func=AF.Exp, scale=0.125),
                     [bank], [pTs[par]])

        def pv(i, g):
            cur, prev = (g0 + i + 1) % 5, (g0 + i) % 5
            pTs = pT[g % 2]
            for par in range(2):
                for jj in range(2):
                    h = 4 * g + 2 * jj + par
                    pb, off = povslot(h)
                    P.op("pe", lambda e, pb=pb, off=off, par=par, jj=jj, pTs=pTs, g=g, prev=prev: e.matmul(
                        pb.flat(65, off=off), pTs[par].flat(128, off=jj * 128), vaug[prev].flat(65, off=g * 66), start=True, stop=False),
                        [pTs[par], vaug[prev]], [pb])
                    P.op("pe", lambda e, pb=pb, off=off, par=par, jj=jj, pTs=pTs, g=g, cur=cur: e.matmul(
                        pb.flat(65, off=off), pTs[par].flat(128, off=(2 + jj) * 128), vaug[cur].flat(65, off=g * 66), start=False, stop=True),
                        [pTs[par], vaug[cur]], [pb])

        def front(i):
            transposes(q16[i], 128, n=8, width=128, dst=qT)
            scores(i, 0)

        def mix(i):
            if samp:
                swa_mixer_sample(P, L, T, d_in, newout, q16[i], kdup[i], vaug[(g0 + i + 1) % 5], pov, povslot)
            else:
                if i == 0:
                    front(0)
                    yield
                for g in range(4):
                    if g + 1 < 4:
                        scores(i, g + 1)
                    elif i + 1 < ntl:
                        front(i + 1)
                    yield
                    pv(i, g)
                    yield
            for bnk in range(3):
                nh = 6 if bnk < 2 else 4
                P.op("dve", lambda e, bnk=bnk, nh=nh: e.tensor_tensor(out=small2.flat(nh, off=16 + bnk * 6, np_=np_), in0=pov[bnk].ap([[65, nh]], off=64, np_=np_),
                                                                     in1=esrep.flat(nh, off=bnk * 6, np_=np_), op=ALU.add), [pov[bnk], esrep], [small2])
            P.op("dve", lambda e: e.reciprocal(out=small2.flat(16, off=16, np_=np_), in_=small2.flat(16, off=16, np_=np_)), [small2], [small2])
            P.op("pool", lambda e, i=i: e.tensor_tensor(out=gate2.ap([[64, 16], [1, 64]], np_=np_), in0=gate[i].ap([[64, 16], [1, 64]], np_=np_),
                                                        in1=small2.ap([[1, 16], [0, 64]], off=16, np_=np_), op=ALU.mult), [gate[i], small2], [gate2])
            for bnk in range(3):
                nh = 6 if bnk < 2 else 4
                P.op("dve", lambda e, bnk=bnk, nh=nh: e.tensor_tensor(out=tok.ap([[64, nh], [1, 64]], off=bnk * 384, np_=np_),
                                                                     in0=pov[bnk].ap([[65, nh], [1, 64]], np_=np_),
                                                                     in1=gate2.ap([[64, nh], [1, 64]], off=bnk * 384, np_=np_), op=ALU.mult),
                     [pov[bnk], gate2], [tok])
            if T["dbg"] and blk == 0:
                P.dma("sp", bass.AP(T["dbg3"], i * 128 * 1024, [[1024, 128], [1, 1024]]), tok.flat(), [tok], [newout()])
            yield
            transposes(tok, np_)
            yield

            def ev(n, pb, i=i):
                P.op("act", lambda e: e.activation(out=brs[i].flat(512, off=n * 512, np_=np_), in_=pb.flat(np_=np_), func=AF.Copy), [pb], [brs[i]])
            outproj(1, xT, np_, ev)
            yield
        for i in range(ntl):
            yield from mix(i)
        gtile[0] += ntl


    return stage1(), mixers()


def sample_cache_copy(P, T, d_in, newout):
    for src_t, dst_t in ((T["ckd"], T["ksd"]), (T["cvd"], T["vsd"])):
        for hb_ in range(2):
            d1 = newout()
            o = hb_ * 8 * 128 * 256
            P.dma("sp", bass.AP(dst_t, o, [[128 * 256, 8], [1, 127 * 256]]), bass.AP(src_t, o + 256, [[128 * 256, 8], [1, 127 * 256]]), [d_in], [d1])


def sample_prefetch(P, L, T, d_in):
    Kc, Vc = L["Kc"], L["Vc"]
    ckd, cvd = T["ckd"], T["cvd"]
    N = 16
    P.op("pool", lambda e: e.memset(Vc.flat(), 1.0), [], [Vc])
    thr = [P.dram(None) for _ in range(6)]
    j = 0
    for p0, npp in ((0, 64), (64, 63)):
        k = P.dram(None)
        L["kc_keys"].append(k)
        P.dma("pool", Kc.ap([[256, 16], [1, 256]], p0=p0, np_=npp), bass.AP(ckd, 256 * (1 + p0), [[256, npp], [128 * 256, 16], [1, 256]]),
              [d_in, Kc], [k, thr[j % 6]])
        j += 1
        for b in range(N):
            k = P.dram(None)
            L["vc_keys"].append(k)
            P.dma("pool", Vc.ap([[66, 4], [1, 64]], off=b * 264, p0=p0, np_=npp),
                  bass.AP(cvd, b * 128 * 256 + 256 * (1 + p0), [[256, npp], [64, 4], [1, 64]]), [d_in, Vc], [k, thr[j % 6]])
            j += 1


def swa_mixer_sample(P, L, T, d_in, newout, q16, kd, vaug_s, pov, povslot):
    ident, kf32, vf32, eye16, Kc, Vc, KTr, pTs, Pm, qT, transposes = (L[k] for k in
        ("ident", "kf32", "vf32", "eye16", "Kc", "Vc", "KTr", "pTs", "Pm", "qT", "transposes"))
    ckd, cvd = T["ckd"], T["cvd"]
    N = 16
    for dst_t, newrow in ((T["ksd"], kf32), (T["vsd"], vf32)):
        d2 = newout()
        P.dma("sp", bass.AP(dst_t, 127 * 256, [[128 * 256, 16], [1, 256]]), newrow.flat(256, np_=N), [newrow], [d2])
    P.dma("sp", Kc.ap([[256, 16], [64, 4], [1, 64]], p0=127, np_=1), kd.ap([[128, 4], [1, 64]], np_=N), [kd, Kc] + L["kc_keys"], [Kc])
    P.dma("sp", Vc.ap([[264, 16], [66, 4], [1, 64]], p0=127, np_=1), vaug_s.ap([[66, 4], [1, 64]], np_=N), [vaug_s, Vc] + L["vc_keys"], [Vc])
    transposes(q16, N, n=16, width=64, dst=qT)
    sc = P.psum(3, F32)
    for bp in range(N // 2):
        ktr = KTr[bp % 2]
        pt = P.psum(0, BF16)
        for j in range(8):
            b, g = 2 * bp + j // 4, j % 4
            P.op("pe", lambda e, j=j, b=b, g=g, pt=pt: e.transpose(pt.ap([[1, 128]], off=j * 128, np_=64), Kc.ap([[1, 64]], off=b * 256 + g * 64), ident.flat()),
                 [Kc, ident], [pt])
        P.op("act", lambda e, pt=pt, ktr=ktr: e.activation(out=ktr.flat(1024, np_=64), in_=pt.flat(1024, np_=64), func=AF.Copy), [pt], [ktr])
        for j in range(8):
            b, g = 2 * bp + j // 4, j % 4
            P.op("pe", lambda e, j=j, b=b, g=g, ktr=ktr: e.matmul(sc.ap([[1, 4]], off=b * 16 + g * 4), ktr.ap([[1, 128]], off=j * 128, np_=64),
                                                                 qT.ap([[16, 4]], off=4 * g * 16 + b, np_=64), start=True, stop=True), [ktr, qT], [sc])
    P.op("act", lambda e: e.activation(out=pTs.flat(256), in_=sc.flat(256), func=AF.Exp, scale=0.125), [sc], [pTs])
    for h in range(16):
        P.op("dve", lambda e, h=h: e.tensor_tensor(out=Pm.ap([[16, 16], [1, 16]], off=h * 256), in0=pTs.ap([[0, 16], [16, 16]], off=h),
                                                   in1=eye16.ap([[16, 16], [1, 16]]), op=ALU.mult), [pTs, eye16], [Pm])
    for h in range(16):
        g = h // 4
        pb, off = povslot(h)
        for b in range(N):
            P.op("pe", lambda e, h=h, g=g, b=b, pb=pb, off=off: e.matmul(pb.flat(65, off=off, np_=N), Pm.ap([[1, 16]], off=h * 256 + b * 16),
                                                                        Vc.ap([[1, 65]], off=b * 264 + g * 66), start=(b == 0), stop=(b == N - 1)),
                 [Pm, Vc], [pb])


def _tables(half):
    f64 = np.float64
    T = np.arange(2048, dtype=f64)
    lg = np.log(np.array(GAM, dtype=f64))
    inv_r = 10000.0 ** (-np.arange(0, 128, 2, dtype=f64) / 128)
    inv_s = 10000.0 ** (-np.arange(0, 64, 2, dtype=f64) / 64)

    def cs2(pos, inv):
        ang = pos[:, None] * inv[None, :]
        c, s_ = np.cos(ang), np.sin(ang)
        return np.concatenate([c, c], 1), np.concatenate([-s_, s_], 1)

    pos_main = half * 2048 + T
    c2, s2 = cs2(pos_main, inv_r)
    aq = np.exp((T[:, None] + 1) * lg[None, :])
    ak = np.exp(-(T[:, None] + 1) * lg[None, :]) / np.sqrt(128.0)
    rt = np.stack([c2[:, None, :] * aq[:, :, None], s2[:, None, :] * aq[:, :, None],
                   c2[:, None, :] * ak[:, :, None], s2[:, None, :] * ak[:, :, None]], 1)
    rtab_main = rt.reshape(16, 128, 2048).astype(np.float32)
    c2p, s2p = cs2(T, inv_r)
    akp = np.exp((2047 - T[:, None]) * lg[None, :]) / np.sqrt(128.0)
    rtp = np.stack([c2p[:, None, :] * akp[:, :, None], s2p[:, None, :] * akp[:, :, None]], 1)
    rtab_pre = rtp.reshape(16, 128, 1024).astype(np.float32)
    c2s, s2s = cs2(np.full(16, 8192.0), inv_r)
    one = np.ones((16, 4, 1))
    rts = np.stack([c2s[:, None, :] * one, s2s[:, None, :] * one, c2s[:, None, :] * one / np.sqrt(128.0),
                    s2s[:, None, :] * one / np.sqrt(128.0)], 1)
    rtab_samp = rts.reshape(16, 2048).astype(np.float32)
    stab = np.zeros((18, 128, 128), np.float32)
    cm, sm = cs2(pos_main, inv_s)
    stab[:16] = np.concatenate([cm, sm], 1).reshape(16, 128, 128)
    cp, sp_ = cs2(1920 + np.arange(128, dtype=f64), inv_s)
    stab[16] = np.concatenate([cp, sp_], 1)
    cs_, ss_ = cs2(np.full(128, 8192.0), inv_s)
    stab[17] = np.concatenate([cs_, ss_], 1)
    k = np.arange(128)[:, None]
    q = np.arange(128)[None, :]
    cmask = (q >= k).astype(np.float32)
    prev = np.where(k > q, 0.0, MASKNEG)
    cur = np.where(k <= q, 0.0, MASKNEG)
    first_prev = prev if half == 1 else np.full((128, 128), MASKNEG)
    swamask = np.stack([first_prev, cur, prev, cur], 1).reshape(128, 512)
    swamask = np.concatenate([swamask, np.zeros((128, 512))], 1).astype(np.float32)
    eye16 = np.tile(np.eye(16, dtype=np.float32).reshape(1, 256), (128, 1))
    return dict(rtab_main=rtab_main, rtab_pre=rtab_pre, rtab_samp=rtab_samp, stab=stab, cmask=cmask,
                swamask=swamask, eye16=eye16, ident=np.eye(128, dtype=np.float32))


_CACHE = {}


def kernel(x_prompt, x_sample, state_ret, cache_swa_k, cache_swa_v, norm_g, w_in, ret_norm_g,
           swa_q_g, swa_k_g, swa_sinks, w_br_ret, w_br_swa, w_out, _with_sample=True, _dbg=False):
    f = lambda a: np.ascontiguousarray(np.asarray(a, dtype=np.float32))
    x_prompt, x_sample, state_ret, cache_swa_k, cache_swa_v = map(f, (x_prompt, x_sample, state_ret, cache_swa_k, cache_swa_v))
    w_in_ = f(w_in)[0]
    w_o3 = np.ascontiguousarray(np.stack([f(w_br_ret)[0], f(w_br_swa)[0], f(w_out)[0]], 0))
    gq, gk = f(swa_q_g)[0], f(swa_k_g)[0]
    sw = lambda g: np.concatenate([g[32:], g[:32]])
    gqk = np.ascontiguousarray(np.stack([gq, sw(gq), gk, sw(gk)], 0))
    ng = np.ascontiguousarray(f(norm_g)[0].reshape(8, 128).T)
    if "nc" not in _CACHE:
        _CACHE["nc"] = build_program(with_sample=_with_sample, dbg=_dbg)[0]
        _CACHE["tabs"] = [_tables(0), _tables(1)]
    nc = _CACHE["nc"]
    in_maps = []
    for c in range(8):
        b, half = c // 2, c % 2
        m = dict(_CACHE["tabs"][half])
        m["xm"] = np.ascontiguousarray(x_prompt[b, half * 2048:(half + 1) * 2048])
        m["xp"] = np.ascontiguousarray(x_prompt[b, 0:2048]) if half == 1 else np.zeros((2048, D), np.float32)
        m["xs"] = np.ascontiguousarray(x_sample[16 * c:16 * c + 16, 0])
        m["w_in"] = w_in_
        m["w_o3"] = w_o3
        m["norm_g"] = ng
        m["ret_g"] = np.ascontiguousarray(f(ret_norm_g)[0].reshape(8, 128).T)
        m["gqk"] = gqk
        m["sinks"] = np.ascontiguousarray(f(swa_sinks)[0])
        m["state"] = np.ascontiguousarray(state_ret[0, 16 * c:16 * c + 16])
        m["ck"] = np.ascontiguousarray(cache_swa_k[0, 16 * c:16 * c + 16].reshape(16, 128, 256))
        m["cv"] = np.ascontiguousarray(cache_swa_v[0, 16 * c:16 * c + 16].reshape(16, 128, 256))
        in_maps.append(m)
    res = run_bass_kernel_spmd(nc, in_maps, core_ids=list(range(8)))
    R = res.results
    if _dbg:
        _CACHE["dbg"] = {k: np.asarray(R[0][k]).astype(np.float32) for k in ("dbg1", "dbg2", "dbg3")}
    yp = np.zeros((4, 4096, D), np.float32)
    ys = np.zeros((128, 1, D), np.float32)
    rsp = np.zeros((1, 4, 4, 128, 256), np.float32)
    rss = np.zeros((1, 128, 4, 128, 256), np.float32)
    kp = np.zeros((1, 4, 128, 4, 64), np.float32)
    vp = np.zeros((1, 4, 128, 4, 64), np.float32)
    ks = np.zeros((1, 128, 128, 4, 64), np.float32)
    vs = np.zeros((1, 128, 128, 4, 64), np.float32)
    for c in range(8):
        b, half = c // 2, c % 2
        r = R[c]
        yp[b, half * 2048:(half + 1) * 2048] = r["yp"]
        ys[16 * c:16 * c + 16, 0] = r["ys"]
        rss[0, 16 * c:16 * c + 16] = r["rss"]
        ks[0, 16 * c:16 * c + 16] = r["ks"].reshape(16, 128, 4, 64)
        vs[0, 16 * c:16 * c + 16] = r["vs"].reshape(16, 128, 4, 64)
        if half == 1:
            rsp[0, b] = r["rsp"]
            kp[0, b] = r["kp"].reshape(128, 4, 64)
            vp[0, b] = r["vp"].reshape(128, 4, 64)
    return yp, ys, rsp, rss, kp, vp, ks, vs
```

```python
import numpy as np
import concourse.bass as bass
import concourse.mybir as mybir
from concourse.bass_utils import run_bass_kernel_spmd

F32 = mybir.dt.float32
BF16 = mybir.dt.bfloat16
AF = mybir.ActivationFunctionType
ALU = mybir.AluOpType
AX = mybir.AxisListType

GRAN = 512


class Buf:
    def __init__(self, tensor, off, n, rowlen, keys, esz):
        self.tensor = tensor
        self.off = off
        self.n = n
        self.rowlen = rowlen
        self.keys = keys
        self.esz = esz

    def ap(self, dims, off=0, p0=0, np_=128):
        return bass.AP(self.tensor, p0 * self.rowlen + self.off + off,
                       [[self.rowlen, np_]] + [list(d) for d in dims])

    def flat(self, n=None, off=0, p0=0, np_=128):
        return self.ap([[1, self.n - off if n is None else n]], off=off, p0=p0, np_=np_)


class Prog:
    def __init__(self, nc, arena_bytes, n_dma_sems=52):
        self.nc = nc
        self.ops = []
        self.last_w = {}
        self.readers = {}
        self.arena_bytes = arena_bytes
        self.arena_top = 0
        self.n_dma_sems = n_dma_sems
        self.dram_key = 0
        self.t16 = None
        self.ps = []

    def setup_mem(self, t16, ps_tensors):
        self.t16 = t16
        self.t32 = t16.bitcast(F32)
        self.ps = [(p, p.bitcast(BF16)) for p in ps_tensors]

    def alloc_off(self, nbytes):
        off = self.arena_top
        self.arena_top = (off + nbytes + GRAN - 1) // GRAN * GRAN
        assert self.arena_top <= self.arena_bytes, (self.arena_top, self.arena_bytes)
        return off

    def sb_at(self, boff, n, dtype):
        esz = 4 if dtype == F32 else 2
        t = self.t32 if dtype == F32 else self.t16
        assert boff % esz == 0
        keys = frozenset(("sb", g) for g in range(boff // GRAN, (boff + n * esz - 1) // GRAN + 1))
        return Buf(t, boff // esz, n, self.arena_bytes // esz, keys, esz)

    def sb(self, n, dtype):
        esz = 4 if dtype == F32 else 2
        return self.sb_at(self.alloc_off(n * esz), n, dtype)

    def psum(self, bank, dtype=F32, off=0, n=None):
        t = self.ps[bank][0 if dtype == F32 else 1]
        full = 512 if dtype == F32 else 1024
        if n is None:
            n = full - off
        return Buf(t, off, n, full, frozenset([("ps", bank)]), 4 if dtype == F32 else 2)

    def dram(self, tensor, esz=4):
        self.dram_key += 1
        return Buf(tensor, 0, 0, 0, frozenset([("dr", self.dram_key)]), esz)

    def op(self, eng, fn, reads=(), writes=(), dma=False):
        idx = len(self.ops)
        deps = set()
        rk = set()
        for b in reads:
            rk |= b.keys
        wk = set()
        for b in writes:
            wk |= b.keys
        for k in rk:
            w = self.last_w.get(k)
            if w is not None:
                deps.add(w)
        for k in wk:
            w = self.last_w.get(k)
            if w is not None:
                deps.add(w)
            for r in self.readers.get(k, ()):
                deps.add(r)
        for k in rk:
            lst = self.readers.setdefault(k, [])
            if not dma:
                lst[:] = [r for r in lst if self.ops[r]["dma"] or self.ops[r]["eng"] != eng]
            lst.append(idx)
        for k in wk:
            self.readers[k] = []
            self.last_w[k] = idx
        deps.discard(idx)
        self.ops.append(dict(eng=eng, fn=fn, deps=deps, dma=dma))
        return idx

    def dma(self, queue, out_ap, in_ap, reads, writes, **kw):
        def fn(e):
            return e.dma_start(out=out_ap, in_=in_ap, **kw)
        return self.op(queue, fn, reads, writes, dma=True)

    def emit(self):
        nc = self.nc
        ops = self.ops
        engs = ["pe", "act", "dve", "pool", "sp"]
        eng_obj = dict(pe=nc.tensor, act=nc.scalar, dve=nc.vector, pool=nc.gpsimd, sp=nc.sync)
        sig = [False] * len(ops)
        for i, o in enumerate(ops):
            nd = set()
            for d in o["deps"]:
                od = ops[d]
                if (not od["dma"]) and od["eng"] == "pe" and o["eng"] == "pe" and not o["dma"]:
                    continue
                nd.add(d)
                sig[d] = True
            o["deps"] = nd
        cnt = {e: 0 for e in engs}
        dma_n = 0
        n_sw = 12
        n_hw = self.n_dma_sems - n_sw
        qn = {"pool": 0, "hw": 0}
        for i, o in enumerate(ops):
            if o["dma"]:
                if o["eng"] == "pool":
                    o["dsem"] = qn["pool"] % n_sw
                    o["duse"] = qn["pool"] // n_sw + 1
                    qn["pool"] += 1
                else:
                    o["dsem"] = n_sw + qn["hw"] % n_hw
                    o["duse"] = qn["hw"] // n_hw + 1
                    qn["hw"] += 1
                dma_n += 1
            elif sig[i]:
                cnt[o["eng"]] += 1
                o["sigval"] = cnt[o["eng"]]
        self.stats = dict(cnt=dict(cnt), dma=dma_n, nops=len(ops))

        import contextlib
        with contextlib.ExitStack() as st:
            esem = {e: st.enter_context(nc.semaphore("s_" + e)) for e in engs}
            dsem = [st.enter_context(nc.semaphore("d%d" % i)) for i in range(self.n_dma_sems)]
            block = st.enter_context(nc.Block())
            per_eng = {e: [i for i, o in enumerate(ops) if o["eng"] == e] for e in engs}

            def run(e, eng):
                seen = {}
                for i in per_eng[e]:
                    o = ops[i]
                    need = {}
                    for d in o["deps"]:
                        od = ops[d]
                        if od["dma"]:
                            key = ("d", od["dsem"])
                            val = 16 * od["duse"]
                        else:
                            key = ("e", od["eng"])
                            val = od["sigval"]
                        if need.get(key, 0) < val:
                            need[key] = val
                    if o["dma"] and o["duse"] > 1:
                        key = ("d", o["dsem"])
                        val = 16 * (o["duse"] - 1)
                        if need.get(key, 0) < val:
                            need[key] = val
                    for key, val in need.items():
                        if seen.get(key, 0) >= val:
                            continue
                        seen[key] = val
                        s = dsem[key[1]] if key[0] == "d" else esem[key[1]]
                        eng.wait_ge(s, val)
                    if o["fn"] is None:
                        continue
                    ins = o["fn"](eng)
                    if o["dma"]:
                        ins.then_inc(dsem[o["dsem"]], 16)
                    elif sig[i]:
                        ins.then_inc(esem[e], 1)

            @block.tensor
            def _(eng):
                run("pe", eng)

            @block.scalar
            def _(eng):
                run("act", eng)

            @block.vector
            def _(eng):
                run("dve", eng)

            @block.gpsimd
            def _(eng):
                run("pool", eng)

            @block.sync
            def _(eng):
                run("sp", eng)


import contextlib
import ml_dtypes

D = 1024
NT = 16
BLK = 4
EPS = 1e-6
GAM = [1.0 - 2.0 ** (-5.0 - h) for h in range(4)]
COLS = dict(rq=0, rk=512, rv=1024, rg=2048, sq=3072, skv=4096, sg=4608, mr=5632, ms=6656)
MASKNEG = -30000.0


def build_program(with_sample=True, dbg=False):
    nc = bass.Bass("TRN2", target_bir_lowering=False)

    def din(name, shape, dt=F32):
        return nc.dram_tensor(name, shape, dt, kind="ExternalInput")

    def dout(name, shape, dt=F32):
        return nc.dram_tensor(name, shape, dt, kind="ExternalOutput")

    xm = din("xm", [2048, D]); xp = din("xp", [2048, D]); xsd = din("xs", [16, D])
    w_in = din("w_in", [D, 7680]); w_o3 = din("w_o3", [3, D, D])
    norm_g = din("norm_g", [128, 8]); ret_g = din("ret_g", [128, 8])
    gqk = din("gqk", [4, 64]); sinks = din("sinks", [16])
    state = din("state", [16, 4, 128, 256]); ckd = din("ck", [16, 128, 256]); cvd = din("cv", [16, 128, 256])
    rtab_main = din("rtab_main", [16, 128, 2048]); rtab_pre = din("rtab_pre", [16, 128, 1024])
    rtab_samp = din("rtab_samp", [16, 2048]); stab = din("stab", [18, 128, 128])
    identd = din("ident", [128, 128]); cmaskd = din("cmask", [128, 128]); swamaskd = din("swamask", [128, 1024])
    eyed = din("eye16", [128, 256])

    yp = dout("yp", [2048, D]); ysd = dout("ys", [16, D]); rsp = dout("rsp", [4, 128, 256])
    rss = dout("rss", [16, 4, 128, 256]); kpd = dout("kp", [128, 256]); vpd = dout("vp", [128, 256])
    ksd = dout("ks", [16, 128, 256]); vsd = dout("vs", [16, 128, 256])

    dbg1 = dout("dbg1", [4, 128, 1024], BF16) if dbg else None
    dbg2 = dout("dbg2", [4, 128, 1024], BF16) if dbg else None
    dbg3 = dout("dbg3", [4, 128, 1024], BF16) if dbg else None
    w_in_bf = nc.dram_tensor("w_in_bf", [15, 128, 4096], BF16, kind="Internal")
    w_o3_bf = nc.dram_tensor("w_o3_bf", [6, 128, 4096], BF16, kind="Internal")

    ARENA = 204 * 1024
    with contextlib.ExitStack() as st:
        t16 = st.enter_context(nc.sbuf_tensor("arena", [128, ARENA // 2], BF16))
        pst = [st.enter_context(nc.psum_tensor("ps%d" % i, [128, 512], F32)) for i in range(8)]
        P = Prog(nc, ARENA)
        P.setup_mem(t16, pst)
        _build(nc, P, locals(), with_sample)
        P.emit()
    return nc, P


def _build(nc, P, T, with_sample):
    xm, xp, xsd, w_in, w_o3 = T["xm"], T["xp"], T["xsd"], T["w_in"], T["w_o3"]
    w_in_bf, w_o3_bf = T["w_in_bf"], T["w_o3_bf"]
    d_in = P.dram(None)
    d_wslab = {}
    for c0 in range(0, 7680, 512):
        d_wslab[("in", c0)] = P.dram(None)
    for m in range(3):
        for n in range(2):
            d_wslab[("o", m, n)] = P.dram(None)
    d_outs = []

    def newout():
        b = P.dram(None)
        d_outs.append(b)
        return b

    ident = P.sb(128, BF16); cmask = P.sb(128, F32); swamask = P.sb(1024, BF16)
    gcol = P.sb(8, F32); gretcol = P.sb(8, F32); gq4 = P.sb(256, F32); esrep = P.sb(16, F32)
    rstd1 = P.sb(40, F32); r2 = P.sb(40, F32); mhalf = P.sb(16, F32); rh = P.sb(40, F32)
    P.dma("pool", ident.flat(), T["identd"].ap(), [d_in], [ident])
    P.dma("pool", swamask.flat(), T["swamaskd"].ap(), [d_in], [swamask])
    P.dma("sp", cmask.flat(), T["cmaskd"].ap(), [d_in], [cmask])
    P.dma("sp", gcol.flat(8), T["norm_g"].ap(), [d_in], [gcol])
    P.dma("sp", gretcol.flat(8), T["ret_g"].ap(), [d_in], [gretcol])
    P.dma("sp", gq4.flat(), bass.AP(T["gqk"], 0, [[0, 128], [1, 256]]), [d_in], [gq4])
    P.dma("sp", esrep.flat(), bass.AP(T["sinks"], 0, [[0, 128], [1, 16]]), [d_in], [esrep])
    P.op("act", lambda e: e.activation(out=esrep.flat(), in_=esrep.flat(), func=AF.Exp), [esrep], [esrep])
    P.op("dve", lambda e: e.memset(mhalf.flat(), -0.5), [], [mhalf])

    NSLAB = 4
    slabs = [P.sb(8 * 512, BF16) for _ in range(NSLAB)]
    slab_i = [0]
    pre_slabs = []
    for j, c0 in enumerate((512, 1024, 1536)):
        sl = slabs[j]
        for kh in range(2):
            half = P.sb_at(sl.off * 2 + kh * 4096, 2048, BF16)
            P.dma("pool", half.ap([[512, 4], [1, 512]]), bass.AP(w_in, kh * 512 * 7680 + c0, [[7680, 128], [128 * 7680, 4], [1, 512]]),
                  [d_in], [half])
        pre_slabs.append(sl)
    slab_i[0] = 0

    thr = [P.dram(None) for _ in range(3)]
    thr_i = [0]

    def conv_in(c0):
        tk = thr[thr_i[0] % 3]
        thr_i[0] += 1
        P.dma("pool", bass.AP(w_in_bf, (c0 // 512) * 128 * 4096, [[4096, 128], [512, 8], [1, 512]]),
              bass.AP(w_in, c0, [[7680, 128], [128 * 7680, 8], [1, 512]]), [d_in], [d_wslab[("in", c0)], tk])

    def conv_o(m, n):
        tk = thr[thr_i[0] % 3]
        thr_i[0] += 1
        P.dma("pool", bass.AP(w_o3_bf, (2 * m + n) * 128 * 4096, [[4096, 128], [512, 8], [1, 512]]),
              bass.AP(w_o3, m * D * D + n * 512, [[D, 128], [128 * D, 8], [1, 512]]), [d_in], [d_wslab[("o", m, n)], tk])

    conv_q = []
    for c0 in (0, 512, 1024, 1536, 2048, 2560):
        conv_q.append(lambda c0=c0: conv_in(c0))
    conv_q.append(lambda: conv_o(0, 0)); conv_q.append(lambda: conv_o(0, 1))
    for c0 in (4096, 3072, 3584, 4608, 5120):
        conv_q.append(lambda c0=c0: conv_in(c0))
    conv_q.append(lambda: conv_o(1, 0)); conv_q.append(lambda: conv_o(1, 1))
    for c0 in range(5632, 7680, 512):
        conv_q.append(lambda c0=c0: conv_in(c0))
    conv_q.append(lambda: conv_o(2, 0)); conv_q.append(lambda: conv_o(2, 1))

    def emit_convs(n):
        for _ in range(n):
            if conv_q:
                conv_q.pop(0)()

    xbuf = [P.sb(1024, F32) for _ in range(2)]
    xb16 = P.sb(1024, BF16); junk = P.sb(1024, BF16)
    hT = [P.sb(8 * 512, BF16) for _ in range(2)]
    hTm1 = P.sb(8 * 128, BF16)
    arena0 = P.alloc_off(BLK * 6144)
    arena1 = P.alloc_off(BLK * 6144)
    ybuf = [P.sb(1024, F32) for _ in range(2)]
    brr = [P.sb(1024, BF16)]
    brs = [P.sb(1024, BF16)]
    br_rest0 = P.arena_top
    brr += [P.sb(1024, BF16) for _ in range(BLK - 1)]
    brs += [P.sb(1024, BF16) for _ in range(BLK - 1)]
    eye16 = P.sb(256, F32)
    pTs = P.sb(256, BF16)
    Pm = P.sb_at(br_rest0, 4096, BF16)
    Sbf2 = [P.sb_at(br_rest0 + 8192 + j * 2048, 1024, BF16) for j in range(2)]
    Kc = P.sb_at(arena0 + 6144, 4096, BF16)
    Vc = P.sb_at(arena0 + 6144 + 8192, 16 * 264, BF16)
    rtab = [P.sb(1024, F32) for _ in range(2)]
    rtab_i = [0]
    stabs = [P.sb(128, F32) for _ in range(2)]
    swt = [P.sb(256, F32) for _ in range(2)]
    tmpXr = [P.sb(512, F32) for _ in range(2)]; tmpAr = [P.sb(512, F32) for _ in range(2)]; tmpBr = [P.sb(512, F32) for _ in range(2)]
    tmpX, tmpA, tmpB = tmpXr[0], tmpAr[0], tmpBr[0]
    tmp_i = [0]
    xT = P.sb(1024, BF16)
    tok = P.sb(1024, BF16)
    Sst = P.sb(1024, F32); Sbf = P.sb(1024, BF16)
    scT = P.sb(512, BF16); small = P.sb(64, F32); small2 = P.sb(64, F32); small3 = P.sb(64, F32)
    qkT2 = [P.sb(1024, BF16) for _ in range(2)]
    scT2 = [scT, P.sb(512, BF16)]
    qT = qkT2[0]
    Qm = qkT2[1]
    Kb = scT2
    kT = [P.sb(512, BF16) for _ in range(5)]
    Sf32 = [P.sb_at(tmpXr[0].off * 4 + j * 4096, 1024, F32) for j in range(2)]
    Sf32 += [P.sb_at(hT[1].off * 2 + j * 4096, 1024, F32) for j in range(2)]
    Sbf2 += [P.sb_at(kT[0].off * 2 + j * 2048, 1024, BF16) for j in range(2)]
    vaug = [P.sb(4 * 66, BF16) for _ in range(5)]
    pT = [[P.sb(512, BF16) for _ in range(2)] for _ in range(2)]
    KTr = [P.sb_at(pT[j][0].off * 2, 1024, BF16) for j in range(2)]
    gate2 = P.sb(1024, BF16)
    kf32 = P.sb(256, F32); vf32 = P.sb(256, F32)
    sg16 = [P.sb(512, BF16) for _ in range(2)]
    for v in vaug:
        P.op("dve", lambda e, v=v: e.memset(v.flat(), 1.0), [], [v])

    def arena_of(which):
        base = arena0 if which == 0 else arena1

        def arena(i, boff, n, dt):
            return P.sb_at(base + i * 6144 + boff, n, dt)
        return arena

    arena = arena_of(0)

    B_TP = 0
    mm_banks = [1, 2]
    mm_i = [0]

    wide_banks = [1, 2]
    wide_i = [0]

    def next_mm(wide=False):
        if wide:
            b = wide_banks[wide_i[0] % len(wide_banks)]
            wide_i[0] += 1
            return b
        b = mm_banks[mm_i[0] % len(mm_banks)]
        mm_i[0] += 1
        return b

    oslab_i = [0]
    no_prefetch = [False]

    BLOCK_SEQ = [0, 512, 1024, 1536, 2048, 2560, 4096, 3072, 3584, 4608, 5120, 5632, 6144, 6656, 7168]
    slab_seq = []
    slab_pos = [0]
    slab_pending = {}

    def issue_in_slab(pos):
        c0 = slab_seq[pos]
        s = slabs[pos % 2]
        src = bass.AP(w_in_bf, (c0 // 512) * 128 * 4096, [[4096, 128], [1, 4096]])
        P.dma("sp", s.flat(4096), src, [d_wslab[("in", c0)]], [s])
        slab_pending[pos] = s

    def load_slab(kind, *a):
        if kind == "in":
            pos = slab_pos[0]
            assert slab_seq[pos] == a[0], (pos, slab_seq[pos], a[0])
            if pos not in slab_pending:
                issue_in_slab(pos)
            s = slab_pending.pop(pos)
            slab_pos[0] += 1
            if pos + 1 < len(slab_seq) and not (no_prefetch[0]):
                issue_in_slab(pos + 1)
            return s
        else:
            m, n = a
            s = slabs[2 + n]
            src = bass.AP(w_o3_bf, (2 * m + n) * 128 * 4096, [[4096, 128], [1, 4096]])
            key = d_wslab[("o", m, n)]
        P.dma("pool", s.flat(4096), src, [key], [s])
        return s

    def rstd_pow(col_buf, col, np_):
        P.op("pool", lambda e: e.tensor_tensor(out=col_buf.flat(1, off=col, np_=np_), in0=col_buf.flat(1, off=col, np_=np_),
                                               in1=mhalf.flat(1, np_=np_), op=ALU.pow), [col_buf, mhalf], [col_buf])

    def xload(src_t, row0, np_, xslot):
        xb_ = xbuf[xslot]
        P.dma("sp", xb_.flat(np_=np_), bass.AP(src_t, row0 * D, [[D, np_], [1, D]]), [d_in], [xb_])

    def stage0(src_t, row0, np_, slot, hbuf, hcol, xslot, preloaded=False):
        xb_ = xbuf[xslot]
        if not preloaded:
            P.dma("sp", xb_.flat(np_=np_), bass.AP(src_t, row0 * D, [[D, np_], [1, D]]), [d_in], [xb_])
        P.op("act", lambda e: e.activation(out=junk.flat(np_=np_), in_=xb_.flat(np_=np_), func=AF.Square,
                                           accum_out=rstd1.flat(1, off=slot, np_=np_)), [xb_], [junk, rstd1])
        P.op("dve", lambda e: e.tensor_scalar(rstd1.flat(1, off=slot, np_=np_), rstd1.flat(1, off=slot, np_=np_), 1.0 / D, EPS,
                                              op0=ALU.mult, op1=ALU.add), [rstd1], [rstd1])
        rstd_pow(rstd1, slot, np_)
        P.op("dve", lambda e: e.scalar_tensor_tensor(out=r2.flat(1, off=slot, np_=np_), in0=rstd1.flat(1, off=slot, np_=np_),
                                                     scalar=1.0 / 64, in1=rstd1.flat(1, off=slot, np_=np_),
                                                     op0=ALU.mult, op1=ALU.mult), [rstd1], [r2])
        P.op("dve", lambda e: e.tensor_scalar(rh.flat(1, off=slot, np_=np_), rstd1.flat(1, off=slot, np_=np_), 0.5, None, op0=ALU.mult), [rstd1], [rh])
        P.op("dve", lambda e: e.tensor_copy(xb16.flat(np_=np_), xb_.flat(np_=np_)), [xb_], [xb16])
        pt = P.psum(B_TP, BF16)
        for j in range(8):
            P.op("pe", lambda e, j=j: e.transpose(pt.flat(np_, off=j * np_), xb16.flat(128, off=j * 128, np_=np_),
                                                  ident.ap([[1, np_]], np_=np_)), [xb16, ident], [pt])
        P.op("dve", lambda e: e.tensor_tensor(out=hbuf.ap([[512, 8], [1, np_]], off=hcol), in0=pt.ap([[np_, 8], [1, np_]]),
                                              in1=gcol.ap([[1, 8], [0, np_]]), op=ALU.mult), [pt, gcol], [hbuf])

    inproj_wide = [True]

    def inproj(slab, hbuf, hcol, np_, kst=512):
        pb = P.psum(next_mm(wide=inproj_wide[0]), F32)
        for k in range(8):
            P.op("pe", lambda e, k=k: e.matmul(pb.flat(512, np_=np_), hbuf.ap([[1, np_]], off=k * kst + hcol),
                                               slab.flat(512, off=k * 512), start=(k == 0), stop=(k == 7)),
                 [hbuf, slab], [pb])
        return pb

    def ret_rope(pb, slot, tabC, tabS, out_bf, np_):
        A = tmpAr[tmp_i[0] % 2]; B = tmpBr[tmp_i[0] % 2]
        tmp_i[0] += 1
        tb_ = tabC_buf[0]
        P.op("dve", lambda e: e.scalar_tensor_tensor(out=A.flat(np_=np_), in0=pb.flat(np_=np_), scalar=rstd1.flat(1, off=slot, np_=np_),
                                                     in1=tabC, op0=ALU.mult, op1=ALU.mult), [pb, rstd1, tb_], [A])
        for hf in range(2):
            P.op("dve", lambda e, hf=hf: e.scalar_tensor_tensor(out=B.ap([[128, 4], [1, 64]], off=hf * 64, np_=np_),
                                                                in0=pb.ap([[128, 4], [1, 64]], off=(1 - hf) * 64, np_=np_),
                                                                scalar=rstd1.flat(1, off=slot, np_=np_), in1=tabS(hf),
                                                                op0=ALU.mult, op1=ALU.mult), [pb, rstd1, tb_], [B])
        P.op("dve", lambda e: e.tensor_tensor(out=out_bf.flat(512, np_=np_), in0=A.flat(np_=np_), in1=B.flat(np_=np_),
                                              op=ALU.add), [A, B], [out_bf])

    tabC_buf = [None]

    def transposes(src_bf, np_, n=8, width=128, dst=None, eng="act", colscale=None):
        if dst is None:
            dst = xT
        pt = P.psum(B_TP, BF16)
        for j in range(n):
            P.op("pe", lambda e, j=j: e.transpose(pt.ap([[1, np_]], off=j * np_, np_=width),
                                                  src_bf.flat(width, off=j * width, np_=np_),
                                                  ident.ap([[1, np_]], np_=np_)), [src_bf, ident], [pt])
        if colscale is not None:
            P.op("dve", lambda e: e.tensor_tensor(out=dst.ap([[np_, n], [1, np_]], np_=width), in0=pt.ap([[np_, n], [1, np_]], np_=width),
                                                  in1=colscale.ap([[1, n], [0, np_]], np_=width), op=ALU.mult), [pt, colscale], [dst])
        elif eng == "act":
            P.op("act", lambda e: e.activation(out=dst.flat(n * np_, np_=width), in_=pt.flat(n * np_, np_=width), func=AF.Copy),
                 [pt], [dst])
        else:
            P.op("dve", lambda e: e.tensor_copy(dst.flat(n * np_, np_=width), pt.flat(n * np_, np_=width)), [pt], [dst])
        return dst

    op_banks = [[1, 2]]
    op_i = [0]

    def outproj(m, src_T, np_, evac):
        for n in range(2):
            s = oslabs[(m, n)]
            pb = P.psum(op_banks[0][op_i[0] % len(op_banks[0])], F32)
            op_i[0] += 1
            for k in range(8):
                P.op("pe", lambda e, k=k, s=s, pb=pb: e.matmul(pb.flat(512, np_=np_), src_T.ap([[1, np_]], off=k * np_),
                                                               s.flat(512, off=k * 512), start=(k == 0), stop=(k == 7)),
                     [src_T, s], [pb])
            evac(n, pb)

    oslabs = {}

    pstb = [P.psum(4 + h, F32) for h in range(4)]
    def pre_block(blk):
        hb = hT[blk % 2]
        xload(xp, blk * BLK * 128, 128, (blk * BLK) % 2)
        for i in range(BLK):
            t = blk * BLK + i
            if i + 1 < BLK:
                xload(xp, (t + 1) * 128, 128, (t + 1) % 2)
            stage0(xp, t * 128, 128, 16 + t, hb, i * 128, t % 2, preloaded=True)
        if blk == NT // BLK - 1:
            P.op("pool", lambda e: e.tensor_copy(hTm1.ap([[128, 8], [1, 128]]), hb.ap([[512, 8], [1, 128]], off=3 * 128)), [hb], [hTm1])
        kr = [arena(i, 1024, 512, BF16) for i in range(BLK)]
        vv = [arena(i, 2048, 1024, BF16) for i in range(BLK)]
        s_rk = pre_slabs[0]
        for i in range(BLK):
            t = blk * BLK + i
            tb = rtab[rtab_i[0] % 2]
            rtab_i[0] += 1
            P.dma("sp", tb.flat(1024), bass.AP(T["rtab_pre"], t * 128 * 1024, [[1024, 128], [1, 1024]]), [d_in], [tb])
            pb = inproj(s_rk, hb, i * 128, 128)
            tabC_buf[0] = tb
            ret_rope(pb, 16 + t, tb.flat(512), lambda hf, tb=tb: tb.ap([[128, 4], [1, 64]], off=512 + hf * 64), kr[i], 128)
            if t < 8:
                emit_convs(1)
        for n in range(2):
            s_rv = pre_slabs[1 + n]
            for i in range(BLK):
                t = blk * BLK + i
                pb = inproj(s_rv, hb, i * 128, 128)
                P.op("act", lambda e, pb=pb, i=i, n=n, t=t: e.activation(out=vv[i].flat(512, off=n * 512), in_=pb.flat(), func=AF.Copy,
                                                                      scale=rstd1.flat(1, off=16 + t)), [pb, rstd1], [vv[i]])
        for i in range(BLK):
            t = blk * BLK + i
            for h in range(4):
                P.op("pe", lambda e, h=h, i=i, t=t: e.matmul(pstb[h].flat(256), kr[i].flat(128, off=h * 128),
                                                          vv[i].flat(256, off=h * 256), start=(t == 0), stop=(t == NT - 1)),
                     [kr[i], vv[i]], [pstb[h]])

    inproj_wide[0] = False
    mm_banks[:] = [1, 2, 3]
    for blk in range(NT // BLK):
        pre_block(blk)
    inproj_wide[0] = True
    mm_banks[:] = [1, 2]
    for j in range(4):
        P.op("dve", lambda e, j=j: e.tensor_copy(Sst.flat(256, off=j * 256), pstb[j].flat(256)), [pstb[j]], [Sst])
    P.op("act", lambda e: e.activation(out=Sbf.flat(), in_=Sst.flat(), func=AF.Copy), [Sst], [Sbf])

    nblocks = NT // BLK + (1 if with_sample else 0)
    slab_seq.extend(BLOCK_SEQ * nblocks)
    d_yp = newout(); d_rsp = newout(); d_kp = newout(); d_vp = newout()
    gtile = [0]
    kc_keys = []
    vc_keys = []
    swt_i = [0]

    def do_stage0_block(blk):
        hb = hT[blk % 2]
        if blk < NT // BLK:
            for i in range(BLK):
                t = blk * BLK + i
                stage0(xm, t * 128, 128, t, hb, i * 128, t % 2)
        else:
            stage0(xsd, 0, 16, 32, hb, 0, 0)

    G = dict(locals())

    def blkinfo(blk):
        samp = blk >= NT // BLK
        return dict(samp=samp, np_=16 if samp else 128, ntl=1 if samp else BLK, hb=hT[blk % 2],
                    slots=[32] if samp else [blk * BLK + i for i in range(BLK)], blk=blk)

    def gen_S0(blk):
        hb = hT[blk % 2]
        if blk < NT // BLK:
            xload(xm, blk * BLK * 128, 128, (blk * BLK) % 2)
            for i in range(BLK):
                t = blk * BLK + i
                if i + 1 < BLK:
                    xload(xm, (t + 1) * 128, 128, (t + 1) % 2)
                stage0(xm, t * 128, 128, t, hb, i * 128, t % 2, preloaded=True)
                yield
        else:
            stage0(xsd, 0, 16, 32, hb, 0, 0)
            yield

    def passA_bufs(blk):
        ar = arena_of((3 * blk) % 2)
        n = 1 if blk >= NT // BLK else BLK
        return dict(qr=[ar(i, 0, 512, BF16) for i in range(n)], kr=[ar(i, 1024, 512, BF16) for i in range(n)],
                    vv=[ar(i, 2048, 1024, BF16) for i in range(n)], GG=[ar(i, 4096, 1024, BF16) for i in range(n)])

    def gen_A1(blk):
        I = blkinfo(blk)
        samp, np_, ntl, hb, slots = I["samp"], I["np_"], I["ntl"], I["hb"], I["slots"]
        Bf = passA_bufs(blk)
        qr, kr, vv, GG = Bf["qr"], Bf["kr"], Bf["vv"], Bf["GG"]
        wide_banks[:] = [1, 2, 3, 4]
        if samp:
            sample_prefetch(P, {**G, **I}, T, d_in)
        if with_sample and blk == 1:
            sample_cache_copy(P, T, d_in, newout)
        tspecs = [(o0, i) for o0 in (0, 1024) for i in range(ntl)]
        tbufs = {}

        def tload(j):
            o0, i = tspecs[j]
            tb = rtab[rtab_i[0] % 2]
            rtab_i[0] += 1
            if samp:
                P.dma("sp", tb.flat(1024, np_=16), bass.AP(T["rtab_samp"], o0, [[2048, 16], [1, 1024]]), [d_in], [tb])
            else:
                P.dma("sp", tb.flat(1024), bass.AP(T["rtab_main"], slots[i] * 128 * 2048 + o0, [[2048, 128], [1, 1024]]), [d_in], [tb])
            tbufs[j] = tb

        tload(0)
        tj = 0
        for nm, c0 in (("q", 0), ("k", 512)):
            s = load_slab("in", c0)
            o0 = 0 if nm == "q" else 1024
            for i in range(ntl):
                tb = tbufs[tj]
                if tj + 1 < len(tspecs):
                    tload(tj + 1)
                tj += 1
                pb = inproj(s, hb, i * 128, np_)
                tabC_buf[0] = tb
                ret_rope(pb, slots[i], tb.flat(512, np_=np_),
                         lambda hf, tb=tb: tb.ap([[128, 4], [1, 64]], off=512 + hf * 64, np_=np_),
                         qr[i] if nm == "q" else kr[i], np_)
                yield
        for n in range(2):
            s = load_slab("in", 1024 + n * 512)
            for i in range(ntl):
                pb = inproj(s, hb, i * 128, np_)
                P.op("act", lambda e, pb=pb, i=i, n=n: e.activation(out=vv[i].flat(512, off=n * 512, np_=np_), in_=pb.flat(np_=np_), func=AF.Copy,
                                                                   scale=rstd1.flat(1, off=slots[i], np_=np_)), [pb, rstd1], [vv[i]])
                yield
        for n in range(2):
            s = load_slab("in", 2048 + n * 512)
            for i in range(ntl):
                pb = inproj(s, hb, i * 128, np_)
                P.op("act", lambda e, pb=pb, i=i, n=n: e.activation(out=GG[i].flat(512, off=n * 512, np_=np_), in_=pb.flat(np_=np_), func=AF.Silu,
                                                                   scale=rstd1.flat(1, off=slots[i], np_=np_)), [pb, rstd1], [GG[i]])
                yield
        wide_banks[:] = [1, 2]

    def gen_A2(blk):
        I = blkinfo(blk)
        samp, np_, ntl = I["samp"], I["np_"], I["ntl"]
        Bf = passA_bufs(blk)
        L = {**G, **I, **Bf}
        oslabs[(0, 0)] = load_slab("o", 0, 0)
        oslabs[(0, 1)] = load_slab("o", 0, 1)
        op_banks[0] = [1, 2] if samp else [4, 5]
        if not samp:
            ret_mixer_front(P, L, 0)
            yield
        for i in range(ntl):
            if not samp:
                yield from ret_mixer_prompt(P, L, i, (lambda i=i: ret_mixer_front(P, L, i + 1)) if i + 1 < ntl else None)
            else:
                ret_mixer_sample(P, L, T, d_in, newout)
            transposes(tok, np_, colscale=gretcol)
            yield

            def ev(n, pb, i=i):
                P.op("act", lambda e: e.activation(out=brr[i].flat(512, off=n * 512, np_=np_), in_=pb.flat(np_=np_), func=AF.Copy), [pb], [brr[i]])
            outproj(0, xT, np_, ev)
            yield
        if blk == NT // BLK - 1:
            for h in range(4):
                P.op("dve", lambda e, h=h: e.tensor_scalar(Sst.flat(256, off=h * 256), Sst.flat(256, off=h * 256), float(GAM[h] ** 2048), None,
                                                           op0=ALU.mult), [Sst], [Sst])
            P.dma("sp", bass.AP(T["rsp"], 0, [[256, 128], [128 * 256, 4], [1, 256]]), Sst.ap([[256, 4], [1, 256]]), [Sst], [d_rsp])

    def gen_C1(blk):
        I = blkinfo(blk)
        samp, np_, ntl, hb, slots = I["samp"], I["np_"], I["ntl"], I["hb"], I["slots"]
        ar = arena_of((3 * blk + 2) % 2)
        mbuf = [ar(i, 0, 1024, F32) for i in range(ntl)]
        tokC = [ar(i, 4096, 1024, BF16) for i in range(ntl)]
        for n in range(2):
            s = load_slab("in", 5632 + n * 512)
            for i in range(ntl):
                pb = inproj(s, hb, i * 128, np_)
                sg = sg16[(n * ntl + i) % 2]
                P.op("act", lambda e, pb=pb, i=i, sg=sg: e.activation(out=sg.flat(np_=np_), in_=pb.flat(np_=np_), func=AF.Tanh,
                                                                     scale=rh.flat(1, off=slots[i], np_=np_)), [pb, rh], [sg])
                P.op("dve", lambda e, i=i, n=n, sg=sg: e.scalar_tensor_tensor(out=mbuf[i].flat(512, off=n * 512, np_=np_), in0=sg.flat(np_=np_), scalar=1.0,
                                                                             in1=brr[i].flat(512, off=n * 512, np_=np_), op0=ALU.add, op1=ALU.mult), [sg, brr[i]], [mbuf[i]])
                yield

    def gen_C1b(blk):
        I = blkinfo(blk)
        samp, np_, ntl, hb, slots = I["samp"], I["np_"], I["ntl"], I["hb"], I["slots"]
        ar = arena_of((3 * blk + 2) % 2)
        mbuf = [ar(i, 0, 1024, F32) for i in range(ntl)]
        tokC = [ar(i, 4096, 1024, BF16) for i in range(ntl)]
        oslabs[(2, 0)] = load_slab("o", 2, 0)
        oslabs[(2, 1)] = load_slab("o", 2, 1)
        for n in range(2):
            s = load_slab("in", 6656 + n * 512)
            for i in range(ntl):
                pb = inproj(s, hb, i * 128, np_)
                sg = sg16[(n * ntl + i) % 2]
                tA = tmpAr[(n * ntl + i) % 2]
                P.op("act", lambda e, pb=pb, i=i, sg=sg: e.activation(out=sg.flat(np_=np_), in_=pb.flat(np_=np_), func=AF.Tanh,
                                                                     scale=rh.flat(1, off=slots[i], np_=np_)), [pb, rh], [sg])
                P.op("dve", lambda e, i=i, n=n, sg=sg, tA=tA: e.scalar_tensor_tensor(out=tA.flat(np_=np_), in0=sg.flat(np_=np_), scalar=1.0,
                                                                                    in1=brs[i].flat(512, off=n * 512, np_=np_), op0=ALU.add, op1=ALU.mult), [sg, brs[i]], [tA])
                P.op("pool", lambda e, i=i, n=n, tA=tA: e.tensor_tensor(out=tokC[i].flat(512, off=n * 512, np_=np_), in0=mbuf[i].flat(512, off=n * 512, np_=np_),
                                                                       in1=tA.flat(np_=np_), op=ALU.add), [mbuf[i], tA], [tokC[i]])
                yield

    def gen_C2(blk):
        I = blkinfo(blk)
        samp, np_, ntl, slots = I["samp"], I["np_"], I["ntl"], I["slots"]
        ar = arena_of((3 * blk + 2) % 2)
        tokC = [ar(i, 4096, 1024, BF16) for i in range(ntl)]
        op_banks[0] = [5, 6, 7]
        for i in range(ntl):
            xb_ = ybuf[i % 2]
            if samp:
                P.dma("sp", xb_.flat(np_=16), xsd.ap(), [d_in], [xb_])
            else:
                P.dma("sp", xb_.flat(), bass.AP(xm, slots[i] * 128 * D, [[D, 128], [1, D]]), [d_in], [xb_])
            transposes(tokC[i], np_)
            yield

            def ev(n, pb, xb_=xb_):
                P.op("dve", lambda e: e.scalar_tensor_tensor(out=xb_.flat(512, off=n * 512, np_=np_), in0=pb.flat(np_=np_), scalar=0.5,
                                                             in1=xb_.flat(512, off=n * 512, np_=np_), op0=ALU.mult, op1=ALU.add), [pb, xb_], [xb_])
            outproj(2, xT, np_, ev)
            if samp:
                d_ys = newout()
                P.dma("sp", T["ysd"].ap(), xb_.flat(np_=16), [xb_], [d_ys])
            else:
                P.dma("pool", bass.AP(T["yp"], slots[i] * 128 * D, [[D, 128], [1, D]]), xb_.flat(), [xb_], [d_yp])
            yield
        op_banks[0] = [1, 2]

    def run(g):
        for _ in g:
            pass

    def inter(g1, g2, r1=1, r2=1):
        live = [[g1, r1], [g2, r2]]
        while live:
            for ent in list(live):
                for _ in range(ent[1]):
                    try:
                        next(ent[0])
                    except StopIteration:
                        live.remove(ent)
                        break

    def chain(*gs):
        for g in gs:
            yield from g

    no_prefetch[0] = True
    run(gen_S0(0))
    run(gen_A1(0))

    def with_convs(g):
        for _ in g:
            emit_convs(1)
            yield
        emit_convs(100)
    for blk in range(nblocks):
        I = blkinfo(blk)
        no_prefetch[0] = (blk == 0)
        gB1, gB2 = swa_pass(P, {**G, **I, "arena": arena_of((3 * blk + 1) % 2)}, T, d_in, newout)
        if I["samp"]:
            run(gen_A2(blk)); run(gB1); run(gB2); run(gen_C1(blk)); run(gen_C1b(blk)); run(gen_C2(blk))
            continue
        inter(with_convs(gen_A2(blk)) if blk == 0 else gen_A2(blk), gB1, 1, 1)
        inter(gB2, gen_C1(blk), 1, 1)
        if blk + 1 < nblocks:
            inter(gen_C1b(blk), gen_S0(blk + 1), 1, 1)
            inter(gen_C2(blk), gen_A1(blk + 1), 1, 3)
        else:
            run(gen_C1b(blk))
            run(gen_C2(blk))
    P.op("sp", None, d_outs, [])


def ret_mixer_front(P, L, i):
    qr, kr = L["qr"][i], L["kr"][i]
    ident, cmask = L["ident"], L["cmask"]
    qkT = L["qkT2"][i % 2]
    scT = L["scT2"][i % 2]
    pt = P.psum(0, BF16)
    for h in range(4):
        P.op("pe", lambda e, h=h: e.transpose(pt.flat(128, off=h * 128), qr.flat(128, off=h * 128), ident.flat()), [qr, ident], [pt])
    for h in range(4):
        P.op("pe", lambda e, h=h: e.transpose(pt.flat(128, off=(4 + h) * 128), kr.flat(128, off=h * 128), ident.flat()), [kr, ident], [pt])
    P.op("act", lambda e: e.activation(out=qkT.flat(), in_=pt.flat(), func=AF.Copy), [pt], [qkT])
    psc = P.psum(3, F32)
    for h in range(4):
        P.op("pe", lambda e, h=h: e.matmul(psc.flat(128, off=h * 128), qkT.flat(128, off=(4 + h) * 128), qkT.flat(128, off=h * 128),
                                           start=True, stop=True), [qkT], [psc])
    P.op("dve", lambda e: e.tensor_tensor(out=scT.ap([[128, 4], [1, 128]]), in0=psc.ap([[128, 4], [1, 128]]),
                                          in1=cmask.ap([[0, 4], [1, 128]]), op=ALU.mult), [psc, cmask], [scT])


def ret_mixer_prompt(P, L, i, mid=None):
    qr, kr, vv, GG = L["qr"][i], L["kr"][i], L["vv"][i], L["GG"][i]
    Sst, Sbf, small, junk, tok, mhalf = (L[k] for k in ("Sst", "Sbf", "small", "junk", "tok", "mhalf"))
    qkT = L["qkT2"][i % 2]
    scT = L["scT2"][i % 2]
    po = [P.psum(4, F32), P.psum(5, F32)]
    for h in range(4):
        pb = po[h // 2]
        P.op("pe", lambda e, h=h, pb=pb: e.matmul(pb.flat(256, off=(h % 2) * 256), scT.flat(128, off=h * 128), vv.flat(256, off=h * 256),
                                                  start=True, stop=False), [scT, vv], [pb])
        P.op("pe", lambda e, h=h, pb=pb: e.matmul(pb.flat(256, off=(h % 2) * 256), qkT.flat(128, off=h * 128), Sbf.flat(256, off=h * 256),
                                                  start=False, stop=True), [qkT, Sbf], [pb])
    yield
    pstb = [P.psum(6, F32), P.psum(7, F32)]
    for h in range(4):
        pb = pstb[h // 2]
        P.op("pe", lambda e, h=h, pb=pb: e.matmul(pb.flat(256, off=(h % 2) * 256), kr.flat(128, off=h * 128), vv.flat(256, off=h * 256),
                                                  start=True, stop=True), [kr, vv], [pb])
    yield
    for j in range(2):
        P.op("dve", lambda e, j=j: e.tensor_tensor(out=Sst.flat(512, off=j * 512), in0=pstb[j].flat(), in1=Sst.flat(512, off=j * 512),
                                                   op=ALU.add), [pstb[j], Sst], [Sst])
    P.op("act", lambda e: e.activation(out=Sbf.flat(), in_=Sst.flat(), func=AF.Copy), [Sst], [Sbf])
    yield
    for h in range(4):
        pb = po[h // 2]
        P.op("act", lambda e, h=h, pb=pb: e.activation(out=junk.flat(256), in_=pb.flat(256, off=(h % 2) * 256), func=AF.Square,
                                                       accum_out=small.flat(1, off=h)), [pb], [junk, small])
    P.op("dve", lambda e: e.tensor_scalar(small.flat(4), small.flat(4), 1.0 / 256, EPS, op0=ALU.mult, op1=ALU.add), [small], [small])
    P.op("pool", lambda e: e.tensor_tensor(out=small.flat(4), in0=small.flat(4), in1=mhalf.flat(4), op=ALU.pow), [small, mhalf], [small])
    for h in range(4):
        pb = po[h // 2]
        P.op("dve", lambda e, h=h, pb=pb: e.scalar_tensor_tensor(out=tok.flat(256, off=h * 256), in0=pb.flat(256, off=(h % 2) * 256),
                                                                 scalar=small.flat(1, off=h), in1=GG.flat(256, off=h * 256),
                                                                 op0=ALU.mult, op1=ALU.mult), [pb, small, GG], [tok])
    yield
    if mid is not None:
        mid()
        yield


def ret_mixer_sample(P, L, T, d_in, newout):
    qr, kr, vv, GG = L["qr"][0], L["kr"][0], L["vv"][0], L["GG"][0]
    ident, small, small2, junk, tok, mhalf, tmpA, Sst = (L[k] for k in ("ident", "small", "small2", "junk", "tok", "mhalf", "tmpA", "Sst"))
    eye16, Qm, Sf32, Kb, Sbf2, qT, transposes = (L[k] for k in ("eye16", "Qm", "Sf32", "Kb", "Sbf2", "qT", "transposes"))
    N = 16
    P.dma("sp", eye16.flat(), T["eyed"].ap(), [d_in], [eye16])
    P.op("dve", lambda e: e.tensor_tensor(out=tmpA.flat(512, np_=N), in0=qr.flat(512, np_=N), in1=kr.flat(512, np_=N), op=ALU.mult), [qr, kr], [tmpA])
    P.op("dve", lambda e: e.tensor_reduce(out=small.flat(4, off=8, np_=N), in_=tmpA.ap([[128, 4], [1, 128]], np_=N), op=ALU.add, axis=AX.X), [tmpA], [small])
    transposes(qr, N, n=4, width=128, dst=qT)
    for h in range(4):
        P.op("dve", lambda e, h=h: e.tensor_tensor(out=Qm.ap([[16, 16], [1, 16]], off=h * 256), in0=qT.ap([[0, 16], [1, 16]], off=h * 16),
                                                   in1=eye16.ap([[16, 16], [1, 16]]), op=ALU.mult), [qT, eye16], [Qm])
    po = [P.psum(4 + h, F32) for h in range(4)]
    d_rss = newout()
    state = T["state"]
    for b in range(N):
        sbf = Sbf2[b % 4]
        sf = Sf32[b % 4]
        kb = Kb[b % 2]
        src = bass.AP(state, b * 4 * 128 * 256, [[256, 128], [128 * 256, 4], [1, 256]])
        P.dma("pool", sbf.ap([[256, 4], [1, 256]]), src, [d_in], [sbf])
        P.dma("sp", sf.ap([[256, 4], [1, 256]]), src, [d_in], [sf])
        for h in range(4):
            P.op("pe", lambda e, h=h, b=b, sbf=sbf: e.matmul(po[h].flat(256, np_=N), Qm.ap([[1, 16]], off=h * 256 + b * 16), sbf.flat(256, off=h * 256),
                                                            start=(b == 0), stop=(b == N - 1)), [Qm, sbf], [po[h]])
        P.op("dve", lambda e, b=b, kb=kb: e.tensor_scalar(kb.flat(512, np_=N), kr.flat(512, np_=N), ident.ap([[1, 1]], off=b, np_=N), None, op0=ALU.mult),
             [kr, ident], [kb])
        for j in range(2):
            pb = P.psum(1 + (2 * b + j) % 3, F32)
            for hh in range(2):
                h = 2 * j + hh
                P.op("pe", lambda e, h=h, hh=hh, pb=pb, kb=kb: e.matmul(pb.flat(256, off=hh * 256), kb.flat(128, off=h * 128, np_=N), vv.flat(256, off=h * 256, np_=N),
                                                                     start=True, stop=True), [kb, vv], [pb])
            for hh in range(2):
                h = 2 * j + hh
                P.op("dve", lambda e, h=h, hh=hh, pb=pb, sf=sf: e.scalar_tensor_tensor(out=sf.flat(256, off=h * 256), in0=sf.flat(256, off=h * 256), scalar=float(GAM[h]),
                                                                                    in1=pb.flat(256, off=hh * 256), op0=ALU.mult, op1=ALU.add), [sf, pb], [sf])
        P.dma("act", bass.AP(T["rss"], b * 4 * 128 * 256, [[256, 128], [128 * 256, 4], [1, 256]]), sf.ap([[256, 4], [1, 256]]), [sf], [d_rss])
    o32 = Sst
    for h in range(4):
        P.op("dve", lambda e, h=h: e.tensor_scalar(o32.flat(256, off=h * 256, np_=N), vv.flat(256, off=h * 256, np_=N), small.flat(1, off=8 + h, np_=N), None, op0=ALU.mult),
             [vv, small], [o32])
        P.op("dve", lambda e, h=h: e.scalar_tensor_tensor(out=o32.flat(256, off=h * 256, np_=N), in0=po[h].flat(256, np_=N), scalar=float(GAM[h]),
                                                          in1=o32.flat(256, off=h * 256, np_=N), op0=ALU.mult, op1=ALU.add), [po[h], o32], [o32])
    for h in range(4):
        P.op("act", lambda e, h=h: e.activation(out=junk.flat(256, np_=N), in_=o32.flat(256, off=h * 256, np_=N), func=AF.Square,
                                                accum_out=small2.flat(1, off=h, np_=N)), [o32], [junk, small2])
    P.op("dve", lambda e: e.tensor_scalar(small2.flat(4, np_=N), small2.flat(4, np_=N), 1.0 / 256, EPS, op0=ALU.mult, op1=ALU.add), [small2], [small2])
    P.op("pool", lambda e: e.tensor_tensor(out=small2.flat(4, np_=N), in0=small2.flat(4, np_=N), in1=mhalf.flat(4, np_=N), op=ALU.pow), [small2, mhalf], [small2])
    for h in range(4):
        P.op("dve", lambda e, h=h: e.scalar_tensor_tensor(out=tok.flat(256, off=h * 256, np_=N), in0=o32.flat(256, off=h * 256, np_=N),
                                                          scalar=small2.flat(1, off=h, np_=N), in1=GG.flat(256, off=h * 256, np_=N),
                                                          op0=ALU.mult, op1=ALU.mult), [o32, small2, GG], [tok])


def swa_pass(P, L, T, d_in, newout):
    samp, np_, ntl, hb, slots, blk = (L[k] for k in ("samp", "np_", "ntl", "hb", "slots", "blk"))
    arena, load_slab, inproj, transposes, outproj, oslabs = (L[k] for k in ("arena", "load_slab", "inproj", "transposes", "outproj", "oslabs"))
    ident, swamask, rstd1, r2, mhalf, gq4, esrep = (L[k] for k in ("ident", "swamask", "rstd1", "r2", "mhalf", "gq4", "esrep"))
    tmpX, tmpA, tmpB, small, small2, junk, tok, xT = (L[k] for k in ("tmpX", "tmpA", "tmpB", "small", "small2", "junk", "tok", "xT"))
    qT, kT, vaug, pT, gate2, kf32, vf32, stabs, swt, brs, gtile = (L[k] for k in
        ("qT", "kT", "vaug", "pT", "gate2", "kf32", "vf32", "stabs", "swt", "brs", "gtile"))
    q16 = [arena(i, 0, 1024, BF16) for i in range(ntl)]
    kdup = [arena(i, 2048, 512, BF16) for i in range(ntl)]
    gate = [arena(i, 4096, 1024, BF16) for i in range(ntl)]
    swtb = {}
    kt_todo = []

    def load_tab(tslot, key):
        r = L["swt_i"][0] % 2
        L["swt_i"][0] += 1
        sb_, w = stabs[r], swt[r]
        P.dma("sp", sb_.flat(np_=np_), bass.AP(T["stab"], tslot * 128 * 128, [[128, np_], [1, 128]]), [d_in], [sb_])
        P.op("pool", lambda e: e.tensor_tensor(out=w.ap([[128, 2], [1, 128]], np_=np_), in0=sb_.ap([[0, 2], [1, 128]], np_=np_),
                                               in1=gq4.ap([[128, 2], [1, 128]], np_=np_), op=ALU.mult), [sb_, gq4], [w])
        swtb[key] = w

    def qk_norm_rope(pb, nh, slot, w, coff, out_ap_fn, outbuf, ssbuf, soff):
        n = nh * 64
        ti = L["tmp_i"][0] % 2
        L["tmp_i"][0] += 1
        tmpX, tmpA, tmpB = L["tmpXr"][ti], L["tmpAr"][ti], L["tmpBr"][ti]
        P.op("act", lambda e: e.activation(out=tmpX.flat(n, np_=np_), in_=pb.flat(n, np_=np_), func=AF.Square), [pb], [tmpX])
        P.op("dve", lambda e: e.tensor_reduce(out=ssbuf.flat(nh, off=soff, np_=np_), in_=tmpX.ap([[64, nh], [1, 64]], np_=np_),
                                              op=ALU.add, axis=AX.X), [tmpX], [ssbuf])
        P.op("dve", lambda e: e.tensor_scalar(ssbuf.flat(nh, off=soff, np_=np_), ssbuf.flat(nh, off=soff, np_=np_),
                                              r2.flat(1, off=slot, np_=np_), EPS, op0=ALU.mult, op1=ALU.add), [ssbuf, r2], [ssbuf])
        P.op("pool", lambda e: e.tensor_tensor(out=ssbuf.flat(nh, off=soff, np_=np_), in0=ssbuf.flat(nh, off=soff, np_=np_),
                                               in1=mhalf.flat(nh, np_=np_), op=ALU.pow), [ssbuf, mhalf], [ssbuf])
        P.op("dve", lambda e: e.tensor_scalar(ssbuf.flat(nh, off=soff, np_=np_), ssbuf.flat(nh, off=soff, np_=np_),
                                              rstd1.flat(1, off=slot, np_=np_), None, op0=ALU.mult), [ssbuf, rstd1], [ssbuf])
        P.op("dve", lambda e: e.tensor_tensor(out=tmpA.ap([[64, nh], [1, 64]], np_=np_), in0=pb.ap([[64, nh], [1, 64]], np_=np_),
                                              in1=w.ap([[0, nh], [1, 64]], off=coff, np_=np_), op=ALU.mult), [pb, w], [tmpA])
        for hf in range(2):
            P.op("dve", lambda e, hf=hf: e.tensor_tensor(out=tmpB.ap([[64, nh], [1, 32]], off=hf * 32, np_=np_),
                                                         in0=pb.ap([[64, nh], [1, 32]], off=(1 - hf) * 32, np_=np_),
                                                         in1=w.ap([[0, nh], [1, 32]], off=coff + 64 + hf * 32, np_=np_), op=ALU.mult),
                 [pb, w], [tmpB])
        P.op("dve", lambda e: e.tensor_tensor(out=tmpA.flat(n, np_=np_), in0=tmpA.flat(n, np_=np_), in1=tmpB.flat(n, np_=np_), op=ALU.add),
             [tmpA, tmpB], [tmpA])
        P.op("pool", lambda e: e.tensor_tensor(out=out_ap_fn(), in0=tmpA.ap([[64, nh], [1, 64]], np_=np_),
                                               in1=ssbuf.ap([[1, nh], [0, 64]], off=soff, np_=np_), op=ALU.mult), [tmpA, ssbuf], [outbuf])

    def kv_tile(pb, slot, w, ring, kd, last):
        qk_norm_rope(pb, 4, slot, w, 128, lambda: kf32.ap([[64, 4], [1, 64]], np_=np_), kf32, L["small3"], 0)
        P.op("pool", lambda e: e.tensor_copy(kd.ap([[128, 4], [64, 2], [1, 64]], np_=np_), kf32.ap([[64, 4], [0, 2], [1, 64]], np_=np_)), [kf32], [kd])
        P.op("act", lambda e: e.activation(out=vaug[ring].ap([[66, 4], [1, 64]], np_=np_), in_=pb.ap([[64, 4], [1, 64]], off=256, np_=np_),
                                           func=AF.Copy, scale=rstd1.flat(1, off=slot, np_=np_)), [pb, rstd1], [vaug[ring]])
        if last or samp:
            P.op("act", lambda e: e.activation(out=vf32.flat(256, np_=np_), in_=pb.flat(256, off=256, np_=np_), func=AF.Copy,
                                               scale=rstd1.flat(1, off=slot, np_=np_)), [pb, rstd1], [vf32])
        if last:
            P.dma("sp", T["kpd"].ap(), kf32.flat(256), [kf32], [L["d_kp"]])
            P.dma("sp", T["vpd"].ap(), vf32.flat(256), [vf32], [L["d_vp"]])
        if not samp:
            kt_todo.append((kd, ring))

    def flush_kt():
        for kd, ring in kt_todo:
            transposes(kd, 128, n=4, width=128, dst=kT[ring])
        del kt_todo[:]

    g0 = gtile[0]

    def stage1():
        s_kv = load_slab("in", 4096)
        if blk == 0:
            load_tab(16, "m1")
            pb = inproj(s_kv, L["hTm1"], 0, 128, kst=128)
            kv_tile(pb, 31, swtb["m1"], 0, kdup[0], False)
        flush_kt()
        tab_seq = [((17 if samp else slots[i]), (u, i)) for u in ("k", "q0", "q1") for i in range(ntl)]
        tpos = [0]

        def tab_next():
            j = tpos[0]
            if j == 0:
                load_tab(*tab_seq[0])
            w = swtb[tab_seq[j][1]]
            if j + 1 < len(tab_seq):
                load_tab(*tab_seq[j + 1])
            tpos[0] += 1
            return w

        for i in range(ntl):
            w = tab_next()
            pb = inproj(s_kv, hb, i * 128, np_)
            kv_tile(pb, slots[i], w, (g0 + i + 1) % 5, kdup[i], (not samp) and slots[i] == NT - 1)
            yield
        for n in range(2):
            s = load_slab("in", 3072 + n * 512)
            for i in range(ntl):
                w = tab_next()
                pb = inproj(s, hb, i * 128, np_)
                qk_norm_rope(pb, 8, slots[i], w, 0, lambda i=i, n=n: q16[i].ap([[64, 8], [1, 64]], off=n * 512, np_=np_), q16[i], small2, n * 8)
                yield
        for n in range(2):
            s = load_slab("in", 4608 + n * 512)
            for i in range(ntl):
                pb = inproj(s, hb, i * 128, np_)
                P.op("act", lambda e, pb=pb, i=i, n=n: e.activation(out=gate[i].flat(512, off=n * 512, np_=np_), in_=pb.flat(np_=np_), func=AF.Silu,
                                                                   scale=rstd1.flat(1, off=slots[i], np_=np_)), [pb, rstd1], [gate[i]])
                yield


    def mixers():
        flush_kt()
        oslabs[(1, 0)] = load_slab("o", 1, 0)
        oslabs[(1, 1)] = load_slab("o", 1, 1)
        L["op_banks"][0] = [1, 2] if samp else [3, 4]
        pov = [P.psum(5, F32), P.psum(6, F32), P.psum(7, F32)]

        def povslot(h):
            return pov[h // 6], (h % 6) * 65

        def scores(i, g):
            var = 0 if slots[i] == 0 else 1
            cur, prev = (g0 + i + 1) % 5, (g0 + i) % 5
            pTs = pT[g % 2]
            for par in range(2):
                bank = P.psum((3 + par) if (i == 0 or g % 2 == 0) else (1 + par), F32)
                for bi, kTb in enumerate((kT[prev], kT[cur])):
                    P.op("pe", lambda e, bank=bank, bi=bi, kTb=kTb, g=g, par=par: e.matmul(
                        bank.flat(256, off=bi * 256), kTb.ap([[1, 128]], off=g * 128, p0=64 * par, np_=64),
                        qT.ap([[128, 2], [1, 128]], off=2 * g * 128, p0=64 * par, np_=64), start=True, stop=False), [kTb, qT], [bank])
                    P.op("pe", lambda e, bank=bank, bi=bi, var=var: e.matmul(
                        bank.flat(256, off=bi * 256), ident.flat(), swamask.ap([[0, 2], [1, 128]], off=var * 256 + bi * 128),
                        start=False, stop=True), [ident, swamask], [bank])
                P.op("act", lambda e, bank=bank, par=par, pTs=pTs: e.activation(out=pTs[par].flat(), in_=bank.flat(), func=AF.Exp, scale=0.125),
                     [bank], [pTs[par]])

        def pv(i, g):
            cur, prev = (g0 + i + 1) % 5, (g0 + i) % 5
            pTs = pT[g % 2]
            for par in range(2):
                for jj in range(2):
                    h = 4 * g + 2 * jj + par
                    pb, off = povslot(h)
                    P.op("pe", lambda e, pb=pb, off=off, par=par, jj=jj, pTs=pTs, g=g, prev=prev: e.matmul(
                        pb.flat(65, off=off), pTs[par].flat(128, off=jj * 128), vaug[prev].flat(65, off=g * 66), start=True, stop=False),
                        [pTs[par], vaug[prev]], [pb])
                    P.op("pe", lambda e, pb=pb, off=off, par=par, jj=jj, pTs=pTs, g=g, cur=cur: e.matmul(
                        pb.flat(65, off=off), pTs[par].flat(128, off=(2 + jj) * 128), vaug[cur].flat(65, off=g * 66), start=False, stop=True),
                        [pTs[par], vaug[cur]], [pb])

        def front(i):
            transposes(q16[i], 128, n=8, width=128, dst=qT)
            scores(i, 0)

        def mix(i):
            if samp:
                swa_mixer_sample(P, L, T, d_in, newout, q16[i], kdup[i], vaug[(g0 + i + 1) % 5], pov, povslot)
            else:
                if i == 0:
                    front(0)
                    yield
                for g in range(4):
                    if g + 1 < 4:
                        scores(i, g + 1)
                    elif i + 1 < ntl:
                        front(i + 1)
                    yield
                    pv(i, g)
                    yield
            for bnk in range(3):
                nh = 6 if bnk < 2 else 4
                P.op("dve", lambda e, bnk=bnk, nh=nh: e.tensor_tensor(out=small2.flat(nh, off=16 + bnk * 6, np_=np_), in0=pov[bnk].ap([[65, nh]], off=64, np_=np_),
                                                                     in1=esrep.flat(nh, off=bnk * 6, np_=np_), op=ALU.add), [pov[bnk], esrep], [small2])
            P.op("dve", lambda e: e.reciprocal(out=small2.flat(16, off=16, np_=np_), in_=small2.flat(16, off=16, np_=np_)), [small2], [small2])
            P.op("pool", lambda e, i=i: e.tensor_tensor(out=gate2.ap([[64, 16], [1, 64]], np_=np_), in0=gate[i].ap([[64, 16], [1, 64]], np_=np_),
                                                        in1=small2.ap([[1, 16], [0, 64]], off=16, np_=np_), op=ALU.mult), [gate[i], small2], [gate2])
            for bnk in range(3):
                nh = 6 if bnk < 2 else 4
                P.op("dve", lambda e, bnk=bnk, nh=nh: e.tensor_tensor(out=tok.ap([[64, nh], [1, 64]], off=bnk * 384, np_=np_),
                                                                     in0=pov[bnk].ap([[65, nh], [1, 64]], np_=np_),
                                                                     in1=gate2.ap([[64, nh], [1, 64]], off=bnk * 384, np_=np_), op=ALU.mult),
                     [pov[bnk], gate2], [tok])
            if T["dbg"] and blk == 0:
                P.dma("sp", bass.AP(T["dbg3"], i * 128 * 1024, [[1024, 128], [1, 1024]]), tok.flat(), [tok], [newout()])
            yield
            transposes(tok, np_)
            yield

            def ev(n, pb, i=i):
                P.op("act", lambda e: e.activation(out=brs[i].flat(512, off=n * 512, np_=np_), in_=pb.flat(np_=np_), func=AF.Copy), [pb], [brs[i]])
            outproj(1, xT, np_, ev)
            yield
        for i in range(ntl):
            yield from mix(i)
        gtile[0] += ntl


    return stage1(), mixers()


def sample_cache_copy(P, T, d_in, newout):
    for src_t, dst_t in ((T["ckd"], T["ksd"]), (T["cvd"], T["vsd"])):
        for hb_ in range(2):
            d1 = newout()
            o = hb_ * 8 * 128 * 256
            P.dma("sp", bass.AP(dst_t, o, [[128 * 256, 8], [1, 127 * 256]]), bass.AP(src_t, o + 256, [[128 * 256, 8], [1, 127 * 256]]), [d_in], [d1])


def sample_prefetch(P, L, T, d_in):
    Kc, Vc = L["Kc"], L["Vc"]
    ckd, cvd = T["ckd"], T["cvd"]
    N = 16
    P.op("pool", lambda e: e.memset(Vc.flat(), 1.0), [], [Vc])
    thr = [P.dram(None) for _ in range(6)]
    j = 0
    for p0, npp in ((0, 64), (64, 63)):
        k = P.dram(None)
        L["kc_keys"].append(k)
        P.dma("pool", Kc.ap([[256, 16], [1, 256]], p0=p0, np_=npp), bass.AP(ckd, 256 * (1 + p0), [[256, npp], [128 * 256, 16], [1, 256]]),
              [d_in, Kc], [k, thr[j % 6]])
        j += 1
        for b in range(N):
            k = P.dram(None)
            L["vc_keys"].append(k)
            P.dma("pool", Vc.ap([[66, 4], [1, 64]], off=b * 264, p0=p0, np_=npp),
                  bass.AP(cvd, b * 128 * 256 + 256 * (1 + p0), [[256, npp], [64, 4], [1, 64]]), [d_in, Vc], [k, thr[j % 6]])
            j += 1


def swa_mixer_sample(P, L, T, d_in, newout, q16, kd, vaug_s, pov, povslot):
    ident, kf32, vf32, eye16, Kc, Vc, KTr, pTs, Pm, qT, transposes = (L[k] for k in
        ("ident", "kf32", "vf32", "eye16", "Kc", "Vc", "KTr", "pTs", "Pm", "qT", "transposes"))
    ckd, cvd = T["ckd"], T["cvd"]
    N = 16
    for dst_t, newrow in ((T["ksd"], kf32), (T["vsd"], vf32)):
        d2 = newout()
        P.dma("sp", bass.AP(dst_t, 127 * 256, [[128 * 256, 16], [1, 256]]), newrow.flat(256, np_=N), [newrow], [d2])
    P.dma("sp", Kc.ap([[256, 16], [64, 4], [1, 64]], p0=127, np_=1), kd.ap([[128, 4], [1, 64]], np_=N), [kd, Kc] + L["kc_keys"], [Kc])
    P.dma("sp", Vc.ap([[264, 16], [66, 4], [1, 64]], p0=127, np_=1), vaug_s.ap([[66, 4], [1, 64]], np_=N), [vaug_s, Vc] + L["vc_keys"], [Vc])
    transposes(q16, N, n=16, width=64, dst=qT)
    sc = P.psum(3, F32)
    for bp in range(N // 2):
        ktr = KTr[bp % 2]
        pt = P.psum(0, BF16)
        for j in range(8):
            b, g = 2 * bp + j // 4, j % 4
            P.op("pe", lambda e, j=j, b=b, g=g, pt=pt: e.transpose(pt.ap([[1, 128]], off=j * 128, np_=64), Kc.ap([[1, 64]], off=b * 256 + g * 64), ident.flat()),
                 [Kc, ident], [pt])
        P.op("act", lambda e, pt=pt, ktr=ktr: e.activation(out=ktr.flat(1024, np_=64), in_=pt.flat(1024, np_=64), func=AF.Copy), [pt], [ktr])
        for j in range(8):
            b, g = 2 * bp + j // 4, j % 4
            P.op("pe", lambda e, j=j, b=b, g=g, ktr=ktr: e.matmul(sc.ap([[1, 4]], off=b * 16 + g * 4), ktr.ap([[1, 128]], off=j * 128, np_=64),
                                                                 qT.ap([[16, 4]], off=4 * g * 16 + b, np_=64), start=True, stop=True), [ktr, qT], [sc])
    P.op("act", lambda e: e.activation(out=pTs.flat(256), in_=sc.flat(256), func=AF.Exp, scale=0.125), [sc], [pTs])
    for h in range(16):
        P.op("dve", lambda e, h=h: e.tensor_tensor(out=Pm.ap([[16, 16], [1, 16]], off=h * 256), in0=pTs.ap([[0, 16], [16, 16]], off=h),
                                                   in1=eye16.ap([[16, 16], [1, 16]]), op=ALU.mult), [pTs, eye16], [Pm])
    for h in range(16):
        g = h // 4
        pb, off = povslot(h)
        for b in range(N):
            P.op("pe", lambda e, h=h, g=g, b=b, pb=pb, off=off: e.matmul(pb.flat(65, off=off, np_=N), Pm.ap([[1, 16]], off=h * 256 + b * 16),
                                                                        Vc.ap([[1, 65]], off=b * 264 + g * 66), start=(b == 0), stop=(b == N - 1)),
                 [Pm, Vc], [pb])


def _tables(half):
    f64 = np.float64
    T = np.arange(2048, dtype=f64)
    lg = np.log(np.array(GAM, dtype=f64))
    inv_r = 10000.0 ** (-np.arange(0, 128, 2, dtype=f64) / 128)
    inv_s = 10000.0 ** (-np.arange(0, 64, 2, dtype=f64) / 64)

    def cs2(pos, inv):
        ang = pos[:, None] * inv[None, :]
        c, s_ = np.cos(ang), np.sin(ang)
        return np.concatenate([c, c], 1), np.concatenate([-s_, s_], 1)

    pos_main = half * 2048 + T
    c2, s2 = cs2(pos_main, inv_r)
    aq = np.exp((T[:, None] + 1) * lg[None, :])
    ak = np.exp(-(T[:, None] + 1) * lg[None, :]) / np.sqrt(128.0)
    rt = np.stack([c2[:, None, :] * aq[:, :, None], s2[:, None, :] * aq[:, :, None],
                   c2[:, None, :] * ak[:, :, None], s2[:, None, :] * ak[:, :, None]], 1)
    rtab_main = rt.reshape(16, 128, 2048).astype(np.float32)
    c2p, s2p = cs2(T, inv_r)
    akp = np.exp((2047 - T[:, None]) * lg[None, :]) / np.sqrt(128.0)
    rtp = np.stack([c2p[:, None, :] * akp[:, :, None], s2p[:, None, :] * akp[:, :, None]], 1)
    rtab_pre = rtp.reshape(16, 128, 1024).astype(np.float32)
    c2s, s2s = cs2(np.full(16, 8192.0), inv_r)
    one = np.ones((16, 4, 1))
    rts = np.stack([c2s[:, None, :] * one, s2s[:, None, :] * one, c2s[:, None, :] * one / np.sqrt(128.0),
                    s2s[:, None, :] * one / np.sqrt(128.0)], 1)
    rtab_samp = rts.reshape(16, 2048).astype(np.float32)
    stab = np.zeros((18, 128, 128), np.float32)
    cm, sm = cs2(pos_main, inv_s)
    stab[:16] = np.concatenate([cm, sm], 1).reshape(16, 128, 128)
    cp, sp_ = cs2(1920 + np.arange(128, dtype=f64), inv_s)
    stab[16] = np.concatenate([cp, sp_], 1)
    cs_, ss_ = cs2(np.full(128, 8192.0), inv_s)
    stab[17] = np.concatenate([cs_, ss_], 1)
    k = np.arange(128)[:, None]
    q = np.arange(128)[None, :]
    cmask = (q >= k).astype(np.float32)
    prev = np.where(k > q, 0.0, MASKNEG)
    cur = np.where(k <= q, 0.0, MASKNEG)
    first_prev = prev if half == 1 else np.full((128, 128), MASKNEG)
    swamask = np.stack([first_prev, cur, prev, cur], 1).reshape(128, 512)
    swamask = np.concatenate([swamask, np.zeros((128, 512))], 1).astype(np.float32)
    eye16 = np.tile(np.eye(16, dtype=np.float32).reshape(1, 256), (128, 1))
    return dict(rtab_main=rtab_main, rtab_pre=rtab_pre, rtab_samp=rtab_samp, stab=stab, cmask=cmask,
                swamask=swamask, eye16=eye16, ident=np.eye(128, dtype=np.float32))


_CACHE = {}


def kernel(x_prompt, x_sample, state_ret, cache_swa_k, cache_swa_v, norm_g, w_in, ret_norm_g,
           swa_q_g, swa_k_g, swa_sinks, w_br_ret, w_br_swa, w_out, _with_sample=True, _dbg=False):
    f = lambda a: np.ascontiguousarray(np.asarray(a, dtype=np.float32))
    x_prompt, x_sample, state_ret, cache_swa_k, cache_swa_v = map(f, (x_prompt, x_sample, state_ret, cache_swa_k, cache_swa_v))
    w_in_ = f(w_in)[0]
    w_o3 = np.ascontiguousarray(np.stack([f(w_br_ret)[0], f(w_br_swa)[0], f(w_out)[0]], 0))
    gq, gk = f(swa_q_g)[0], f(swa_k_g)[0]
    sw = lambda g: np.concatenate([g[32:], g[:32]])
    gqk = np.ascontiguousarray(np.stack([gq, sw(gq), gk, sw(gk)], 0))
    ng = np.ascontiguousarray(f(norm_g)[0].reshape(8, 128).T)
    if "nc" not in _CACHE:
        _CACHE["nc"] = build_program(with_sample=_with_sample, dbg=_dbg)[0]
        _CACHE["tabs"] = [_tables(0), _tables(1)]
    nc = _CACHE["nc"]
    in_maps = []
    for c in range(8):
        b, half = c // 2, c % 2
        m = dict(_CACHE["tabs"][half])
        m["xm"] = np.ascontiguousarray(x_prompt[b, half * 2048:(half + 1) * 2048])
        m["xp"] = np.ascontiguousarray(x_prompt[b, 0:2048]) if half == 1 else np.zeros((2048, D), np.float32)
        m["xs"] = np.ascontiguousarray(x_sample[16 * c:16 * c + 16, 0])
        m["w_in"] = w_in_
        m["w_o3"] = w_o3
        m["norm_g"] = ng
        m["ret_g"] = np.ascontiguousarray(f(ret_norm_g)[0].reshape(8, 128).T)
        m["gqk"] = gqk
        m["sinks"] = np.ascontiguousarray(f(swa_sinks)[0])
        m["state"] = np.ascontiguousarray(state_ret[0, 16 * c:16 * c + 16])
        m["ck"] = np.ascontiguousarray(cache_swa_k[0, 16 * c:16 * c + 16].reshape(16, 128, 256))
        m["cv"] = np.ascontiguousarray(cache_swa_v[0, 16 * c:16 * c + 16].reshape(16, 128, 256))
        in_maps.append(m)
    res = run_bass_kernel_spmd(nc, in_maps, core_ids=list(range(8)))
    R = res.results
    if _dbg:
        _CACHE["dbg"] = {k: np.asarray(R[0][k]).astype(np.float32) for k in ("dbg1", "dbg2", "dbg3")}
    yp = np.zeros((4, 4096, D), np.float32)
    ys = np.zeros((128, 1, D), np.float32)
    rsp = np.zeros((1, 4, 4, 128, 256), np.float32)
    rss = np.zeros((1, 128, 4, 128, 256), np.float32)
    kp = np.zeros((1, 4, 128, 4, 64), np.float32)
    vp = np.zeros((1, 4, 128, 4, 64), np.float32)
    ks = np.zeros((1, 128, 128, 4, 64), np.float32)
    vs = np.zeros((1, 128, 128, 4, 64), np.float32)
    for c in range(8):
        b, half = c // 2, c % 2
        r = R[c]
        yp[b, half * 2048:(half + 1) * 2048] = r["yp"]
        ys[16 * c:16 * c + 16, 0] = r["ys"]
        rss[0, 16 * c:16 * c + 16] = r["rss"]
        ks[0, 16 * c:16 * c + 16] = r["ks"].reshape(16, 128, 4, 64)
        vs[0, 16 * c:16 * c + 16] = r["vs"].reshape(16, 128, 4, 64)
        if half == 1:
            rsp[0, b] = r["rsp"]
            kp[0, b] = r["kp"].reshape(128, 4, 64)
            vp[0, b] = r["vp"].reshape(128, 4, 64)
    return yp, ys, rsp, rss, kp, vp, ks, vs
```

```python
import numpy as np
import concourse.bass as bass
import concourse.mybir as mybir
from concourse.bass_utils import run_bass_kernel_spmd

F32 = mybir.dt.float32
BF16 = mybir.dt.bfloat16
AF = mybir.ActivationFunctionType
ALU = mybir.AluOpType
AX = mybir.AxisListType

GRAN = 512


class Buf:
    def __init__(self, tensor, off, n, rowlen, keys, esz):
        self.tensor = tensor
        self.off = off
        self.n = n
        self.rowlen = rowlen
        self.keys = keys
        self.esz = esz

    def ap(self, dims, off=0, p0=0, np_=128):
        return bass.AP(self.tensor, p0 * self.rowlen + self.off + off,
                       [[self.rowlen, np_]] + [list(d) for d in dims])

    def flat(self, n=None, off=0, p0=0, np_=128):
        return self.ap([[1, self.n - off if n is None else n]], off=off, p0=p0, np_=np_)


class Prog:
    def __init__(self, nc, arena_bytes, n_dma_sems=52):
        self.nc = nc
        self.ops = []
        self.last_w = {}
        self.readers = {}
        self.arena_bytes = arena_bytes
        self.arena_top = 0
        self.n_dma_sems = n_dma_sems
        self.dram_key = 0
        self.t16 = None
        self.ps = []

    def setup_mem(self, t16, ps_tensors):
        self.t16 = t16
        self.t32 = t16.bitcast(F32)
        self.ps = [(p, p.bitcast(BF16)) for p in ps_tensors]

    def alloc_off(self, nbytes):
        off = self.arena_top
        self.arena_top = (off + nbytes + GRAN - 1) // GRAN * GRAN
        assert self.arena_top <= self.arena_bytes, (self.arena_top, self.arena_bytes)
        return off

    def sb_at(self, boff, n, dtype):
        esz = 4 if dtype == F32 else 2
        t = self.t32 if dtype == F32 else self.t16
        assert boff % esz == 0
        keys = frozenset(("sb", g) for g in range(boff // GRAN, (boff + n * esz - 1) // GRAN + 1))
        return Buf(t, boff // esz, n, self.arena_bytes // esz, keys, esz)

    def sb(self, n, dtype):
        esz = 4 if dtype == F32 else 2
        return self.sb_at(self.alloc_off(n * esz), n, dtype)

    def psum(self, bank, dtype=F32, off=0, n=None):
        t = self.ps[bank][0 if dtype == F32 else 1]
        full = 512 if dtype == F32 else 1024
        if n is None:
            n = full - off
        return Buf(t, off, n, full, frozenset([("ps", bank)]), 4 if dtype == F32 else 2)

    def dram(self, tensor, esz=4):
        self.dram_key += 1
        return Buf(tensor, 0, 0, 0, frozenset([("dr", self.dram_key)]), esz)

    def op(self, eng, fn, reads=(), writes=(), dma=False):
        idx = len(self.ops)
        deps = set()
        rk = set()
        for b in reads:
            rk |= b.keys
        wk = set()
        for b in writes:
            wk |= b.keys
        for k in rk:
            w = self.last_w.get(k)
            if w is not None:
                deps.add(w)
        for k in wk:
            w = self.last_w.get(k)
            if w is not None:
                deps.add(w)
            for r in self.readers.get(k, ()):
                deps.add(r)
        for k in rk:
            lst = self.readers.setdefault(k, [])
            if not dma:
                lst[:] = [r for r in lst if self.ops[r]["dma"] or self.ops[r]["eng"] != eng]
            lst.append(idx)
        for k in wk:
            self.readers[k] = []
            self.last_w[k] = idx
        deps.discard(idx)
        self.ops.append(dict(eng=eng, fn=fn, deps=deps, dma=dma))
        return idx

    def dma(self, queue, out_ap, in_ap, reads, writes, **kw):
        def fn(e):
            return e.dma_start(out=out_ap, in_=in_ap, **kw)
        return self.op(queue, fn, reads, writes, dma=True)

    def emit(self):
        nc = self.nc
        ops = self.ops
        engs = ["pe", "act", "dve", "pool", "sp"]
        eng_obj = dict(pe=nc.tensor, act=nc.scalar, dve=nc.vector, pool=nc.gpsimd, sp=nc.sync)
        sig = [False] * len(ops)
        for i, o in enumerate(ops):
            nd = set()
            for d in o["deps"]:
                od = ops[d]
                if (not od["dma"]) and od["eng"] == "pe" and o["eng"] == "pe" and not o["dma"]:
                    continue
                nd.add(d)
                sig[d] = True
            o["deps"] = nd
        cnt = {e: 0 for e in engs}
        dma_n = 0
        n_sw = 12
        n_hw = self.n_dma_sems - n_sw
        qn = {"pool": 0, "hw": 0}
        for i, o in enumerate(ops):
            if o["dma"]:
                if o["eng"] == "pool":
                    o["dsem"] = qn["pool"] % n_sw
                    o["duse"] = qn["pool"] // n_sw + 1
                    qn["pool"] += 1
                else:
                    o["dsem"] = n_sw + qn["hw"] % n_hw
                    o["duse"] = qn["hw"] // n_hw + 1
                    qn["hw"] += 1
                dma_n += 1
            elif sig[i]:
                cnt[o["eng"]] += 1
                o["sigval"] = cnt[o["eng"]]
        self.stats = dict(cnt=dict(cnt), dma=dma_n, nops=len(ops))

        import contextlib
        with contextlib.ExitStack() as st:
            esem = {e: st.enter_context(nc.semaphore("s_" + e)) for e in engs}
            dsem = [st.enter_context(nc.semaphore("d%d" % i)) for i in range(self.n_dma_sems)]
            block = st.enter_context(nc.Block())
            per_eng = {e: [i for i, o in enumerate(ops) if o["eng"] == e] for e in engs}

            def run(e, eng):
                seen = {}
                for i in per_eng[e]:
                    o = ops[i]
                    need = {}
                    for d in o["deps"]:
                        od = ops[d]
                        if od["dma"]:
                            key = ("d", od["dsem"])
                            val = 16 * od["duse"]
                        else:
                            key = ("e", od["eng"])
                            val = od["sigval"]
                        if need.get(key, 0) < val:
                            need[key] = val
                    if o["dma"] and o["duse"] > 1:
                        key = ("d", o["dsem"])
                        val = 16 * (o["duse"] - 1)
                        if need.get(key, 0) < val:
                            need[key] = val
                    for key, val in need.items():
                        if seen.get(key, 0) >= val:
                            continue
                        seen[key] = val
                        s = dsem[key[1]] if key[0] == "d" else esem[key[1]]
                        eng.wait_ge(s, val)
                    if o["fn"] is None:
                        continue
                    ins = o["fn"](eng)
                    if o["dma"]:
                        ins.then_inc(dsem[o["dsem"]], 16)
                    elif sig[i]:
                        ins.then_inc(esem[e], 1)

            @block.tensor
            def _(eng):
                run("pe", eng)

            @block.scalar
            def _(eng):
                run("act", eng)

            @block.vector
            def _(eng):
                run("dve", eng)

            @block.gpsimd
            def _(eng):
                run("pool", eng)

            @block.sync
            def _(eng):
                run("sp", eng)


import contextlib
import ml_dtypes

D = 1024
NT = 16
BLK = 4
EPS = 1e-6
GAM = [1.0 - 2.0 ** (-5.0 - h) for h in range(4)]
COLS = dict(rq=0, rk=512, rv=1024, rg=2048, sq=3072, skv=4096, sg=4608, mr=5632, ms=6656)
MASKNEG = -30000.0


def build_program(with_sample=True, dbg=False):
    nc = bass.Bass("TRN2", target_bir_lowering=False)

    def din(name, shape, dt=F32):
        return nc.dram_tensor(name, shape, dt, kind="ExternalInput")

    def dout(name, shape, dt=F32):
        return nc.dram_tensor(name, shape, dt, kind="ExternalOutput")

    xm = din("xm", [2048, D]); xp = din("xp", [2048, D]); xsd = din("xs", [16, D])
    w_in = din("w_in", [D, 7680]); w_o3 = din("w_o3", [3, D, D])
    norm_g = din("norm_g", [128, 8]); ret_g = din("ret_g", [128, 8])
    gqk = din("gqk", [4, 64]); sinks = din("sinks", [16])
    state = din("state", [16, 4, 128, 256]); ckd = din("ck", [16, 128, 256]); cvd = din("cv", [16, 128, 256])
    rtab_main = din("rtab_main", [16, 128, 2048]); rtab_pre = din("rtab_pre", [16, 128, 1024])
    rtab_samp = din("rtab_samp", [16, 2048]); stab = din("stab", [18, 128, 128])
    identd = din("ident", [128, 128]); cmaskd = din("cmask", [128, 128]); swamaskd = din("swamask", [128, 1024])
    eyed = din("eye16", [128, 256])

    yp = dout("yp", [2048, D]); ysd = dout("ys", [16, D]); rsp = dout("rsp", [4, 128, 256])
    rss = dout("rss", [16, 4, 128, 256]); kpd = dout("kp", [128, 256]); vpd = dout("vp", [128, 256])
    ksd = dout("ks", [16, 128, 256]); vsd = dout("vs", [16, 128, 256])

    dbg1 = dout("dbg1", [4, 128, 1024], BF16) if dbg else None
    dbg2 = dout("dbg2", [4, 128, 1024], BF16) if dbg else None
    dbg3 = dout("dbg3", [4, 128, 1024], BF16) if dbg else None
    w_in_bf = nc.dram_tensor("w_in_bf", [15, 128, 4096], BF16, kind="Internal")
    w_o3_bf = nc.dram_tensor("w_o3_bf", [6, 128, 4096], BF16, kind="Internal")

    ARENA = 204 * 1024
    with contextlib.ExitStack() as st:
        t16 = st.enter_context(nc.sbuf_tensor("arena", [128, ARENA // 2], BF16))
        pst = [st.enter_context(nc.psum_tensor("ps%d" % i, [128, 512], F32)) for i in range(8)]
        P = Prog(nc, ARENA)
        P.setup_mem(t16, pst)
        _build(nc, P, locals(), with_sample)
        P.emit()
    return nc, P


def _build(nc, P, T, with_sample):
    xm, xp, xsd, w_in, w_o3 = T["xm"], T["xp"], T["xsd"], T["w_in"], T["w_o3"]
    w_in_bf, w_o3_bf = T["w_in_bf"], T["w_o3_bf"]
    d_in = P.dram(None)
    d_wslab = {}
    for c0 in range(0, 7680, 512):
        d_wslab[("in", c0)] = P.dram(None)
    for m in range(3):
        for n in range(2):
            d_wslab[("o", m, n)] = P.dram(None)
    d_outs = []

    def newout():
        b = P.dram(None)
        d_outs.append(b)
        return b

    ident = P.sb(128, BF16); cmask = P.sb(128, F32); swamask = P.sb(1024, BF16)
    gcol = P.sb(8, F32); gretcol = P.sb(8, F32); gq4 = P.sb(256, F32); esrep = P.sb(16, F32)
    rstd1 = P.sb(40, F32); r2 = P.sb(40, F32); mhalf = P.sb(16, F32); rh = P.sb(40, F32)
    P.dma("pool", ident.flat(), T["identd"].ap(), [d_in], [ident])
    P.dma("pool", swamask.flat(), T["swamaskd"].ap(), [d_in], [swamask])
    P.dma("sp", cmask.flat(), T["cmaskd"].ap(), [d_in], [cmask])
    P.dma("sp", gcol.flat(8), T["norm_g"].ap(), [d_in], [gcol])
    P.dma("sp", gretcol.flat(8), T["ret_g"].ap(), [d_in], [gretcol])
    P.dma("sp", gq4.flat(), bass.AP(T["gqk"], 0, [[0, 128], [1, 256]]), [d_in], [gq4])
    P.dma("sp", esrep.flat(), bass.AP(T["sinks"], 0, [[0, 128], [1, 16]]), [d_in], [esrep])
    P.op("act", lambda e: e.activation(out=esrep.flat(), in_=esrep.flat(), func=AF.Exp), [esrep], [esrep])
    P.op("dve", lambda e: e.memset(mhalf.flat(), -0.5), [], [mhalf])

    NSLAB = 4
    slabs = [P.sb(8 * 512, BF16) for _ in range(NSLAB)]
    slab_i = [0]
    pre_slabs = []
    for j, c0 in enumerate((512, 1024, 1536)):
        sl = slabs[j]
        for kh in range(2):
            half = P.sb_at(sl.off * 2 + kh * 4096, 2048, BF16)
            P.dma("pool", half.ap([[512, 4], [1, 512]]), bass.AP(w_in, kh * 512 * 7680 + c0, [[7680, 128], [128 * 7680, 4], [1, 512]]),
                  [d_in], [half])
        pre_slabs.append(sl)
    slab_i[0] = 0

    thr = [P.dram(None) for _ in range(3)]
    thr_i = [0]

    def conv_in(c0):
        tk = thr[thr_i[0] % 3]
        thr_i[0] += 1
        P.dma("pool", bass.AP(w_in_bf, (c0 // 512) * 128 * 4096, [[4096, 128], [512, 8], [1, 512]]),
              bass.AP(w_in, c0, [[7680, 128], [128 * 7680, 8], [1, 512]]), [d_in], [d_wslab[("in", c0)], tk])

    def conv_o(m, n):
        tk = thr[thr_i[0] % 3]
        thr_i[0] += 1
        P.dma("pool", bass.AP(w_o3_bf, (2 * m + n) * 128 * 4096, [[4096, 128], [512, 8], [1, 512]]),
              bass.AP(w_o3, m * D * D + n * 512, [[D, 128], [128 * D, 8], [1, 512]]), [d_in], [d_wslab[("o", m, n)], tk])

    conv_q = []
    for c0 in (0, 512, 1024, 1536, 2048, 2560):
        conv_q.append(lambda c0=c0: conv_in(c0))
    conv_q.append(lambda: conv_o(0, 0)); conv_q.append(lambda: conv_o(0, 1))
    for c0 in (4096, 3072, 3584, 4608, 5120):
        conv_q.append(lambda c0=c0: conv_in(c0))
    conv_q.append(lambda: conv_o(1, 0)); conv_q.append(lambda: conv_o(1, 1))
    for c0 in range(5632, 7680, 512):
        conv_q.append(lambda c0=c0: conv_in(c0))
    conv_q.append(lambda: conv_o(2, 0)); conv_q.append(lambda: conv_o(2, 1))

    def emit_convs(n):
        for _ in range(n):
            if conv_q:
                conv_q.pop(0)()

    xbuf = [P.sb(1024, F32) for _ in range(2)]
    xb16 = P.sb(1024, BF16); junk = P.sb(1024, BF16)
    hT = [P.sb(8 * 512, BF16) for _ in range(2)]
    hTm1 = P.sb(8 * 128, BF16)
    arena0 = P.alloc_off(BLK * 6144)
    arena1 = P.alloc_off(BLK * 6144)
    ybuf = [P.sb(1024, F32) for _ in range(2)]
    brr = [P.sb(1024, BF16)]
    brs = [P.sb(1024, BF16)]
    br_rest0 = P.arena_top
    brr += [P.sb(1024, BF16) for _ in range(BLK - 1)]
    brs += [P.sb(1024, BF16) for _ in range(BLK - 1)]
    eye16 = P.sb(256, F32)
    pTs = P.sb(256, BF16)
    Pm = P.sb_at(br_rest0, 4096, BF16)
    Sbf2 = [P.sb_at(br_rest0 + 8192 + j * 2048, 1024, BF16) for j in range(2)]
    Kc = P.sb_at(arena0 + 6144, 4096, BF16)
    Vc = P.sb_at(arena0 + 6144 + 8192, 16 * 264, BF16)
    rtab = [P.sb(1024, F32) for _ in range(2)]
    rtab_i = [0]
    stabs = [P.sb(128, F32) for _ in range(2)]
    swt = [P.sb(256, F32) for _ in range(2)]
    tmpXr = [P.sb(512, F32) for _ in range(2)]; tmpAr = [P.sb(512, F32) for _ in range(2)]; tmpBr = [P.sb(512, F32) for _ in range(2)]
    tmpX, tmpA, tmpB = tmpXr[0], tmpAr[0], tmpBr[0]
    tmp_i = [0]
    xT = P.sb(1024, BF16)
    tok = P.sb(1024, BF16)
    Sst = P.sb(1024, F32); Sbf = P.sb(1024, BF16)
    scT = P.sb(512, BF16); small = P.sb(64, F32); small2 = P.sb(64, F32); small3 = P.sb(64, F32)
    qkT2 = [P.sb(1024, BF16) for _ in range(2)]
    scT2 = [scT, P.sb(512, BF16)]
    qT = qkT2[0]
    Qm = qkT2[1]
    Kb = scT2
    kT = [P.sb(512, BF16) for _ in range(5)]
    Sf32 = [P.sb_at(tmpXr[0].off * 4 + j * 4096, 1024, F32) for j in range(2)]
    Sf32 += [P.sb_at(hT[1].off * 2 + j * 4096, 1024, F32) for j in range(2)]
    Sbf2 += [P.sb_at(kT[0].off * 2 + j * 2048, 1024, BF16) for j in range(2)]
    vaug = [P.sb(4 * 66, BF16) for _ in range(5)]
    pT = [[P.sb(512, BF16) for _ in range(2)] for _ in range(2)]
    KTr = [P.sb_at(pT[j][0].off * 2, 1024, BF16) for j in range(2)]
    gate2 = P.sb(1024, BF16)
    kf32 = P.sb(256, F32); vf32 = P.sb(256, F32)
    sg16 = [P.sb(512, BF16) for _ in range(2)]
    for v in vaug:
        P.op("dve", lambda e, v=v: e.memset(v.flat(), 1.0), [], [v])

    def arena_of(which):
        base = arena0 if which == 0 else arena1

        def arena(i, boff, n, dt):
            return P.sb_at(base + i * 6144 + boff, n, dt)
        return arena

    arena = arena_of(0)

    B_TP = 0
    mm_banks = [1, 2]
    mm_i = [0]

    wide_banks = [1, 2]
    wide_i = [0]

    def next_mm(wide=False):
        if wide:
            b = wide_banks[wide_i[0] % len(wide_banks)]
            wide_i[0] += 1
            return b
        b = mm_banks[mm_i[0] % len(mm_banks)]
        mm_i[0] += 1
        return b

    oslab_i = [0]
    no_prefetch = [False]

    BLOCK_SEQ = [0, 512, 1024, 1536, 2048, 2560, 4096, 3072, 3584, 4608, 5120, 5632, 6144, 6656, 7168]
    slab_seq = []
    slab_pos = [0]
    slab_pending = {}

    def issue_in_slab(pos):
        c0 = slab_seq[pos]
        s = slabs[pos % 2]
        src = bass.AP(w_in_bf, (c0 // 512) * 128 * 4096, [[4096, 128], [1, 4096]])
        P.dma("sp", s.flat(4096), src, [d_wslab[("in", c0)]], [s])
        slab_pending[pos] = s

    def load_slab(kind, *a):
        if kind == "in":
            pos = slab_pos[0]
            assert slab_seq[pos] == a[0], (pos, slab_seq[pos], a[0])
            if pos not in slab_pending:
                issue_in_slab(pos)
            s = slab_pending.pop(pos)
            slab_pos[0] += 1
            if pos + 1 < len(slab_seq) and not (no_prefetch[0]):
                issue_in_slab(pos + 1)
            return s
        else:
            m, n = a
            s = slabs[2 + n]
            src = bass.AP(w_o3_bf, (2 * m + n) * 128 * 4096, [[4096, 128], [1, 4096]])
            key = d_wslab[("o", m, n)]
        P.dma("pool", s.flat(4096), src, [key], [s])
        return s

    def rstd_pow(col_buf, col, np_):
        P.op("pool", lambda e: e.tensor_tensor(out=col_buf.flat(1, off=col, np_=np_), in0=col_buf.flat(1, off=col, np_=np_),
                                               in1=mhalf.flat(1, np_=np_), op=ALU.pow), [col_buf, mhalf], [col_buf])

    def xload(src_t, row0, np_, xslot):
        xb_ = xbuf[xslot]
        P.dma("sp", xb_.flat(np_=np_), bass.AP(src_t, row0 * D, [[D, np_], [1, D]]), [d_in], [xb_])

    def stage0(src_t, row0, np_, slot, hbuf, hcol, xslot, preloaded=False):
        xb_ = xbuf[xslot]
        if not preloaded:
            P.dma("sp", xb_.flat(np_=np_), bass.AP(src_t, row0 * D, [[D, np_], [1, D]]), [d_in], [xb_])
        P.op("act", lambda e: e.activation(out=junk.flat(np_=np_), in_=xb_.flat(np_=np_), func=AF.Square,
                                           accum_out=rstd1.flat(1, off=slot, np_=np_)), [xb_], [junk, rstd1])
        P.op("dve", lambda e: e.tensor_scalar(rstd1.flat(1, off=slot, np_=np_), rstd1.flat(1, off=slot, np_=np_), 1.0 / D, EPS,
                                              op0=ALU.mult, op1=ALU.add), [rstd1], [rstd1])
        rstd_pow(rstd1, slot, np_)
        P.op("dve", lambda e: e.scalar_tensor_tensor(out=r2.flat(1, off=slot, np_=np_), in0=rstd1.flat(1, off=slot, np_=np_),
                                                     scalar=1.0 / 64, in1=rstd1.flat(1, off=slot, np_=np_),
                                                     op0=ALU.mult, op1=ALU.mult), [rstd1], [r2])
        P.op("dve", lambda e: e.tensor_scalar(rh.flat(1, off=slot, np_=np_), rstd1.flat(1, off=slot, np_=np_), 0.5, None, op0=ALU.mult), [rstd1], [rh])
        P.op("dve", lambda e: e.tensor_copy(xb16.flat(np_=np_), xb_.flat(np_=np_)), [xb_], [xb16])
        pt = P.psum(B_TP, BF16)
        for j in range(8):
            P.op("pe", lambda e, j=j: e.transpose(pt.flat(np_, off=j * np_), xb16.flat(128, off=j * 128, np_=np_),
                                                  ident.ap([[1, np_]], np_=np_)), [xb16, ident], [pt])
        P.op("dve", lambda e: e.tensor_tensor(out=hbuf.ap([[512, 8], [1, np_]], off=hcol), in0=pt.ap([[np_, 8], [1, np_]]),
                                              in1=gcol.ap([[1, 8], [0, np_]]), op=ALU.mult), [pt, gcol], [hbuf])

    inproj_wide = [True]

    def inproj(slab, hbuf, hcol, np_, kst=512):
        pb = P.psum(next_mm(wide=inproj_wide[0]), F32)
        for k in range(8):
            P.op("pe", lambda e, k=k: e.matmul(pb.flat(512, np_=np_), hbuf.ap([[1, np_]], off=k * kst + hcol),
                                               slab.flat(512, off=k * 512), start=(k == 0), stop=(k == 7)),
                 [hbuf, slab], [pb])
        return pb

    def ret_rope(pb, slot, tabC, tabS, out_bf, np_):
        A = tmpAr[tmp_i[0] % 2]; B = tmpBr[tmp_i[0] % 2]
        tmp_i[0] += 1
        tb_ = tabC_buf[0]
        P.op("dve", lambda e: e.scalar_tensor_tensor(out=A.flat(np_=np_), in0=pb.flat(np_=np_), scalar=rstd1.flat(1, off=slot, np_=np_),
                                                     in1=tabC, op0=ALU.mult, op1=ALU.mult), [pb, rstd1, tb_], [A])
        for hf in range(2):
            P.op("dve", lambda e, hf=hf: e.scalar_tensor_tensor(out=B.ap([[128, 4], [1, 64]], off=hf * 64, np_=np_),
                                                                in0=pb.ap([[128, 4], [1, 64]], off=(1 - hf) * 64, np_=np_),
                                                                scalar=rstd1.flat(1, off=slot, np_=np_), in1=tabS(hf),
                                                                op0=ALU.mult, op1=ALU.mult), [pb, rstd1, tb_], [B])
        P.op("dve", lambda e: e.tensor_tensor(out=out_bf.flat(512, np_=np_), in0=A.flat(np_=np_), in1=B.flat(np_=np_),
                                              op=ALU.add), [A, B], [out_bf])

    tabC_buf = [None]

    def transposes(src_bf, np_, n=8, width=128, dst=None, eng="act", colscale=None):
        if dst is None:
            dst = xT
        pt = P.psum(B_TP, BF16)
        for j in range(n):
            P.op("pe", lambda e, j=j: e.transpose(pt.ap([[1, np_]], off=j * np_, np_=width),
                                                  src_bf.flat(width, off=j * width, np_=np_),
                                                  ident.ap([[1, np_]], np_=np_)), [src_bf, ident], [pt])
        if colscale is not None:
            P.op("dve", lambda e: e.tensor_tensor(out=dst.ap([[np_, n], [1, np_]], np_=width), in0=pt.ap([[np_, n], [1, np_]], np_=width),
                                                  in1=colscale.ap([[1, n], [0, np_]], np_=width), op=ALU.mult), [pt, colscale], [dst])
        elif eng == "act":
            P.op("act", lambda e: e.activation(out=dst.flat(n * np_, np_=width), in_=pt.flat(n * np_, np_=width), func=AF.Copy),
                 [pt], [dst])
        else:
            P.op("dve", lambda e: e.tensor_copy(dst.flat(n * np_, np_=width), pt.flat(n * np_, np_=width)), [pt], [dst])
        return dst

    op_banks = [[1, 2]]
    op_i = [0]

    def outproj(m, src_T, np_, evac):
        for n in range(2):
            s = oslabs[(m, n)]
            pb = P.psum(op_banks[0][op_i[0] % len(op_banks[0])], F32)
            op_i[0] += 1
            for k in range(8):
                P.op("pe", lambda e, k=k, s=s, pb=pb: e.matmul(pb.flat(512, np_=np_), src_T.ap([[1, np_]], off=k * np_),
                                                               s.flat(512, off=k * 512), start=(k == 0), stop=(k == 7)),
                     [src_T, s], [pb])
            evac(n, pb)

    oslabs = {}

    pstb = [P.psum(4 + h, F32) for h in range(4)]
    def pre_block(blk):
        hb = hT[blk % 2]
        xload(xp, blk * BLK * 128, 128, (blk * BLK) % 2)
        for i in range(BLK):
            t = blk * BLK + i
            if i + 1 < BLK:
                xload(xp, (t + 1) * 128, 128, (t + 1) % 2)
            stage0(xp, t * 128, 128, 16 + t, hb, i * 128, t % 2, preloaded=True)
        if blk == NT // BLK - 1:
            P.op("pool", lambda e: e.tensor_copy(hTm1.ap([[128, 8], [1, 128]]), hb.ap([[512, 8], [1, 128]], off=3 * 128)), [hb], [hTm1])
        kr = [arena(i, 1024, 512, BF16) for i in range(BLK)]
        vv = [arena(i, 2048, 1024, BF16) for i in range(BLK)]
        s_rk = pre_slabs[0]
        for i in range(BLK):
            t = blk * BLK + i
            tb = rtab[rtab_i[0] % 2]
            rtab_i[0] += 1
            P.dma("sp", tb.flat(1024), bass.AP(T["rtab_pre"], t * 128 * 1024, [[1024, 128], [1, 1024]]), [d_in], [tb])
            pb = inproj(s_rk, hb, i * 128, 128)
            tabC_buf[0] = tb
            ret_rope(pb, 16 + t, tb.flat(512), lambda hf, tb=tb: tb.ap([[128, 4], [1, 64]], off=512 + hf * 64), kr[i], 128)
            if t < 8:
                emit_convs(1)
        for n in range(2):
            s_rv = pre_slabs[1 + n]
            for i in range(BLK):
                t = blk * BLK + i
                pb = inproj(s_rv, hb, i * 128, 128)
                P.op("act", lambda e, pb=pb, i=i, n=n, t=t: e.activation(out=vv[i].flat(512, off=n * 512), in_=pb.flat(), func=AF.Copy,
                                                                      scale=rstd1.flat(1, off=16 + t)), [pb, rstd1], [vv[i]])
        for i in range(BLK):
            t = blk * BLK + i
            for h in range(4):
                P.op("pe", lambda e, h=h, i=i, t=t: e.matmul(pstb[h].flat(256), kr[i].flat(128, off=h * 128),
                                                          vv[i].flat(256, off=h * 256), start=(t == 0), stop=(t == NT - 1)),
                     [kr[i], vv[i]], [pstb[h]])

    inproj_wide[0] = False
    mm_banks[:] = [1, 2, 3]
    for blk in range(NT // BLK):
        pre_block(blk)
    inproj_wide[0] = True
    mm_banks[:] = [1, 2]
    for j in range(4):
        P.op("dve", lambda e, j=j: e.tensor_copy(Sst.flat(256, off=j * 256), pstb[j].flat(256)), [pstb[j]], [Sst])
    P.op("act", lambda e: e.activation(out=Sbf.flat(), in_=Sst.flat(), func=AF.Copy), [Sst], [Sbf])

    nblocks = NT // BLK + (1 if with_sample else 0)
    slab_seq.extend(BLOCK_SEQ * nblocks)
    d_yp = newout(); d_rsp = newout(); d_kp = newout(); d_vp = newout()
    gtile = [0]
    kc_keys = []
    vc_keys = []
    swt_i = [0]

    def do_stage0_block(blk):
        hb = hT[blk % 2]
        if blk < NT // BLK:
            for i in range(BLK):
                t = blk * BLK + i
                stage0(xm, t * 128, 128, t, hb, i * 128, t % 2)
        else:
            stage0(xsd, 0, 16, 32, hb, 0, 0)

    G = dict(locals())

    def blkinfo(blk):
        samp = blk >= NT // BLK
        return dict(samp=samp, np_=16 if samp else 128, ntl=1 if samp else BLK, hb=hT[blk % 2],
                    slots=[32] if samp else [blk * BLK + i for i in range(BLK)], blk=blk)

    def gen_S0(blk):
        hb = hT[blk % 2]
        if blk < NT // BLK:
            xload(xm, blk * BLK * 128, 128, (blk * BLK) % 2)
            for i in range(BLK):
                t = blk * BLK + i
                if i + 1 < BLK:
                    xload(xm, (t + 1) * 128, 128, (t + 1) % 2)
                stage0(xm, t * 128, 128, t, hb, i * 128, t % 2, preloaded=True)
                yield
        else:
            stage0(xsd, 0, 16, 32, hb, 0, 0)
            yield

    def passA_bufs(blk):
        ar = arena_of((3 * blk) % 2)
        n = 1 if blk >= NT // BLK else BLK
        return dict(qr=[ar(i, 0, 512, BF16) for i in range(n)], kr=[ar(i, 1024, 512, BF16) for i in range(n)],
                    vv=[ar(i, 2048, 1024, BF16) for i in range(n)], GG=[ar(i, 4096, 1024, BF16) for i in range(n)])

    def gen_A1(blk):
        I = blkinfo(blk)
        samp, np_, ntl, hb, slots = I["samp"], I["np_"], I["ntl"], I["hb"], I["slots"]
        Bf = passA_bufs(blk)
        qr, kr, vv, GG = Bf["qr"], Bf["kr"], Bf["vv"], Bf["GG"]
        wide_banks[:] = [1, 2, 3, 4]
        if samp:
            sample_prefetch(P, {**G, **I}, T, d_in)
        if with_sample and blk == 1:
            sample_cache_copy(P, T, d_in, newout)
        tspecs = [(o0, i) for o0 in (0, 1024) for i in range(ntl)]
        tbufs = {}

        def tload(j):
            o0, i = tspecs[j]
            tb = rtab[rtab_i[0] % 2]
            rtab_i[0] += 1
            if samp:
                P.dma("sp", tb.flat(1024, np_=16), bass.AP(T["rtab_samp"], o0, [[2048, 16], [1, 1024]]), [d_in], [tb])
            else:
                P.dma("sp", tb.flat(1024), bass.AP(T["rtab_main"], slots[i] * 128 * 2048 + o0, [[2048, 128], [1, 1024]]), [d_in], [tb])
            tbufs[j] = tb

        tload(0)
        tj = 0
        for nm, c0 in (("q", 0), ("k", 512)):
            s = load_slab("in", c0)
            o0 = 0 if nm == "q" else 1024
            for i in range(ntl):
                tb = tbufs[tj]
                if tj + 1 < len(tspecs):
                    tload(tj + 1)
                tj += 1
                pb = inproj(s, hb, i * 128, np_)
                tabC_buf[0] = tb
                ret_rope(pb, slots[i], tb.flat(512, np_=np_),
                         lambda hf, tb=tb: tb.ap([[128, 4], [1, 64]], off=512 + hf * 64, np_=np_),
                         qr[i] if nm == "q" else kr[i], np_)
                yield
        for n in range(2):
            s = load_slab("in", 1024 + n * 512)
            for i in range(ntl):
                pb = inproj(s, hb, i * 128, np_)
                P.op("act", lambda e, pb=pb, i=i, n=n: e.activation(out=vv[i].flat(512, off=n * 512, np_=np_), in_=pb.flat(np_=np_), func=AF.Copy,
                                                                   scale=rstd1.flat(1, off=slots[i], np_=np_)), [pb, rstd1], [vv[i]])
                yield
        for n in range(2):
            s = load_slab("in", 2048 + n * 512)
            for i in range(ntl):
                pb = inproj(s, hb, i * 128, np_)
                P.op("act", lambda e, pb=pb, i=i, n=n: e.activation(out=GG[i].flat(512, off=n * 512, np_=np_), in_=pb.flat(np_=np_), func=AF.Silu,
                                                                   scale=rstd1.flat(1, off=slots[i], np_=np_)), [pb, rstd1], [GG[i]])
                yield
        wide_banks[:] = [1, 2]

    def gen_A2(blk):
        I = blkinfo(blk)
        samp, np_, ntl = I["samp"], I["np_"], I["ntl"]
        Bf = passA_bufs(blk)
        L = {**G, **I, **Bf}
        oslabs[(0, 0)] = load_slab("o", 0, 0)
        oslabs[(0, 1)] = load_slab("o", 0, 1)
        op_banks[0] = [1, 2] if samp else [4, 5]
        if not samp:
            ret_mixer_front(P, L, 0)
            yield
        for i in range(ntl):
            if not samp:
                yield from ret_mixer_prompt(P, L, i, (lambda i=i: ret_mixer_front(P, L, i + 1)) if i + 1 < ntl else None)
            else:
                ret_mixer_sample(P, L, T, d_in, newout)
            transposes(tok, np_, colscale=gretcol)
            yield

            def ev(n, pb, i=i):
                P.op("act", lambda e: e.activation(out=brr[i].flat(512, off=n * 512, np_=np_), in_=pb.flat(np_=np_), func=AF.Copy), [pb], [brr[i]])
            outproj(0, xT, np_, ev)
            yield
        if blk == NT // BLK - 1:
            for h in range(4):
                P.op("dve", lambda e, h=h: e.tensor_scalar(Sst.flat(256, off=h * 256), Sst.flat(256, off=h * 256), float(GAM[h] ** 2048), None,
                                                           op0=ALU.mult), [Sst], [Sst])
            P.dma("sp", bass.AP(T["rsp"], 0, [[256, 128], [128 * 256, 4], [1, 256]]), Sst.ap([[256, 4], [1, 256]]), [Sst], [d_rsp])

    def gen_C1(blk):
        I = blkinfo(blk)
        samp, np_, ntl, hb, slots = I["samp"], I["np_"], I["ntl"], I["hb"], I["slots"]
        ar = arena_of((3 * blk + 2) % 2)
        mbuf = [ar(i, 0, 1024, F32) for i in range(ntl)]
        tokC = [ar(i, 4096, 1024, BF16) for i in range(ntl)]
        for n in range(2):
            s = load_slab("in", 5632 + n * 512)
            for i in range(ntl):
                pb = inproj(s, hb, i * 128, np_)
                sg = sg16[(n * ntl + i) % 2]
                P.op("act", lambda e, pb=pb, i=i, sg=sg: e.activation(out=sg.flat(np_=np_), in_=pb.flat(np_=np_), func=AF.Tanh,
                                                                     scale=rh.flat(1, off=slots[i], np_=np_)), [pb, rh], [sg])
                P.op("dve", lambda e, i=i, n=n, sg=sg: e.scalar_tensor_tensor(out=mbuf[i].flat(512, off=n * 512, np_=np_), in0=sg.flat(np_=np_), scalar=1.0,
                                                                             in1=brr[i].flat(512, off=n * 512, np_=np_), op0=ALU.add, op1=ALU.mult), [sg, brr[i]], [mbuf[i]])
                yield

    def gen_C1b(blk):
        I = blkinfo(blk)
        samp, np_, ntl, hb, slots = I["samp"], I["np_"], I["ntl"], I["hb"], I["slots"]
        ar = arena_of((3 * blk + 2) % 2)
        mbuf = [ar(i, 0, 1024, F32) for i in range(ntl)]
        tokC = [ar(i, 4096, 1024, BF16) for i in range(ntl)]
        oslabs[(2, 0)] = load_slab("o", 2, 0)
        oslabs[(2, 1)] = load_slab("o", 2, 1)
        for n in range(2):
            s = load_slab("in", 6656 + n * 512)
            for i in range(ntl):
                pb = inproj(s, hb, i * 128, np_)
                sg = sg16[(n * ntl + i) % 2]
                tA = tmpAr[(n * ntl + i) % 2]
                P.op("act", lambda e, pb=pb, i=i, sg=sg: e.activation(out=sg.flat(np_=np_), in_=pb.flat(np_=np_), func=AF.Tanh,
                                                                     scale=rh.flat(1, off=slots[i], np_=np_)), [pb, rh], [sg])
                P.op("dve", lambda e, i=i, n=n, sg=sg, tA=tA: e.scalar_tensor_tensor(out=tA.flat(np_=np_), in0=sg.flat(np_=np_), scalar=1.0,
                                                                                    in1=brs[i].flat(512, off=n * 512, np_=np_), op0=ALU.add, op1=ALU.mult), [sg, brs[i]], [tA])
                P.op("pool", lambda e, i=i, n=n, tA=tA: e.tensor_tensor(out=tokC[i].flat(512, off=n * 512, np_=np_), in0=mbuf[i].flat(512, off=n * 512, np_=np_),
                                                                       in1=tA.flat(np_=np_), op=ALU.add), [mbuf[i], tA], [tokC[i]])
                yield

    def gen_C2(blk):
        I = blkinfo(blk)
        samp, np_, ntl, slots = I["samp"], I["np_"], I["ntl"], I["slots"]
        ar = arena_of((3 * blk + 2) % 2)
        tokC = [ar(i, 4096, 1024, BF16) for i in range(ntl)]
        op_banks[0] = [5, 6, 7]
        for i in range(ntl):
            xb_ = ybuf[i % 2]
            if samp:
                P.dma("sp", xb_.flat(np_=16), xsd.ap(), [d_in], [xb_])
            else:
                P.dma("sp", xb_.flat(), bass.AP(xm, slots[i] * 128 * D, [[D, 128], [1, D]]), [d_in], [xb_])
            transposes(tokC[i], np_)
            yield

            def ev(n, pb, xb_=xb_):
                P.op("dve", lambda e: e.scalar_tensor_tensor(out=xb_.flat(512, off=n * 512, np_=np_), in0=pb.flat(np_=np_), scalar=0.5,
                                                             in1=xb_.flat(512, off=n * 512, np_=np_), op0=ALU.mult, op1=ALU.add), [pb, xb_], [xb_])
            outproj(2, xT, np_, ev)
            if samp:
                d_ys = newout()
                P.dma("sp", T["ysd"].ap(), xb_.flat(np_=16), [xb_], [d_ys])
            else:
                P.dma("pool", bass.AP(T["yp"], slots[i] * 128 * D, [[D, 128], [1, D]]), xb_.flat(), [xb_], [d_yp])
            yield
        op_banks[0] = [1, 2]

    def run(g):
        for _ in g:
            pass

    def inter(g1, g2, r1=1, r2=1):
        live = [[g1, r1], [g2, r2]]
        while live:
            for ent in list(live):
                for _ in range(ent[1]):
                    try:
                        next(ent[0])
                    except StopIteration:
                        live.remove(ent)
                        break

    def chain(*gs):
        for g in gs:
            yield from g

    no_prefetch[0] = True
    run(gen_S0(0))
    run(gen_A1(0))

    def with_convs(g):
        for _ in g:
            emit_convs(1)
            yield
        emit_convs(100)
    for blk in range(nblocks):
        I = blkinfo(blk)
        no_prefetch[0] = (blk == 0)
        gB1, gB2 = swa_pass(P, {**G, **I, "arena": arena_of((3 * blk + 1) % 2)}, T, d_in, newout)
        if I["samp"]:
            run(gen_A2(blk)); run(gB1); run(gB2); run(gen_C1(blk)); run(gen_C1b(blk)); run(gen_C2(blk))
            continue
        inter(with_convs(gen_A2(blk)) if blk == 0 else gen_A2(blk), gB1, 1, 1)
        inter(gB2, gen_C1(blk), 1, 1)
        if blk + 1 < nblocks:
            inter(gen_C1b(blk), gen_S0(blk + 1), 2, 1)
            inter(gen_C2(blk), gen_A1(blk + 1), 1, 3)
        else:
            run(gen_C1b(blk))
            run(gen_C2(blk))
    P.op("sp", None, d_outs, [])


def ret_mixer_front(P, L, i):
    qr, kr = L["qr"][i], L["kr"][i]
    ident, cmask = L["ident"], L["cmask"]
    qkT = L["qkT2"][i % 2]
    scT = L["scT2"][i % 2]
    pt = P.psum(0, BF16)
    for h in range(4):
        P.op("pe", lambda e, h=h: e.transpose(pt.flat(128, off=h * 128), qr.flat(128, off=h * 128), ident.flat()), [qr, ident], [pt])
    for h in range(4):
        P.op("pe", lambda e, h=h: e.transpose(pt.flat(128, off=(4 + h) * 128), kr.flat(128, off=h * 128), ident.flat()), [kr, ident], [pt])
    P.op("act", lambda e: e.activation(out=qkT.flat(), in_=pt.flat(), func=AF.Copy), [pt], [qkT])
    psc = P.psum(3, F32)
    for h in range(4):
        P.op("pe", lambda e, h=h: e.matmul(psc.flat(128, off=h * 128), qkT.flat(128, off=(4 + h) * 128), qkT.flat(128, off=h * 128),
                                           start=True, stop=True), [qkT], [psc])
    P.op("dve", lambda e: e.tensor_tensor(out=scT.ap([[128, 4], [1, 128]]), in0=psc.ap([[128, 4], [1, 128]]),
                                          in1=cmask.ap([[0, 4], [1, 128]]), op=ALU.mult), [psc, cmask], [scT])


def ret_mixer_prompt(P, L, i, mid=None):
    qr, kr, vv, GG = L["qr"][i], L["kr"][i], L["vv"][i], L["GG"][i]
    Sst, Sbf, small, junk, tok, mhalf = (L[k] for k in ("Sst", "Sbf", "small", "junk", "tok", "mhalf"))
    qkT = L["qkT2"][i % 2]
    scT = L["scT2"][i % 2]
    po = [P.psum(4, F32), P.psum(5, F32)]
    for h in range(4):
        pb = po[h // 2]
        P.op("pe", lambda e, h=h, pb=pb: e.matmul(pb.flat(256, off=(h % 2) * 256), scT.flat(128, off=h * 128), vv.flat(256, off=h * 256),
                                                  start=True, stop=False), [scT, vv], [pb])
        P.op("pe", lambda e, h=h, pb=pb: e.matmul(pb.flat(256, off=(h % 2) * 256), qkT.flat(128, off=h * 128), Sbf.flat(256, off=h * 256),
                                                  start=False, stop=True), [qkT, Sbf], [pb])
    yield
    pstb = [P.psum(6, F32), P.psum(7, F32)]
    for h in range(4):
        pb = pstb[h // 2]
        P.op("pe", lambda e, h=h, pb=pb: e.matmul(pb.flat(256, off=(h % 2) * 256), kr.flat(128, off=h * 128), vv.flat(256, off=h * 256),
                                                  start=True, stop=True), [kr, vv], [pb])
    yield
    for j in range(2):
        P.op("dve", lambda e, j=j: e.tensor_tensor(out=Sst.flat(512, off=j * 512), in0=pstb[j].flat(), in1=Sst.flat(512, off=j * 512),
                                                   op=ALU.add), [pstb[j], Sst], [Sst])
    P.op("act", lambda e: e.activation(out=Sbf.flat(), in_=Sst.flat(), func=AF.Copy), [Sst], [Sbf])
    yield
    for h in range(4):
        pb = po[h // 2]
        P.op("act", lambda e, h=h, pb=pb: e.activation(out=junk.flat(256), in_=pb.flat(256, off=(h % 2) * 256), func=AF.Square,
                                                       accum_out=small.flat(1, off=h)), [pb], [junk, small])
    P.op("dve", lambda e: e.tensor_scalar(small.flat(4), small.flat(4), 1.0 / 256, EPS, op0=ALU.mult, op1=ALU.add), [small], [small])
    P.op("pool", lambda e: e.tensor_tensor(out=small.flat(4), in0=small.flat(4), in1=mhalf.flat(4), op=ALU.pow), [small, mhalf], [small])
    for h in range(4):
        pb = po[h // 2]
        P.op("dve", lambda e, h=h, pb=pb: e.scalar_tensor_tensor(out=tok.flat(256, off=h * 256), in0=pb.flat(256, off=(h % 2) * 256),
                                                                 scalar=small.flat(1, off=h), in1=GG.flat(256, off=h * 256),
                                                                 op0=ALU.mult, op1=ALU.mult), [pb, small, GG], [tok])
    yield
    if mid is not None:
        mid()
        yield


def ret_mixer_sample(P, L, T, d_in, newout):
    qr, kr, vv, GG = L["qr"][0], L["kr"][0], L["vv"][0], L["GG"][0]
    ident, small, small2, junk, tok, mhalf, tmpA, Sst = (L[k] for k in ("ident", "small", "small2", "junk", "tok", "mhalf", "tmpA", "Sst"))
    eye16, Qm, Sf32, Kb, Sbf2, qT, transposes = (L[k] for k in ("eye16", "Qm", "Sf32", "Kb", "Sbf2", "qT", "transposes"))
    N = 16
    P.dma("sp", eye16.flat(), T["eyed"].ap(), [d_in], [eye16])
    P.op("dve", lambda e: e.tensor_tensor(out=tmpA.flat(512, np_=N), in0=qr.flat(512, np_=N), in1=kr.flat(512, np_=N), op=ALU.mult), [qr, kr], [tmpA])
    P.op("dve", lambda e: e.tensor_reduce(out=small.flat(4, off=8, np_=N), in_=tmpA.ap([[128, 4], [1, 128]], np_=N), op=ALU.add, axis=AX.X), [tmpA], [small])
    transposes(qr, N, n=4, width=128, dst=qT)
    for h in range(4):
        P.op("dve", lambda e, h=h: e.tensor_tensor(out=Qm.ap([[16, 16], [1, 16]], off=h * 256), in0=qT.ap([[0, 16], [1, 16]], off=h * 16),
                                                   in1=eye16.ap([[16, 16], [1, 16]]), op=ALU.mult), [qT, eye16], [Qm])
    po = [P.psum(4 + h, F32) for h in range(4)]
    d_rss = newout()
    state = T["state"]
    for b in range(N):
        sbf = Sbf2[b % 4]
        sf = Sf32[b % 4]
        kb = Kb[b % 2]
        src = bass.AP(state, b * 4 * 128 * 256, [[256, 128], [128 * 256, 4], [1, 256]])
        P.dma("pool", sbf.ap([[256, 4], [1, 256]]), src, [d_in], [sbf])
        P.dma("sp", sf.ap([[256, 4], [1, 256]]), src, [d_in], [sf])
        for h in range(4):
            P.op("pe", lambda e, h=h, b=b, sbf=sbf: e.matmul(po[h].flat(256, np_=N), Qm.ap([[1, 16]], off=h * 256 + b * 16), sbf.flat(256, off=h * 256),
                                                            start=(b == 0), stop=(b == N - 1)), [Qm, sbf], [po[h]])
        P.op("dve", lambda e, b=b, kb=kb: e.tensor_scalar(kb.flat(512, np_=N), kr.flat(512, np_=N), ident.ap([[1, 1]], off=b, np_=N), None, op0=ALU.mult),
             [kr, ident], [kb])
        for j in range(2):
            pb = P.psum(1 + (2 * b + j) % 3, F32)
            for hh in range(2):
                h = 2 * j + hh
                P.op("pe", lambda e, h=h, hh=hh, pb=pb, kb=kb: e.matmul(pb.flat(256, off=hh * 256), kb.flat(128, off=h * 128, np_=N), vv.flat(256, off=h * 256, np_=N),
                                                                     start=True, stop=True), [kb, vv], [pb])
            for hh in range(2):
                h = 2 * j + hh
                P.op("dve", lambda e, h=h, hh=hh, pb=pb, sf=sf: e.scalar_tensor_tensor(out=sf.flat(256, off=h * 256), in0=sf.flat(256, off=h * 256), scalar=float(GAM[h]),
                                                                                    in1=pb.flat(256, off=hh * 256), op0=ALU.mult, op1=ALU.add), [sf, pb], [sf])
        P.dma("act", bass.AP(T["rss"], b * 4 * 128 * 256, [[256, 128], [128 * 256, 4], [1, 256]]), sf.ap([[256, 4], [1, 256]]), [sf], [d_rss])
    o32 = Sst
    for h in range(4):
        P.op("dve", lambda e, h=h: e.tensor_scalar(o32.flat(256, off=h * 256, np_=N), vv.flat(256, off=h * 256, np_=N), small.flat(1, off=8 + h, np_=N), None, op0=ALU.mult),
             [vv, small], [o32])
        P.op("dve", lambda e, h=h: e.scalar_tensor_tensor(out=o32.flat(256, off=h * 256, np_=N), in0=po[h].flat(256, np_=N), scalar=float(GAM[h]),
                                                          in1=o32.flat(256, off=h * 256, np_=N), op0=ALU.mult, op1=ALU.add), [po[h], o32], [o32])
    for h in range(4):
        P.op("act", lambda e, h=h: e.activation(out=junk.flat(256, np_=N), in_=o32.flat(256, off=h * 256, np_=N), func=AF.Square,
                                                accum_out=small2.flat(1, off=h, np_=N)), [o32], [junk, small2])
    P.op("dve", lambda e: e.tensor_scalar(small2.flat(4, np_=N), small2.flat(4, np_=N), 1.0 / 256, EPS, op0=ALU.mult, op1=ALU.add), [small2], [small2])
    P.op("pool", lambda e: e.tensor_tensor(out=small2.flat(4, np_=N), in0=small2.flat(4, np_=N), in1=mhalf.flat(4, np_=N), op=ALU.pow), [small2, mhalf], [small2])
    for h in range(4):
        P.op("dve", lambda e, h=h: e.scalar_tensor_tensor(out=tok.flat(256, off=h * 256, np_=N), in0=o32.flat(256, off=h * 256, np_=N),
                                                          scalar=small2.flat(1, off=h, np_=N), in1=GG.flat(256, off=h * 256, np_=N),
                                                          op0=ALU.mult, op1=ALU.mult), [o32, small2, GG], [tok])


def swa_pass(P, L, T, d_in, newout):
    samp, np_, ntl, hb, slots, blk = (L[k] for k in ("samp", "np_", "ntl", "hb", "slots", "blk"))
    arena, load_slab, inproj, transposes, outproj, oslabs = (L[k] for k in ("arena", "load_slab", "inproj", "transposes", "outproj", "oslabs"))
    ident, swamask, rstd1, r2, mhalf, gq4, esrep = (L[k] for k in ("ident", "swamask", "rstd1", "r2", "mhalf", "gq4", "esrep"))
    tmpX, tmpA, tmpB, small, small2, junk, tok, xT = (L[k] for k in ("tmpX", "tmpA", "tmpB", "small", "small2", "junk", "tok", "xT"))
    qT, kT, vaug, pT, gate2, kf32, vf32, stabs, swt, brs, gtile = (L[k] for k in
        ("qT", "kT", "vaug", "pT", "gate2", "kf32", "vf32", "stabs", "swt", "brs", "gtile"))
    q16 = [arena(i, 0, 1024, BF16) for i in range(ntl)]
    kdup = [arena(i, 2048, 512, BF16) for i in range(ntl)]
    gate = [arena(i, 4096, 1024, BF16) for i in range(ntl)]
    swtb = {}
    kt_todo = []

    def load_tab(tslot, key):
        r = L["swt_i"][0] % 2
        L["swt_i"][0] += 1
        sb_, w = stabs[r], swt[r]
        P.dma("sp", sb_.flat(np_=np_), bass.AP(T["stab"], tslot * 128 * 128, [[128, np_], [1, 128]]), [d_in], [sb_])
        P.op("pool", lambda e: e.tensor_tensor(out=w.ap([[128, 2], [1, 128]], np_=np_), in0=sb_.ap([[0, 2], [1, 128]], np_=np_),
                                               in1=gq4.ap([[128, 2], [1, 128]], np_=np_), op=ALU.mult), [sb_, gq4], [w])
        swtb[key] = w

    def qk_norm_rope(pb, nh, slot, w, coff, out_ap_fn, outbuf, ssbuf, soff):
        n = nh * 64
        ti = L["tmp_i"][0] % 2
        L["tmp_i"][0] += 1
        tmpX, tmpA, tmpB = L["tmpXr"][ti], L["tmpAr"][ti], L["tmpBr"][ti]
        P.op("act", lambda e: e.activation(out=tmpX.flat(n, np_=np_), in_=pb.flat(n, np_=np_), func=AF.Square), [pb], [tmpX])
        P.op("dve", lambda e: e.tensor_reduce(out=ssbuf.flat(nh, off=soff, np_=np_), in_=tmpX.ap([[64, nh], [1, 64]], np_=np_),
                                              op=ALU.add, axis=AX.X), [tmpX], [ssbuf])
        P.op("dve", lambda e: e.tensor_scalar(ssbuf.flat(nh, off=soff, np_=np_), ssbuf.flat(nh, off=soff, np_=np_),
                                              r2.flat(1, off=slot, np_=np_), EPS, op0=ALU.mult, op1=ALU.add), [ssbuf, r2], [ssbuf])
        P.op("pool", lambda e: e.tensor_tensor(out=ssbuf.flat(nh, off=soff, np_=np_), in0=ssbuf.flat(nh, off=soff, np_=np_),
                                               in1=mhalf.flat(nh, np_=np_), op=ALU.pow), [ssbuf, mhalf], [ssbuf])
        P.op("dve", lambda e: e.tensor_scalar(ssbuf.flat(nh, off=soff, np_=np_), ssbuf.flat(nh, off=soff, np_=np_),
                                              rstd1.flat(1, off=slot, np_=np_), None, op0=ALU.mult), [ssbuf, rstd1], [ssbuf])
        P.op("dve", lambda e: e.tensor_tensor(out=tmpA.ap([[64, nh], [1, 64]], np_=np_), in0=pb.ap([[64, nh], [1, 64]], np_=np_),
                                              in1=w.ap([[0, nh], [1, 64]], off=coff, np_=np_), op=ALU.mult), [pb, w], [tmpA])
        for hf in range(2):
            P.op("dve", lambda e, hf=hf: e.tensor_tensor(out=tmpB.ap([[64, nh], [1, 32]], off=hf * 32, np_=np_),
                                                         in0=pb.ap([[64, nh], [1, 32]], off=(1 - hf) * 32, np_=np_),
                                                         in1=w.ap([[0, nh], [1, 32]], off=coff + 64 + hf * 32, np_=np_), op=ALU.mult),
                 [pb, w], [tmpB])
        P.op("dve", lambda e: e.tensor_tensor(out=tmpA.flat(n, np_=np_), in0=tmpA.flat(n, np_=np_), in1=tmpB.flat(n, np_=np_), op=ALU.add),
             [tmpA, tmpB], [tmpA])
        P.op("pool", lambda e: e.tensor_tensor(out=out_ap_fn(), in0=tmpA.ap([[64, nh], [1, 64]], np_=np_),
                                               in1=ssbuf.ap([[1, nh], [0, 64]], off=soff, np_=np_), op=ALU.mult), [tmpA, ssbuf], [outbuf])

    def kv_tile(pb, slot, w, ring, kd, last):
        qk_norm_rope(pb, 4, slot, w, 128, lambda: kf32.ap([[64, 4], [1, 64]], np_=np_), kf32, L["small3"], 0)
        P.op("pool", lambda e: e.tensor_copy(kd.ap([[128, 4], [64, 2], [1, 64]], np_=np_), kf32.ap([[64, 4], [0, 2], [1, 64]], np_=np_)), [kf32], [kd])
        P.op("act", lambda e: e.activation(out=vaug[ring].ap([[66, 4], [1, 64]], np_=np_), in_=pb.ap([[64, 4], [1, 64]], off=256, np_=np_),
                                           func=AF.Copy, scale=rstd1.flat(1, off=slot, np_=np_)), [pb, rstd1], [vaug[ring]])
        if last or samp:
            P.op("act", lambda e: e.activation(out=vf32.flat(256, np_=np_), in_=pb.flat(256, off=256, np_=np_), func=AF.Copy,
                                               scale=rstd1.flat(1, off=slot, np_=np_)), [pb, rstd1], [vf32])
        if last:
            P.dma("sp", T["kpd"].ap(), kf32.flat(256), [kf32], [L["d_kp"]])
            P.dma("sp", T["vpd"].ap(), vf32.flat(256), [vf32], [L["d_vp"]])
        if not samp:
            kt_todo.append((kd, ring))

    def flush_kt():
        for kd, ring in kt_todo:
            transposes(kd, 128, n=4, width=128, dst=kT[ring])
        del kt_todo[:]

    g0 = gtile[0]

    def stage1():
        s_kv = load_slab("in", 4096)
        if blk == 0:
            load_tab(16, "m1")
            pb = inproj(s_kv, L["hTm1"], 0, 128, kst=128)
            kv_tile(pb, 31, swtb["m1"], 0, kdup[0], False)
        flush_kt()
        tab_seq = [((17 if samp else slots[i]), (u, i)) for u in ("k", "q0", "q1") for i in range(ntl)]
        tpos = [0]

        def tab_next():
            j = tpos[0]
            if j == 0:
                load_tab(*tab_seq[0])
            w = swtb[tab_seq[j][1]]
            if j + 1 < len(tab_seq):
                load_tab(*tab_seq[j + 1])
            tpos[0] += 1
            return w

        for i in range(ntl):
            w = tab_next()
            pb = inproj(s_kv, hb, i * 128, np_)
            kv_tile(pb, slots[i], w, (g0 + i + 1) % 5, kdup[i], (not samp) and slots[i] == NT - 1)
            yield
        for n in range(2):
            s = load_slab("in", 3072 + n * 512)
            for i in range(ntl):
                w = tab_next()
                pb = inproj(s, hb, i * 128, np_)
                qk_norm_rope(pb, 8, slots[i], w, 0, lambda i=i, n=n: q16[i].ap([[64, 8], [1, 64]], off=n * 512, np_=np_), q16[i], small2, n * 8)
                yield
        for n in range(2):
            s = load_slab("in", 4608 + n * 512)
            for i in range(ntl):
                pb = inproj(s, hb, i * 128, np_)
                P.op("act", lambda e, pb=pb, i=i, n=n: e.activation(out=gate[i].flat(512, off=n * 512, np_=np_), in_=pb.flat(np_=np_), func=AF.Silu,
                                                                   scale=rstd1.flat(1, off=slots[i], np_=np_)), [pb, rstd1], [gate[i]])
                yield


    def mixers():
        flush_kt()
        oslabs[(1, 0)] = load_slab("o", 1, 0)
        oslabs[(1, 1)] = load_slab("o", 1, 1)
        L["op_banks"][0] = [1, 2] if samp else [3, 4]
        pov = [P.psum(5, F32), P.psum(6, F32), P.psum(7, F32)]

        def povslot(h):
            return pov[h // 6], (h % 6) * 65

        def scores(i, g):
            var = 0 if slots[i] == 0 else 1
            cur, prev = (g0 + i + 1) % 5, (g0 + i) % 5
            pTs = pT[g % 2]
            for par in range(2):
                bank = P.psum((3 + par) if (i == 0 or g % 2 == 0) else (1 + par), F32)
                for bi, kTb in enumerate((kT[prev], kT[cur])):
                    P.op("pe", lambda e, bank=bank, bi=bi, kTb=kTb, g=g, par=par: e.matmul(
                        bank.flat(256, off=bi * 256), kTb.ap([[1, 128]], off=g * 128, p0=64 * par, np_=64),
                        qT.ap([[128, 2], [1, 128]], off=2 * g * 128, p0=64 * par, np_=64), start=True, stop=False), [kTb, qT], [bank])
                    P.op("pe", lambda e, bank=bank, bi=bi, var=var: e.matmul(
                        bank.flat(256, off=bi * 256), ident.flat(), swamask.ap([[0, 2], [1, 128]], off=var * 256 + bi * 128),
                        start=False, stop=True), [ident, swamask], [bank])
                P.op("act", lambda e, bank=bank, par=par, pTs=pTs: e.activation(out=pTs[par].flat(), in_=bank.flat(), func=AF.Exp, scale=0.125),
                     [bank], [pTs[par]])

        def pv(i, g):
            cur, prev = (g0 + i + 1) % 5, (g0 + i) % 5
            pTs = pT[g % 2]
            for par in range(2):
                for jj in range(2):
                    h = 4 * g + 2 * jj + par
                    pb, off = povslot(h)
                    P.op("pe", lambda e, pb=pb, off=off, par=par, jj=jj, pTs=pTs, g=g, prev=prev: e.matmul(
                        pb.flat(65, off=off), pTs[par].flat(128, off=jj * 128), vaug[prev].flat(65, off=g * 66), start=True, stop=False),
                        [pTs[par], vaug[prev]], [pb])
                    P.op("pe", lambda e, pb=pb, off=off, par=par, jj=jj, pTs=pTs, g=g, cur=cur: e.matmul(
                        pb.flat(65, off=off), pTs[par].flat(128, off=(2 + jj) * 128), vaug[cur].flat(65, off=g * 66), start=False, stop=True),
                        [pTs[par], vaug[cur]], [pb])

        def front(i):
            transposes(q16[i], 128, n=8, width=128, dst=qT)
            scores(i, 0)

        def mix(i):
            if samp:
                swa_mixer_sample(P, L, T, d_in, newout, q16[i], kdup[i], vaug[(g0 + i + 1) % 5], pov, povslot)
            else:
                if i == 0:
                    front(0)
                    yield
                for g in range(4):
                    if g + 1 < 4:
                        scores(i, g + 1)
                    elif i + 1 < ntl:
                        front(i + 1)
                    yield
                    pv(i, g)
                    yield
            for bnk in range(3):
                nh = 6 if bnk < 2 else 4
                P.op("dve", lambda e, bnk=bnk, nh=nh: e.tensor_tensor(out=small2.flat(nh, off=16 + bnk * 6, np_=np_), in0=pov[bnk].ap([[65, nh]], off=64, np_=np_),
                                                                     in1=esrep.flat(nh, off=bnk * 6, np_=np_), op=ALU.add), [pov[bnk], esrep], [small2])
            P.op("dve", lambda e: e.reciprocal(out=small2.flat(16, off=16, np_=np_), in_=small2.flat(16, off=16, np_=np_)), [small2], [small2])
            P.op("pool", lambda e, i=i: e.tensor_tensor(out=gate2.ap([[64, 16], [1, 64]], np_=np_), in0=gate[i].ap([[64, 16], [1, 64]], np_=np_),
                                                        in1=small2.ap([[1, 16], [0, 64]], off=16, np_=np_), op=ALU.mult), [gate[i], small2], [gate2])
            for bnk in range(3):
                nh = 6 if bnk < 2 else 4
                P.op("dve", lambda e, bnk=bnk, nh=nh: e.tensor_tensor(out=tok.ap([[64, nh], [1, 64]], off=bnk * 384, np_=np_),
                                                                     in0=pov[bnk].ap([[65, nh], [1, 64]], np_=np_),
                                                                     in1=gate2.ap([[64, nh], [1, 64]], off=bnk * 384, np_=np_), op=ALU.mult),
                     [pov[bnk], gate2], [tok])
            if T["dbg"] and blk == 0:
                P.dma("sp", bass.AP(T["dbg3"], i * 128 * 1024, [[1024, 128], [1, 1024]]), tok.flat(), [tok], [newout()])
            yield
            transposes(tok, np_)
            yield

            def ev(n, pb, i=i):
                P.op("act", lambda e: e.activation(out=brs[i].flat(512, off=n * 512, np_=np_), in_=pb.flat(np_=np_), func=AF.Copy), [pb], [brs[i]])
            outproj(1, xT, np_, ev)
            yield
        for i in range(ntl):
            yield from mix(i)
        gtile[0] += ntl


    return stage1(), mixers()


def sample_cache_copy(P, T, d_in, newout):
    for src_t, dst_t in ((T["ckd"], T["ksd"]), (T["cvd"], T["vsd"])):
        for hb_ in range(2):
            d1 = newout()
            o = hb_ * 8 * 128 * 256
            P.dma("sp", bass.AP(dst_t, o, [[128 * 256, 8], [1, 127 * 256]]), bass.AP(src_t, o + 256, [[128 * 256, 8], [1, 127 * 256]]), [d_in], [d1])


def sample_prefetch(P, L, T, d_in):
    Kc, Vc = L["Kc"], L["Vc"]
    ckd, cvd = T["ckd"], T["cvd"]
    N = 16
    P.op("pool", lambda e: e.memset(Vc.flat(), 1.0), [], [Vc])
    thr = [P.dram(None) for _ in range(6)]
    j = 0
    for p0, npp in ((0, 64), (64, 63)):
        k = P.dram(None)
        L["kc_keys"].append(k)
        P.dma("pool", Kc.ap([[256, 16], [1, 256]], p0=p0, np_=npp), bass.AP(ckd, 256 * (1 + p0), [[256, npp], [128 * 256, 16], [1, 256]]),
              [d_in, Kc], [k, thr[j % 6]])
        j += 1
        for b in range(N):
            k = P.dram(None)
            L["vc_keys"].append(k)
            P.dma("pool", Vc.ap([[66, 4], [1, 64]], off=b * 264, p0=p0, np_=npp),
                  bass.AP(cvd, b * 128 * 256 + 256 * (1 + p0), [[256, npp], [64, 4], [1, 64]]), [d_in, Vc], [k, thr[j % 6]])
            j += 1


def swa_mixer_sample(P, L, T, d_in, newout, q16, kd, vaug_s, pov, povslot):
    ident, kf32, vf32, eye16, Kc, Vc, KTr, pTs, Pm, qT, transposes = (L[k] for k in
        ("ident", "kf32", "vf32", "eye16", "Kc", "Vc", "KTr", "pTs", "Pm", "qT", "transposes"))
    ckd, cvd = T["ckd"], T["cvd"]
    N = 16
    for dst_t, newrow in ((T["ksd"], kf32), (T["vsd"], vf32)):
        d2 = newout()
        P.dma("sp", bass.AP(dst_t, 127 * 256, [[128 * 256, 16], [1, 256]]), newrow.flat(256, np_=N), [newrow], [d2])
    P.dma("sp", Kc.ap([[256, 16], [64, 4], [1, 64]], p0=127, np_=1), kd.ap([[128, 4], [1, 64]], np_=N), [kd, Kc] + L["kc_keys"], [Kc])
    P.dma("sp", Vc.ap([[264, 16], [66, 4], [1, 64]], p0=127, np_=1), vaug_s.ap([[66, 4], [1, 64]], np_=N), [vaug_s, Vc] + L["vc_keys"], [Vc])
    transposes(q16, N, n=16, width=64, dst=qT)
    sc = P.psum(3, F32)
    for bp in range(N // 2):
        ktr = KTr[bp % 2]
        pt = P.psum(0, BF16)
        for j in range(8):
            b, g = 2 * bp + j // 4, j % 4
            P.op("pe", lambda e, j=j, b=b, g=g, pt=pt: e.transpose(pt.ap([[1, 128]], off=j * 128, np_=64), Kc.ap([[1, 64]], off=b * 256 + g * 64), ident.flat()),
                 [Kc, ident], [pt])
        P.op("act", lambda e, pt=pt, ktr=ktr: e.activation(out=ktr.flat(1024, np_=64), in_=pt.flat(1024, np_=64), func=AF.Copy), [pt], [ktr])
        for j in range(8):
            b, g = 2 * bp + j // 4, j % 4
            P.op("pe", lambda e, j=j, b=b, g=g, ktr=ktr: e.matmul(sc.ap([[1, 4]], off=b * 16 + g * 4), ktr.ap([[1, 128]], off=j * 128, np_=64),
                                                                 qT.ap([[16, 4]], off=4 * g * 16 + b, np_=64), start=True, stop=True), [ktr, qT], [sc])
    P.op("act", lambda e: e.activation(out=pTs.flat(256), in_=sc.flat(256), func=AF.Exp, scale=0.125), [sc], [pTs])
    for h in range(16):
        P.op("dve", lambda e, h=h: e.tensor_tensor(out=Pm.ap([[16, 16], [1, 16]], off=h * 256), in0=pTs.ap([[0, 16], [16, 16]], off=h),
                                                   in1=eye16.ap([[16, 16], [1, 16]]), op=ALU.mult), [pTs, eye16], [Pm])
    for h in range(16):
        g = h // 4
        pb, off = povslot(h)
        for b in range(N):
            P.op("pe", lambda e, h=h, g=g, b=b, pb=pb, off=off: e.matmul(pb.flat(65, off=off, np_=N), Pm.ap([[1, 16]], off=h * 256 + b * 16),
                                                                        Vc.ap([[1, 65]], off=b * 264 + g * 66), start=(b == 0), stop=(b == N - 1)),
                 [Pm, Vc], [pb])


def _tables(half):
    f64 = np.float64
    T = np.arange(2048, dtype=f64)
    lg = np.log(np.array(GAM, dtype=f64))
    inv_r = 10000.0 ** (-np.arange(0, 128, 2, dtype=f64) / 128)
    inv_s = 10000.0 ** (-np.arange(0, 64, 2, dtype=f64) / 64)

    def cs2(pos, inv):
        ang = pos[:, None] * inv[None, :]
        c, s_ = np.cos(ang), np.sin(ang)
        return np.concatenate([c, c], 1), np.concatenate([-s_, s_], 1)

    pos_main = half * 2048 + T
    c2, s2 = cs2(pos_main, inv_r)
    aq = np.exp((T[:, None] + 1) * lg[None, :])
    ak = np.exp(-(T[:, None] + 1) * lg[None, :]) / np.sqrt(128.0)
    rt = np.stack([c2[:, None, :] * aq[:, :, None], s2[:, None, :] * aq[:, :, None],
                   c2[:, None, :] * ak[:, :, None], s2[:, None, :] * ak[:, :, None]], 1)
    rtab_main = rt.reshape(16, 128, 2048).astype(np.float32)
    c2p, s2p = cs2(T, inv_r)
    akp = np.exp((2047 - T[:, None]) * lg[None, :]) / np.sqrt(128.0)
    rtp = np.stack([c2p[:, None, :] * akp[:, :, None], s2p[:, None, :] * akp[:, :, None]], 1)
    rtab_pre = rtp.reshape(16, 128, 1024).astype(np.float32)
    c2s, s2s = cs2(np.full(16, 8192.0), inv_r)
    one = np.ones((16, 4, 1))
    rts = np.stack([c2s[:, None, :] * one, s2s[:, None, :] * one, c2s[:, None, :] * one / np.sqrt(128.0),
                    s2s[:, None, :] * one / np.sqrt(128.0)], 1)
    rtab_samp = rts.reshape(16, 2048).astype(np.float32)
    stab = np.zeros((18, 128, 128), np.float32)
    cm, sm = cs2(pos_main, inv_s)
    stab[:16] = np.concatenate([cm, sm], 1).reshape(16, 128, 128)
    cp, sp_ = cs2(1920 + np.arange(128, dtype=f64), inv_s)
    stab[16] = np.concatenate([cp, sp_], 1)
    cs_, ss_ = cs2(np.full(128, 8192.0), inv_s)
    stab[17] = np.concatenate([cs_, ss_], 1)
    k = np.arange(128)[:, None]
    q = np.arange(128)[None, :]
    cmask = (q >= k).astype(np.float32)
    prev = np.where(k > q, 0.0, MASKNEG)
    cur = np.where(k <= q, 0.0, MASKNEG)
    first_prev = prev if half == 1 else np.full((128, 128), MASKNEG)
    swamask = np.stack([first_prev, cur, prev, cur], 1).reshape(128, 512)
    swamask = np.concatenate([swamask, np.zeros((128, 512))], 1).astype(np.float32)
    eye16 = np.tile(np.eye(16, dtype=np.float32).reshape(1, 256), (128, 1))
    return dict(rtab_main=rtab_main, rtab_pre=rtab_pre, rtab_samp=rtab_samp, stab=stab, cmask=cmask,
                swamask=swamask, eye16=eye16, ident=np.eye(128, dtype=np.float32))


_CACHE = {}


def kernel(x_prompt, x_sample, state_ret, cache_swa_k, cache_swa_v, norm_g, w_in, ret_norm_g,
           swa_q_g, swa_k_g, swa_sinks, w_br_ret, w_br_swa, w_out, _with_sample=True, _dbg=False):
    f = lambda a: np.ascontiguousarray(np.asarray(a, dtype=np.float32))
    x_prompt, x_sample, state_ret, cache_swa_k, cache_swa_v = map(f, (x_prompt, x_sample, state_ret, cache_swa_k, cache_swa_v))
    w_in_ = f(w_in)[0]
    w_o3 = np.ascontiguousarray(np.stack([f(w_br_ret)[0], f(w_br_swa)[0], f(w_out)[0]], 0))
    gq, gk = f(swa_q_g)[0], f(swa_k_g)[0]
    sw = lambda g: np.concatenate([g[32:], g[:32]])
    gqk = np.ascontiguousarray(np.stack([gq, sw(gq), gk, sw(gk)], 0))
    ng = np.ascontiguousarray(f(norm_g)[0].reshape(8, 128).T)
    if "nc" not in _CACHE:
        _CACHE["nc"] = build_program(with_sample=_with_sample, dbg=_dbg)[0]
        _CACHE["tabs"] = [_tables(0), _tables(1)]
    nc = _CACHE["nc"]
    in_maps = []
    for c in range(8):
        b, half = c // 2, c % 2
        m = dict(_CACHE["tabs"][half])
        m["xm"] = np.ascontiguousarray(x_prompt[b, half * 2048:(half + 1) * 2048])
        m["xp"] = np.ascontiguousarray(x_prompt[b, 0:2048]) if half == 1 else np.zeros((2048, D), np.float32)
        m["xs"] = np.ascontiguousarray(x_sample[16 * c:16 * c + 16, 0])
        m["w_in"] = w_in_
        m["w_o3"] = w_o3
        m["norm_g"] = ng
        m["ret_g"] = np.ascontiguousarray(f(ret_norm_g)[0].reshape(8, 128).T)
        m["gqk"] = gqk
        m["sinks"] = np.ascontiguousarray(f(swa_sinks)[0])
        m["state"] = np.ascontiguousarray(state_ret[0, 16 * c:16 * c + 16])
        m["ck"] = np.ascontiguousarray(cache_swa_k[0, 16 * c:16 * c + 16].reshape(16, 128, 256))
        m["cv"] = np.ascontiguousarray(cache_swa_v[0, 16 * c:16 * c + 16].reshape(16, 128, 256))
        in_maps.append(m)
    res = run_bass_kernel_spmd(nc, in_maps, core_ids=list(range(8)))
    R = res.results
    if _dbg:
        _CACHE["dbg"] = {k: np.asarray(R[0][k]).astype(np.float32) for k in ("dbg1", "dbg2", "dbg3")}
    yp = np.zeros((4, 4096, D), np.float32)
    ys = np.zeros((128, 1, D), np.float32)
    rsp = np.zeros((1, 4, 4, 128, 256), np.float32)
    rss = np.zeros((1, 128, 4, 128, 256), np.float32)
    kp = np.zeros((1, 4, 128, 4, 64), np.float32)
    vp = np.zeros((1, 4, 128, 4, 64), np.float32)
    ks = np.zeros((1, 128, 128, 4, 64), np.float32)
    vs = np.zeros((1, 128, 128, 4, 64), np.float32)
    for c in range(8):
        b, half = c // 2, c % 2
        r = R[c]
        yp[b, half * 2048:(half + 1) * 2048] = r["yp"]
        ys[16 * c:16 * c + 16, 0] = r["ys"]
        rss[0, 16 * c:16 * c + 16] = r["rss"]
        ks[0, 16 * c:16 * c + 16] = r["ks"].reshape(16, 128, 4, 64)
        vs[0, 16 * c:16 * c + 16] = r["vs"].reshape(16, 128, 4, 64)
        if half == 1:
            rsp[0, b] = r["rsp"]
            kp[0, b] = r["kp"].reshape(128, 4, 64)
            vp[0, b] = r["vp"].reshape(128, 4, 64)
    return yp, ys, rsp, rss, kp, vp, ks, vs
```
